# Optimizing a Trainium2 kernel written in Bass

```python
import jax, jax.numpy as jnp
from jax import lax
import numpy as np


D_MODEL = 1024
BATCH = 2
SEQ = 8192
DEPTH = 1

D_MIX = D_MODEL
D_CONV = D_MIX // 2
D_GMLP = D_MIX - D_CONV
HEAD_DIM = 64
N_CONV_HEADS = D_CONV // HEAD_DIM
N_GMLP_HEADS = D_GMLP // HEAD_DIM
CONV_WIDTH = 31
CHUNK = 128
D_FF = 2816
FFN_CONV_WIDTH = 3
N_MOD = 6
RMS_EPS = 1e-6
LN_EPS = 1e-5

kernel_name = "hybrid_conv_gmlp_adaln_block"


def rms_norm(x, g):
    xf = x.astype(jnp.float32)
    y = xf * lax.rsqrt(jnp.mean(xf * xf, axis=-1, keepdims=True) + RMS_EPS)
    return (y * g.astype(jnp.float32)).astype(x.dtype)


def layer_norm(x, g, b):
    xf = x.astype(jnp.float32)
    mu = jnp.mean(xf, axis=-1, keepdims=True)
    var = jnp.mean(jnp.square(xf - mu), axis=-1, keepdims=True)
    y = (xf - mu) * lax.rsqrt(var + LN_EPS)
    return (y * g.astype(jnp.float32) + b.astype(jnp.float32)).astype(x.dtype)


def causal_dwconv(x, w, b):
    k = w.shape[0]
    ch = x.shape[-1]
    y = lax.conv_general_dilated(
        x, w[:, None, :].astype(x.dtype), window_strides=(1,), padding=[(k - 1, 0)],
        dimension_numbers=("NWC", "WIO", "NWC"), feature_group_count=ch)
    return y + b.astype(x.dtype)


def modulate(h, shift, scale):
    return h * (1.0 + scale[:, None, :]) + shift[:, None, :]


def setup_inputs(seed: int = 0) -> dict:
    key = jax.random.key(seed)
    ks = jax.random.split(key, 24)
    f32 = jnp.float32
    L, D = DEPTH, D_MODEL

    def nrm(k, shape, scale):
        return jax.random.normal(k, shape, f32) * scale

    return {
        "x": nrm(ks[0], (BATCH, SEQ, D), 1.0),
        "c": nrm(ks[1], (BATCH, D), 1.0),
        "w_ada": nrm(ks[2], (L, D, N_MOD * D), 0.5 * D ** -0.5),
        "b_ada": nrm(ks[3], (L, N_MOD * D), 0.01),
        "norm1_gain": 1.0 + nrm(ks[4], (L, D), 0.02),
        "w_in": nrm(ks[5], (L, D, 2 * D_MIX), D ** -0.5),
        "conv_dw_w": nrm(ks[6], (L, CONV_WIDTH, D_CONV), CONV_WIDTH ** -0.5),
        "conv_dw_b": nrm(ks[7], (L, D_CONV), 0.01),
        "conv_ln_g": 1.0 + nrm(ks[8], (L, D_CONV), 0.02),
        "conv_ln_b": nrm(ks[9], (L, D_CONV), 0.01),
        "gm_ln_g": 1.0 + nrm(ks[10], (L, D_GMLP), 0.02),
        "gm_ln_b": nrm(ks[11], (L, D_GMLP), 0.01),
        "gm_ws": nrm(ks[12], (L, N_GMLP_HEADS, CHUNK, CHUNK), CHUNK ** -0.5),
        "gm_bs": 1.0 + nrm(ks[13], (L, N_GMLP_HEADS, CHUNK), 0.01),
        "mix_out_gain": 1.0 + nrm(ks[14], (L, D_MIX), 0.02),
        "w_out": nrm(ks[15], (L, D_MIX, D), D_MIX ** -0.5),
        "norm2_gain": 1.0 + nrm(ks[16], (L, D), 0.02),
        "w_up": nrm(ks[17], (L, D, 2 * D_FF), D ** -0.5),
        "ffn_dw_w": nrm(ks[18], (L, FFN_CONV_WIDTH, 2 * D_FF), FFN_CONV_WIDTH ** -0.5),
        "ffn_dw_b": nrm(ks[19], (L, 2 * D_FF), 0.01),
        "w_down": nrm(ks[20], (L, D_FF, D), D_FF ** -0.5),
        "final_gain": 1.0 + nrm(ks[21], (D,), 0.02),
    }


def reference(x, c, w_ada, b_ada, norm1_gain, w_in, conv_dw_w, conv_dw_b, conv_ln_g, conv_ln_b,
              gm_ln_g, gm_ln_b, gm_ws, gm_bs, mix_out_gain, w_out, norm2_gain, w_up,
              ffn_dw_w, ffn_dw_b, w_down, final_gain):
    bsz, seq, _ = x.shape
    n_chunks = seq // CHUNK
    causal_mask = jnp.tril(jnp.ones((CHUNK, CHUNK), dtype=x.dtype))
    c_act = jax.nn.silu(c)

    for l in range(DEPTH):
        mod = c_act @ w_ada[l] + b_ada[l]
        sh1, sc1, gt1, sh2, sc2, gt2 = jnp.split(mod, N_MOD, axis=-1)

        h = modulate(rms_norm(x, norm1_gain[l]), sh1, sc1)
        z = h @ w_in[l]
        ca, cg, gu, gv = jnp.split(z, [D_CONV, 2 * D_CONV, 2 * D_CONV + D_GMLP], axis=-1)

        a = ca * jax.nn.sigmoid(cg)
        a = causal_dwconv(a, conv_dw_w[l], conv_dw_b[l])
        a = jax.nn.silu(layer_norm(a, conv_ln_g[l], conv_ln_b[l]))

        gu = jax.nn.gelu(gu)
        gv = layer_norm(jax.nn.gelu(gv), gm_ln_g[l], gm_ln_b[l])
        gv = gv.reshape(bsz, n_chunks, CHUNK, N_GMLP_HEADS, HEAD_DIM)
        ws = gm_ws[l] * causal_mask[None]
        sp = jnp.einsum("hts,bnshc->bnthc", ws, gv)
        sp = sp + jnp.transpose(gm_bs[l])[None, None, :, :, None]
        g = gu * sp.reshape(bsz, seq, D_GMLP)

        y = jnp.concatenate([rms_norm(a, mix_out_gain[l, :D_CONV]),
                             rms_norm(g, mix_out_gain[l, D_CONV:])], axis=-1)
        x = x + gt1[:, None, :] * (y @ w_out[l])

        h = modulate(rms_norm(x, norm2_gain[l]), sh2, sc2)
        up = causal_dwconv(h @ w_up[l], ffn_dw_w[l], ffn_dw_b[l])
        val, gate = jnp.split(up, 2, axis=-1)
        x = x + gt2[:, None, :] * ((jax.nn.silu(gate) * val) @ w_down[l])

    return rms_norm(x, final_gain)
```

```python
import contextlib
import numpy as np
import concourse.bass as bass
import concourse.mybir as mybir
from concourse.bass_utils import run_bass_kernel_spmd

F32 = mybir.dt.float32
BF16 = mybir.dt.bfloat16
AF = mybir.ActivationFunctionType
ALU = mybir.AluOpType

D = 1024
DFF = 2816
NFT = 22
SEQ = 8192
NCORES = 8
TOK_PER_CORE = 2048
NCH = 16
RMS_EPS = 1e-6
LN_EPS = 1e-5
KW = 31

R_BADA = 0
R_G1 = 48
R_CW = 56
R_CB = 180
R_CLG = 184
R_CLB = 188
R_MOG = 192
R_G2 = 200
R_FW = 208
R_FB = 340
NROWS = 384
O_GMG = 0
O_GMB = 512
O_FING = 1024
O_BS = 2048
NRP = 2560
O_BGT1 = 2560
O_BGT2 = 3584
NRB = 4608

SLICES = [list(range(0, 6)), list(range(6, 12)), list(range(12, 17)), list(range(17, 22))]
DIAG_ENG = "dve"
SLOT_ELEMS = 18432


class Tracker:
    def __init__(self, nc, es):
        self.nc = nc
        self.eng = {"pe": nc.tensor, "act": nc.scalar, "dve": nc.vector,
                    "pool": nc.gpsimd, "sp": nc.sync}
        self.sem = {}
        for e in ("pe", "act", "dve", "pool"):
            self.sem[e] = es.enter_context(nc.semaphore("c_" + e))
        self.cnt = {e: 0 for e in self.sem}
        self.waited = {e: {} for e in self.eng}
        self.state = {}
        self.ndma = 72
        self.dsem = [es.enter_context(nc.semaphore("d%d" % i)) for i in range(self.ndma)]
        self.dval = [0] * self.ndma
        self.drr = 0
        self.semname = {}
        for e, s in self.sem.items():
            self.semname[id(s)] = e
        self.n_wait = 0
        self.n_inst = 0

    def _deps(self, eng, reads, writes):
        deps = []
        for k in reads:
            st = self.state.get(k)
            if st is not None and st[0] is not None:
                deps.append(st[0] + ("raw",))
        for k in writes:
            st = self.state.get(k)
            if st is not None:
                if st[0] is not None:
                    deps.append(st[0] + ("waw",))
                for d in st[1].values():
                    deps.append(d + ("war",))
        return deps

    def _wait(self, eng, deps):
        best = {}
        for (sem, val, prod, kind) in deps:
            if prod == eng:
                if eng == "pe" or kind != "raw":
                    continue
            key = id(sem)
            if key not in best or best[key][1] < val:
                best[key] = (sem, val)
        for key, (sem, val) in best.items():
            if self.waited[eng].get(key, 0) >= val:
                continue
            self.eng[eng].wait_ge(sem, val)
            self.waited[eng][key] = val
            self.n_wait += 1

    def _update(self, eng, dep, reads, writes):
        for k in writes:
            self.state[k] = [dep, {}]
        for k in reads:
            st = self.state.get(k)
            if st is None:
                st = [None, {}]
                self.state[k] = st
            st[1][dep[2] if dep[2] != "dma" else ("dma", id(dep[0]))] = dep

    def op(self, eng, fn, reads=(), writes=()):
        self._wait(eng, self._deps(eng, reads, writes))
        ins = fn(self.eng[eng])
        self.cnt[eng] += 1
        ins.then_inc(self.sem[eng], 1)
        self.n_inst += 1
        dep = (self.sem[eng], self.cnt[eng], eng)
        self._update(eng, dep, reads, writes)
        return dep

    def mm(self, mms, reads=(), writes=(), single=False):
        self._wait("pe", self._deps("pe", reads, writes))
        n = len(mms)
        ins = None
        for i, (out, lhsT, rhs) in enumerate(mms):
            if single:
                ins = self.nc.tensor.matmul(out, lhsT, rhs, start=True, stop=True)
            else:
                ins = self.nc.tensor.matmul(out, lhsT, rhs, start=(i == 0), stop=(i == n - 1))
        self.cnt["pe"] += 1
        ins.then_inc(self.sem["pe"], 1)
        self.n_inst += n
        dep = (self.sem["pe"], self.cnt["pe"], "pe")
        self._update("pe", dep, reads, writes)
        return dep

    def transposes(self, tps, ident, reads=(), writes=()):
        self._wait("pe", self._deps("pe", reads, writes))
        ins = None
        for (out, in_) in tps:
            ins = self.nc.tensor.transpose(out, in_, ident)
        self.cnt["pe"] += 1
        ins.then_inc(self.sem["pe"], 1)
        self.n_inst += len(tps)
        dep = (self.sem["pe"], self.cnt["pe"], "pe")
        self._update("pe", dep, reads, writes)
        return dep

    def dma(self, q, out, in_, reads=(), writes=()):
        self._wait(q, self._deps(q, reads, writes))
        i = self.drr
        self.drr = (self.drr + 1) % self.ndma
        sem = self.dsem[i]
        if self.dval[i] > 0 and self.waited[q].get(id(sem), 0) < self.dval[i]:
            self.eng[q].wait_ge(sem, self.dval[i])
            self.waited[q][id(sem)] = self.dval[i]
        self.eng[q].dma_start(out=out, in_=in_).then_inc(sem, 16)
        self.dval[i] += 16
        self.n_inst += 1
        dep = (sem, self.dval[i], "dma")
        self._update("dma", dep, reads, writes)
        return dep

    def barrier(self):
        for e in self.eng:
            for p in self.sem:
                if p == e or self.cnt[p] == 0:
                    continue
                if self.waited[e].get(id(self.sem[p]), 0) < self.cnt[p]:
                    self.eng[e].wait_ge(self.sem[p], self.cnt[p])
                    self.waited[e][id(self.sem[p])] = self.cnt[p]
            for i in range(self.ndma):
                if self.dval[i] > 0 and self.waited[e].get(id(self.dsem[i]), 0) < self.dval[i]:
                    self.eng[e].wait_ge(self.dsem[i], self.dval[i])
                    self.waited[e][id(self.dsem[i])] = self.dval[i]

    def finish(self):
        e = "sp"
        for i in range(self.ndma):
            if self.dval[i] > 0 and self.waited[e].get(id(self.dsem[i]), 0) < self.dval[i]:
                self.eng[e].wait_ge(self.dsem[i], self.dval[i])
                self.waited[e][id(self.dsem[i])] = self.dval[i]
        for p in self.sem:
            if self.cnt[p] and self.waited[e].get(id(self.sem[p]), 0) < self.cnt[p]:
                self.eng[e].wait_ge(self.sem[p], self.cnt[p])


def build_program(debug=None):
    nc = bass.Bass("TRN2", target_bir_lowering=False)
    dbg_out = {}

    def din(name, shape):
        return nc.dram_tensor(name, list(shape), F32, kind="ExternalInput").ap()

    x_ext = din("x_ext", [17 * 128, D])
    msk_d = din("msk", [128, 1])
    ccol_d = din("c_col", [128, 8])
    vecs_d = din("vecs", [NROWS, 128])
    rows_d = din("rows", [128, NRB])
    wsT_d = din("wsT", [128, 8, 128])
    w_ada = din("w_ada", [D, 6 * D])
    w_in = din("w_in", [D, 2 * D])
    w_out = din("w_out", [D, D])
    w_up = din("w_up", [D, 2 * DFF])
    w_down = din("w_down", [DFF, D])
    out_d = nc.dram_tensor("out", [TOK_PER_CORE, D], F32, kind="ExternalOutput").ap()
    diag_scr = nc.dram_tensor("diag_scr", [4, 128, KW * 128], BF16, kind="Internal").ap()

    uid = [0]

    with contextlib.ExitStack() as es:
        T = Tracker(nc, es)

        def sb(st, name, shape, dt):
            uid[0] += 1
            return st.enter_context(nc.sbuf_tensor("%s_%d" % (name, uid[0]), list(shape), dt))

        banks = [es.enter_context(nc.psum_tensor("bank%d" % i, [128, 512], F32)) for i in range(8)]
        brr = [0]

        held = set()

        def bank(hold=False):
            for _ in range(8):
                i = brr[0]
                brr[0] = (i + 1) % 8
                if i not in held:
                    if hold:
                        held.add(i)
                    return i
            raise RuntimeError("all PSUM banks held")

        def release(i):
            held.discard(i)

        cols = sb(es, "cols", [128, NROWS], F32)
        modc = sb(es, "modc", [128, 48], F32)
        ident_f = sb(es, "ident_f", [128, 128], F32)
        ident_b = sb(es, "ident_b", [128, 128], BF16)
        ones_b = sb(es, "ones_b", [128, 128], BF16)
        cmask = sb(es, "cmask", [128, 128], F32)
        iot = sb(es, "iot", [128, 128], F32)
        rows = sb(es, "rows", [128, NRP], F32)
        gt1 = sb(es, "gt1", [128, D], F32)
        gt2 = sb(es, "gt2", [128, D], F32)
        wsT = sb(es, "wsT", [128, 8, 128], BF16)
        msk = sb(es, "msk", [128, 1], F32)
        ccol = sb(es, "ccol", [128, 8], F32)
        cact = sb(es, "cact", [128, 8], BF16)
        cact_rep = sb(es, "cact_rep", [128, 8, 128], BF16)
        carry = sb(es, "carry", [128, 2, NFT, 2, 2], F32)
        cwb = sb(es, "cwb", [128, 4, KW], BF16)
        a_carry = sb(es, "a_carry", [128, 4, 30], BF16)
        small = sb(es, "small", [128, 96], F32)
        mraw = sb(es, "mraw", [128, 32], F32)
        mhalf = sb(es, "mhalf", [128, 8], F32)
        epsr = sb(es, "epsr", [128, 2], F32)
        hb = sb(es, "hb", [128, 8, 512], BF16)
        gua = sb(es, "gua", [128, 4, 512], F32)
        h2h = sb(es, "h2h", [128, 8, 2], BF16)
        x1 = sb(es, "x1", [128, 9, D], F32)
        slot = [sb(es, "slot0", [128, SLOT_ELEMS], BF16), sb(es, "slot1", [128, SLOT_ELEMS], BF16)]

        def dbg(name, ap, shape, reads):
            if debug is None or name not in debug:
                return
            t = nc.dram_tensor("dbg_" + name, list(shape), ap.dtype, kind="ExternalOutput").ap()
            dbg_out[name] = t
            T.dma("sp", t, ap, reads=reads)

        for m_, off_, keys_ in ((0, 0, ["wout"]), (1, 8192, [("diag", 0), ("diag", 1)])):
            T.dma("pool", slot[1][:, off_:off_ + 8192].rearrange("p (k n) -> p k n", k=8),
                  w_ada[:, m_ * D:(m_ + 1) * D].rearrange("(kt p) n -> p kt n", p=128), writes=keys_)
        T.dma("sp", msk[:], msk_d, writes=["msk"])
        T.dma("sp", ccol[:], ccol_d, writes=["ccol"])
        T.dma("sp", rows[:], rows_d[:, 0:NRP], writes=["rows"])
        T.op("pool", lambda e: e.iota(iot[:], [[1, 128]], base=0, channel_multiplier=-1,
                                      allow_small_or_imprecise_dtypes=True), writes=["iot"])
        T.op("pool", lambda e: e.tensor_single_scalar(ident_f[:], iot[:], 0.0, ALU.is_equal),
             reads=["iot"], writes=["ident_f"])
        T.op("pool", lambda e: e.tensor_single_scalar(cmask[:], iot[:], 0.0, ALU.is_ge),
             reads=["iot"], writes=["cmask"])
        T.op("pool", lambda e: e.tensor_copy(ident_b[:], ident_f[:]), reads=["ident_f"], writes=["ident_b"])
        T.op("pool", lambda e: e.memset(ones_b[:], 1.0), writes=["ones_b"])
        T.op("pool", lambda e: e.memset(small[:], 0.0), writes=["small"])
        T.op("pool", lambda e: e.memset(mhalf[:], -0.5), writes=["mhalf"])
        T.op("pool", lambda e: e.memset(epsr[:, 0:1], RMS_EPS), writes=["epsr"])
        T.op("pool", lambda e: e.memset(epsr[:, 1:2], LN_EPS), writes=["epsr"])

        wadaA = slot[1][:, 0:8192].rearrange("p (k n) -> p k n", k=8)
        wadaB = slot[1][:, 8192:16384].rearrange("p (k n) -> p k n", k=8)
        wadaC = x1[:, 5:9, :].rearrange("p c d -> p (c d)").bitcast(BF16).rearrange("p (k n) -> p k n", k=8)
        KA = ["wout"]
        KB = [("diag", 0), ("diag", 1)]
        KC = [("x1", c) for c in range(5, 9)]

        def ada_cols(wb, keys, bcol, cm):
            for j in range(8):
                T.mm([(banks[bcol][:, cm * 8 + j: cm * 8 + j + 1], wb[:, kt, j * 128:(j + 1) * 128],
                       cact[:, kt:kt + 1]) for kt in range(8)], reads=keys + ["cact"], writes=[("ps", bcol)])

        def ada_rows(wb, keys, dst, dkey, brow, bkeys):
            for hf in range(2):
                b = bank()
                T.mm([(banks[b][:], cact_rep[:, kt, :], wb[:, kt, hf * 512:(hf + 1) * 512])
                      for kt in range(8)], reads=keys + ["cact_rep"], writes=[("ps", b)])
                T.op("dve", lambda e, b=b, hf=hf: e.tensor_tensor(
                    dst[:, hf * 512:(hf + 1) * 512], banks[b][:], brow[:, hf * 512:(hf + 1) * 512], ALU.add),
                    reads=[("ps", b)] + bkeys, writes=[dkey])

        def ada_dma(m, wb, keys):
            T.dma("pool", wb, w_ada[:, m * D:(m + 1) * D].rearrange("(kt p) n -> p kt n", p=128), writes=keys)

        def mod_raw(bcol, cm, mod):
            T.op("dve", lambda e: e.tensor_tensor(
                mraw[:, cm * 8:(cm + 1) * 8], banks[bcol][:, cm * 8:(cm + 1) * 8],
                cols[:, R_BADA + mod * 8: R_BADA + mod * 8 + 8], ALU.add),
                reads=[("ps", bcol), "cols"], writes=["mraw"])

        def mod_finish(sh_c, sc_c, so, g):
            T.op("dve", lambda e: e.scalar_tensor_tensor(
                modc[:, so:so + 8], mraw[:, sc_c * 8:sc_c * 8 + 8], 1.0, cols[:, g:g + 8], ALU.add, ALU.mult),
                reads=["mraw", "cols"], writes=["modc"])
            T.op("dve", lambda e: e.tensor_copy(modc[:, so + 8:so + 16], mraw[:, sh_c * 8:sh_c * 8 + 8]),
                 reads=["mraw"], writes=["modc"])

        bgt = slot[0][:, 16384:18432].bitcast(F32)
        kst = [(0, "c")]

        def setup_part1(gl, sq, stat):
            vecs = gl[:, 0:384].rearrange("p (j f) -> p j f", j=3)
            wsf = sq[:].rearrange("p a b -> p (a b)").bitcast(F32).rearrange("p (h t) -> p h t", h=8)
            ksq = [("sq", i) for i in range(4)]
            T.dma("sp", vecs, vecs_d.rearrange("(j p) f -> p j f", p=128), writes=["gl"])
            T.dma("sp", wsf, wsT_d, writes=ksq)
            load_w_in(after=KB)
            ada_dma(2, wadaC, KC)
            T.dma("sp", bgt, rows_d[:, O_BGT1:O_BGT1 + D], writes=kst)
            for j in range(3):
                b = bank()
                T.transposes([(banks[b][:, 0:128], vecs[:, j, :])], ident_f[:],
                             reads=["gl", "ident_f"], writes=[("ps", b)])
                T.op("dve", lambda e, b=b, j=j: e.tensor_copy(cols[:, j * 128:(j + 1) * 128], banks[b][:, 0:128]),
                     reads=[("ps", b)], writes=["cols"])
            T.op("dve", lambda e: e.tensor_tensor(
                wsT[:], wsf, cmask[:].unsqueeze(1).to_broadcast([128, 8, 128]), ALU.mult),
                reads=ksq + ["cmask"], writes=["wsT"])
            T.op("act", lambda e: e.activation(out=cact[:], in_=ccol[:], func=AF.Silu),
                 reads=["ccol"], writes=["cact"])
            for kt in range(8):
                T.op("dve", lambda e, kt=kt: e.tensor_copy(
                    cact_rep[:, kt, :], cact[:, kt:kt + 1].to_broadcast([128, 128])),
                    reads=["cact"], writes=["cact_rep"])
            T.op("dve", lambda e: e.tensor_copy(
                cwb[:], cols[:, R_CW:R_CW + 4 * KW].rearrange("p (k c) -> p c k", c=4)),
                reads=["cols"], writes=["cwb"])
            regs = [
                (hb[:].rearrange("p a b -> p (a b)")[:, 0:KW * 128], ["hb"]),
                (gua[:].rearrange("p a b -> p (a b)").bitcast(BF16)[:, 0:KW * 128], [("gua", i) for i in range(4)]),
                (x1[:, 1:3, :].rearrange("p a b -> p (a b)").bitcast(BF16)[:, 0:KW * 128], [("x1", 1), ("x1", 2)]),
                (x1[:, 3:5, :].rearrange("p a b -> p (a b)").bitcast(BF16)[:, 0:KW * 128], [("x1", 3), ("x1", 4)]),
            ]
            for ct in range(4):
                rv, rk = regs[ct]
                T.op("dve", lambda e, ct=ct, rv=rv: e.tensor_tensor(
                    rv.rearrange("p (k n) -> p k n", k=KW), ident_b[:].unsqueeze(1).to_broadcast([128, KW, 128]),
                    cwb[:, ct, :].unsqueeze(2).to_broadcast([128, KW, 128]), ALU.mult),
                    reads=["ident_b", "cwb"], writes=rk)
                T.dma("sp", diag_scr[ct], rv, reads=rk, writes=[("dscr", ct)])
            bcol = bank()
            ada_cols(wadaA, KA, bcol, 0)
            mod_raw(bcol, 0, 0)
            T.dma("pool", w_out_sb, w_out.rearrange("(kt p) n -> p kt n", p=128), writes=["wout"])
            bcol = bank()
            ada_cols(wadaB, KB, bcol, 1)
            mod_raw(bcol, 1, 1)
            mod_finish(0, 1, 0, R_G1)

        def hook_gt1():
            ada_rows(wadaC, KC, gt1, "gt1", bgt, kst)
            T.dma("sp", bgt, rows_d[:, O_BGT2:O_BGT2 + D], writes=kst)
            load_x(0, [5, 6, 7, 8])

        late = {}

        def late_regions(gvn):
            regs = []
            for r in range(2):
                v = gua[:, 2 * r:2 * r + 2, :].rearrange("p a b -> p (a b)").bitcast(BF16).rearrange("p (k n) -> p k n", k=8)
                regs.append((v, [("gua", 2 * r), ("gua", 2 * r + 1)]))
            for r in range(2):
                v = hb[:, 4 * r:4 * r + 4, :].rearrange("p a b -> p (a b)").rearrange("p (k n) -> p k n", k=8)
                regs.append((v, ["hb"]))
            regs.append((gvn[:].rearrange("p a b -> p (a b)").rearrange("p (k n) -> p k n", k=8), [("gvn", i) for i in range(4)]))
            return regs

        RMAP = [0, 1, 4, 2, 3, 0, 1, 4, 0, 1, 4, 0]

        def late_issue(i):
            m, q = late["plan"][i]
            wb, keys = late["regs"][RMAP[i]]
            T.dma("pool", wb, w_ada[:, m * D + q * 256: m * D + (q + 1) * 256].rearrange("(kt p) n -> p kt n", p=128),
                  writes=keys)
            late["issued"][i] = (wb, keys)

        def late_blocks_start(gvn):
            late["regs"] = late_regions(gvn)
            late["plan"] = [(m, q) for m in (3, 4, 5) for q in range(4)]
            late["issued"] = {}
            seen = set()
            for i, r in enumerate(RMAP):
                if r not in seen:
                    seen.add(r)
                    late_issue(i)

        def late_blocks(after_mod=None):
            plan = late["plan"]
            for i, (m, q) in enumerate(plan):
                wb, keys = late["issued"][i]
                if m in (3, 4):
                    cm = 2 if m == 3 else 3
                    bcol = bank()
                    for j in range(2):
                        T.mm([(banks[bcol][:, j:j + 1], wb[:, kt, j * 128:(j + 1) * 128], cact[:, kt:kt + 1]) for kt in range(8)],
                             reads=keys + ["cact"], writes=[("ps", bcol)])
                    c0 = cm * 8 + q * 2
                    T.op("dve", lambda e, bcol=bcol, c0=c0, m=m, q=q: e.tensor_tensor(
                        mraw[:, c0:c0 + 2], banks[bcol][:, 0:2],
                        cols[:, R_BADA + m * 8 + q * 2: R_BADA + m * 8 + q * 2 + 2], ALU.add),
                        reads=[("ps", bcol), "cols"], writes=["mraw"])
                else:
                    b = bank()
                    T.mm([(banks[b][:, 0:256], cact_rep[:, kt, :], wb[:, kt, :]) for kt in range(8)],
                         reads=keys + ["cact_rep"], writes=[("ps", b)])
                    T.op("dve", lambda e, b=b, q=q: e.tensor_tensor(
                        gt2[:, q * 256:(q + 1) * 256], banks[b][:, 0:256], bgt[:, q * 256:(q + 1) * 256], ALU.add),
                        reads=[("ps", b)] + kst, writes=["gt2"])
                for j in range(i + 1, len(plan)):
                    if RMAP[j] == RMAP[i]:
                        if j not in late["issued"]:
                            late_issue(j)
                        break
                if (m, q) == (4, 3):
                    mod_finish(2, 3, 16, R_G2)
                    if after_mod is not None:
                        after_mod()

        def load_x(half_, tl):
            xrow0_ = 0 if half_ == 0 else 9 * 128
            c0, n = tl[0], len(tl)
            T.dma("sp", x1[:, c0:c0 + n, :],
                  x_ext[xrow0_ + c0 * 128: xrow0_ + (c0 + n) * 128, :].rearrange("(c p) d -> p c d", p=128),
                  writes=[("x1", c) for c in tl])

        S0ALL = [(0, "a"), (0, "b"), (0, "c")]

        def slice_views(s_):
            nf = len(SLICES[s_])
            buf = slot[s_ % 2]
            wv = buf[:, 0:8 * nf * 128].rearrange("p (k n) -> p k n", k=8)
            wg = buf[:, 6144:6144 + 8 * nf * 128].rearrange("p (k n) -> p k n", k=8)
            wd = buf[:, 12288:12288 + nf * 1024].rearrange("p (f n) -> p f n", f=nf)
            return wv, wg, wd, s_ % 2

        def load_w_in(after=()):
            wv_ = w_in.rearrange("(kt p) n -> p kt n", p=128)
            for pi, (c0, c1) in enumerate(((1536, 2048), (0, 1024), (1024, 1536))):
                T.dma("pool", w_in_sb[:, :, c0:c1], wv_[:, :, c0:c1], reads=list(after) if pi == 1 else [],
                      writes=(S0ALL if pi == 0 else []) + [("win", pi)])

        def load_slice_down(s_):
            wv, wg, wd, key = slice_views(s_)
            f0 = SLICES[s_][0]
            nf = len(SLICES[s_])
            T.dma("pool", wd, w_down[f0 * 128:(f0 + nf) * 128, :].rearrange("(f p) n -> p f n", p=128), writes=[(key, "c")])
            for fi in range(nf):
                T.op("pool", lambda e, fi=fi, wd=wd: e.tensor_tensor(wd[:, fi, :], wd[:, fi, :], gt2[:], ALU.mult),
                     reads=[(key, "c"), "gt2"], writes=[(key, "c")])

        def load_slice_up(s_):
            nf = len(SLICES[s_])
            f0 = SLICES[s_][0] * 128
            wv, wg, wd, key = slice_views(s_)
            T.dma("pool", wv, w_up[:, f0:f0 + nf * 128].rearrange("(kt p) n -> p kt n", p=128), writes=[(key, "a")])
            T.dma("pool", wg, w_up[:, DFF + f0:DFF + f0 + nf * 128].rearrange("(kt p) n -> p kt n", p=128), writes=[(key, "b")])


        w_in_sb = slot[0][:, 0:16384].rearrange("p (k n) -> p k n", k=8)
        w_out_sb = slot[1][:, 0:8192].rearrange("p (k n) -> p k n", k=8)
        diag = [slot[1][:, 8192 + i * 3968: 8192 + (i + 1) * 3968].rearrange("p (k n) -> p k n", k=KW)
                for i in range(2)]

        def norm_group(x_aps, xkeys, scol, shcol, dst_fn, dst_key, tmp, ssl, lo=0, junk_key="junk"):
            junk, xns = tmp
            n = len(x_aps)
            dkl = dst_key if isinstance(dst_key, list) else [dst_key]
            for ci in range(n):
                T.op("act", lambda e, ci=ci: e.activation(out=junk, in_=x_aps[ci], func=AF.Square,
                                                          accum_out=small[:, ssl + ci:ssl + ci + 1]),
                     reads=[xkeys[ci]], writes=[junk_key, ("small", ssl)])
            T.op("dve", lambda e: e.tensor_scalar(small[:, ssl + 4:ssl + 4 + n], small[:, ssl:ssl + n], 1.0 / D, RMS_EPS,
                                                   ALU.mult, ALU.add),
                 reads=[("small", ssl)], writes=[("small", ssl + 4)])
            T.op("pool", lambda e: e.tensor_tensor(small[:, ssl + 8:ssl + 8 + n], small[:, ssl + 4:ssl + 4 + n],
                                                    mhalf[:, 0:n], ALU.pow),
                 reads=[("small", ssl + 4), "mhalf"], writes=[("small", ssl + 8)])
            bks = [bank() for _ in range(4)]
            pbs = [banks[b][:].bitcast(BF16) for b in bks]
            for ci in range(n):
                xn = xns[ci % 2]
                xk = ("xn", ci % 2)
                T.op("dve", lambda e, ci=ci, xn=xn: e.tensor_scalar(xn[:], x_aps[ci], small[:, ssl + 8 + ci:ssl + 9 + ci], None, ALU.mult),
                     reads=[xkeys[ci], ("small", ssl + 8)], writes=[xk])
                T.transposes([(pbs[dt // 2][:, (dt % 2) * 512 + ci * 128:(dt % 2) * 512 + (ci + 1) * 128],
                               xn[:, dt * 128:(dt + 1) * 128]) for dt in range(8)], ident_b[:],
                             reads=[xk, "ident_b"], writes=[("ps", b) for b in bks])
            for dt in range(8):
                pb = pbs[dt // 2]
                q = dt % 2
                if dt % 2 == 0:
                    T.op("act", lambda e, dt=dt, q=q, pb=pb: e.activation(
                        out=dst_fn(dt), in_=pb[:, q * 512 + lo:q * 512 + n * 128], func=AF.Identity,
                        scale=modc[:, scol + dt:scol + dt + 1], bias=modc[:, shcol + dt:shcol + dt + 1]),
                        reads=[("ps", bks[dt // 2]), "modc"], writes=dkl)
                else:
                    T.op("dve", lambda e, dt=dt, q=q, pb=pb: e.tensor_scalar(
                        dst_fn(dt), pb[:, q * 512 + lo:q * 512 + n * 128],
                        modc[:, scol + dt:scol + dt + 1], modc[:, shcol + dt:shcol + dt + 1], ALU.mult, ALU.add),
                        reads=[("ps", bks[dt // 2]), "modc"], writes=dkl)

        for half in range(2):
            nchk = 9 if half == 0 else 8
            ntok = nchk * 128
            if half == 0:
                tiles = [[0], [1, 2, 3, 4], [5, 6, 7, 8]]
            else:
                tiles = [[0, 1, 2, 3], [4, 5, 6, 7]]
            if half == 0:
                load_x(0, tiles[0])
            with contextlib.ExitStack() as sA:
                a_buf = sb(sA, "a_buf", [128, 4, 30 + ntok], BF16)
                xns = [sb(sA, "xn%d" % i, [128, D], BF16) for i in range(2)]
                sig = sb(sA, "sig", [128, 512], F32)
                gl = sb(sA, "gl", [128, 512], F32)
                junk = gl[:].bitcast(BF16)
                gvn = sb(sA, "gvn", [128, 4, 512], BF16)
                cv = sb(sA, "cv", [128, 4, 512], F32)
                cvb = sb(sA, "cvb", [128, 4, 512], BF16)
                sq = sb(sA, "sq", [128, 4, 512], BF16)
                stat = sb(sA, "stat", [128, 3, 512], F32)
                gsq = sb(sA, "gsq", [128, 4, 512], BF16)
                yb = sb(sA, "yb", [128, 8, 512], BF16)
                bst = sb(sA, "bst", [128, 6], F32)
                if half == 0:
                    print("phase A SBUF bytes remaining:", nc.sbuf_bytes_remaining)

                if half == 0:
                    setup_part1(gl, sq, stat)
                    load_x(0, tiles[1])
                if half == 0:
                    T.op("pool", lambda e: e.memset(a_buf[:, :, 0:30], 0.0), writes=["a_pre"])
                else:
                    T.op("pool", lambda e: e.tensor_copy(a_buf[:, :, 0:30], a_carry[:]),
                         reads=["a_carry"], writes=["a_pre"])

                def load_w_out():
                    T.dma("pool", w_out_sb, w_out.rearrange("(kt p) n -> p kt n", p=128), writes=["wout"])

                def fold_w_out():
                    for ct in range(8):
                        T.op("dve", lambda e, ct=ct: e.scalar_tensor_tensor(
                            w_out_sb[:, ct, :], w_out_sb[:, ct, :], cols[:, R_MOG + ct:R_MOG + ct + 1], gt1[:], ALU.mult, ALU.mult),
                            reads=["wout", "gt1", "cols"], writes=["wout"])

                dstate = {"n": 0}
                TL = []
                for ti, tl in enumerate(tiles):
                    TL.append(dict(ti=ti, tl=tl, n=len(tl), NT=len(tl) * 128, toff=tl[0] * 128,
                                   halo=(half == 0 and ti == 0)))

                def st_norm(t):
                    tl, NT = t["tl"], t["NT"]
                    norm_group([x1[:, c, :] for c in tl], [("x1", c) for c in tl], 0, 8,
                               lambda dt: hb[:, dt, 0:NT], "hb", (junk, xns), 0, junk_key="gl")

                def st_z_a(t, cts=(0, 1, 2, 3)):
                    NT, toff, ti = t["NT"], t["toff"], t["ti"]
                    for ct in cts:
                        b1 = bank()
                        T.mm([(banks[b1][:, 0:NT], w_in_sb[:, kt, 512 + ct * 128: 512 + (ct + 1) * 128], hb[:, kt, 0:NT])
                              for kt in range(8)], reads=S0ALL + [("win", 1), "hb"], writes=[("ps", b1)])
                        T.op("act", lambda e, b1=b1: e.activation(out=sig[:, 0:NT], in_=banks[b1][:, 0:NT], func=AF.Sigmoid),
                             reads=[("ps", b1)], writes=["sig"])
                        b2 = bank()
                        T.mm([(banks[b2][:, 0:NT], w_in_sb[:, kt, ct * 128:(ct + 1) * 128], hb[:, kt, 0:NT])
                              for kt in range(8)], reads=S0ALL + [("win", 1), "hb"], writes=[("ps", b2)])
                        adst = a_buf[:, ct, 30 + toff: 30 + toff + NT]
                        if t["halo"]:
                            T.op("dve", lambda e, b2=b2, adst=adst: e.scalar_tensor_tensor(
                                adst, banks[b2][:, 0:NT], msk[:, 0:1], sig[:, 0:NT], ALU.mult, ALU.mult),
                                reads=[("ps", b2), "sig", "msk"], writes=[("a", ti)])
                        else:
                            T.op("dve", lambda e, b2=b2, adst=adst: e.tensor_tensor(
                                adst, banks[b2][:, 0:NT], sig[:, 0:NT], ALU.mult),
                                reads=[("ps", b2), "sig"], writes=[("a", ti)])

                def st_z_gu(t):
                    NT = t["NT"]
                    for ct in range(4):
                        b = bank()
                        T.mm([(banks[b][:, 0:NT], w_in_sb[:, kt, 1024 + ct * 128: 1024 + (ct + 1) * 128], hb[:, kt, 0:NT])
                              for kt in range(8)], reads=S0ALL + [("win", 2), "hb"], writes=[("ps", b)])
                        T.op("act", lambda e, b=b, ct=ct: e.activation(out=gua[:, ct, 0:NT], in_=banks[b][:, 0:NT],
                                                                       func=AF.Gelu_apprx_tanh),
                             reads=[("ps", b)], writes=[("gua", ct)])

                def st_z_gv(t):
                    for ci in range(t["n"]):
                        b = bank()
                        T.mm([(banks[b][:], hb[:, kt, ci * 128:(ci + 1) * 128], w_in_sb[:, kt, 1536:2048])
                              for kt in range(8)], reads=S0ALL + [("win", 0), "hb"], writes=[("ps", b)])
                        T.op("act", lambda e, b=b: e.activation(out=gl[:], in_=banks[b][:], func=AF.Gelu_apprx_tanh),
                             reads=[("ps", b)], writes=["gl"])
                        T.op("dve", lambda e: e.bn_stats(bst[:, 0:6], gl[:]), reads=["gl"], writes=["bst"])
                        T.op("dve", lambda e: e.bn_aggr(small[:, 8:10], bst[:, 0:6]), reads=["bst"], writes=[("small", 8)])
                        T.op("dve", lambda e: e.tensor_scalar(small[:, 11:12], small[:, 9:10], LN_EPS, None, ALU.add),
                             reads=[("small", 8)], writes=[("small", 11)])
                        T.op("pool", lambda e: e.tensor_tensor(small[:, 10:11], small[:, 11:12], mhalf[:, 0:1], ALU.pow),
                             reads=[("small", 11), "mhalf"], writes=[("small", 10)])
                        T.op("dve", lambda e: e.tensor_scalar(gl[:], gl[:], small[:, 8:9], small[:, 10:11],
                                                               ALU.subtract, ALU.mult),
                             reads=["gl", ("small", 8), ("small", 10)], writes=["gl"])
                        T.op("dve", lambda e: e.tensor_tensor(gl[:], gl[:], rows[:, O_GMG:O_GMG + 512], ALU.mult),
                             reads=["gl", "rows"], writes=["gl"])
                        T.op("dve", lambda e, ci=ci: e.tensor_tensor(gvn[:, ci, :], gl[:], rows[:, O_GMB:O_GMB + 512], ALU.add),
                             reads=["gl", "rows"], writes=[("gvn", ci)])

                def st_spatial(t):
                    n, NT = t["n"], t["NT"]
                    for ct in range(4):
                        b = bank()
                        mms = []
                        for ci in range(n):
                            for hh in range(2):
                                hd = 2 * ct + hh
                                mms.append((banks[b][hh * 64:(hh + 1) * 64, ci * 128:(ci + 1) * 128],
                                            gvn[:, ci, hd * 64:(hd + 1) * 64], wsT[:, hd, :]))
                        T.mm(mms, reads=[("gvn", ci) for ci in range(n)] + ["wsT"], writes=[("ps", b)], single=True)
                        T.op("dve", lambda e, b=b, ct=ct: e.tensor_tensor(
                            stat[:, 0, 0:NT].rearrange("p (c t) -> p c t", t=128),
                            banks[b][:, 0:NT].rearrange("p (c t) -> p c t", t=128),
                            rows[:, O_BS + ct * 128: O_BS + (ct + 1) * 128].unsqueeze(1).to_broadcast([128, n, 128]),
                            ALU.add), reads=[("ps", b), "rows"], writes=[("stat", 0)])
                        T.op("dve", lambda e, ct=ct: e.tensor_tensor(yb[:, 4 + ct, 0:NT], gua[:, ct, 0:NT], stat[:, 0, 0:NT], ALU.mult),
                             reads=[("gua", ct), ("stat", 0)], writes=[("yb", 4 + ct)])
                        T.op("act", lambda e, ct=ct: e.activation(out=gsq[:, ct, 0:NT], in_=yb[:, 4 + ct, 0:NT], func=AF.Square),
                             reads=[("yb", 4 + ct)], writes=[("gsq", ct)])

                def rms_cols(t, base, src, skey):
                    n = t["n"]
                    par = t["ti"] % 2
                    b = bank()
                    for ci in range(n):
                        T.mm([(banks[b][:, ci:ci + 1], src[:, ct, ci * 128:(ci + 1) * 128], ones_b[:, 0:1]) for ct in range(4)],
                             reads=[(skey, ct) for ct in range(4)] + ["ones_b"], writes=[("ps", b)])
                    c0 = base + par * 4
                    T.op("dve", lambda e: e.tensor_scalar(small[:, 88:88 + n], banks[b][:, 0:n], 1.0 / 512, RMS_EPS, ALU.mult, ALU.add),
                         reads=[("ps", b)], writes=[("small", 88)])
                    T.op("pool", lambda e: e.tensor_tensor(small[:, c0:c0 + n], small[:, 88:88 + n], mhalf[:, 0:n], ALU.pow),
                         reads=[("small", 88), "mhalf"], writes=[("small", c0)])

                def st_ones_g(t):
                    rms_cols(t, 64, gsq, "gsq")

                def st_yb_g(t):
                    pass

                def st_conv_mm(t, cts=(0, 1, 2, 3)):
                    NT, toff, ti = t["NT"], t["toff"], t["ti"]
                    if "cb" not in t:
                        t["cb"] = []
                    for ct in cts:
                        ds_ = dstate["n"] % 2
                        dstate["n"] += 1
                        T.dma("sp", diag[ds_].rearrange("p k n -> p (k n)"), diag_scr[ct],
                              reads=[("dscr", ct)], writes=[("diag", ds_)])
                        b = bank(hold=True)
                        rd = [("diag", ds_), ("a", ti), "a_pre"] + ([("a", ti - 1)] if ti > 0 else [])
                        T.mm([(banks[b][:, 0:NT], diag[ds_][:, k, :], a_buf[:, ct, toff + k: toff + k + NT]) for k in range(KW)],
                             reads=rd, writes=[("ps", b)])
                        t["cb"].append(b)

                def st_conv_evac(t, cts=(0, 1, 2, 3)):
                    NT = t["NT"]
                    for ct in cts:
                        b = t["cb"][ct]
                        bias = cols[:, R_CB + ct:R_CB + ct + 1]
                        T.op("act", lambda e, b=b, ct=ct, bias=bias: e.activation(
                            out=cvb[:, ct, 0:NT], in_=banks[b][:, 0:NT], func=AF.Identity, bias=bias),
                            reads=[("ps", b), "cols"], writes=[("cvb", ct)])
                        T.op("act", lambda e, b=b, ct=ct, bias=bias: e.activation(
                            out=sq[:, ct, 0:NT], in_=banks[b][:, 0:NT], func=AF.Square, bias=bias),
                            reads=[("ps", b), "cols"], writes=[("sq", ct)])
                        T.op("act", lambda e, b=b, ct=ct, bias=bias: e.activation(
                            out=cv[:, ct, 0:NT], in_=banks[b][:, 0:NT], func=AF.Identity, bias=bias),
                            reads=[("ps", b), "cols"], writes=[("cv", ct // 2), ("cvx", ct)])
                        release(b)

                def st_ones_c(t):
                    NT = t["NT"]
                    bm = bank()
                    T.mm([(banks[bm][:, 0:NT], ones_b[:], cvb[:, ct, 0:NT]) for ct in range(4)],
                         reads=[("cvb", ct) for ct in range(4)] + ["ones_b"], writes=[("ps", bm)])
                    be = bank()
                    T.mm([(banks[be][:, 0:NT], ones_b[:], sq[:, ct, 0:NT]) for ct in range(4)],
                         reads=[("sq", ct) for ct in range(4)] + ["ones_b"], writes=[("ps", be)])
                    T.op("act", lambda e: e.activation(out=stat[:, 1, 0:NT], in_=banks[bm][:, 0:NT], func=AF.Identity, scale=1.0 / 512),
                         reads=[("ps", bm)], writes=[("stat", 1)])
                    T.op("dve", lambda e: e.tensor_tensor(stat[:, 2, 0:NT], stat[:, 1, 0:NT], stat[:, 1, 0:NT], ALU.mult),
                         reads=[("stat", 1)], writes=[("stat", 2)])
                    T.op("dve", lambda e: e.scalar_tensor_tensor(stat[:, 2, 0:NT], banks[be][:, 0:NT], 1.0 / 512, stat[:, 2, 0:NT],
                                                                  ALU.mult, ALU.subtract),
                         reads=[("ps", be), ("stat", 2)], writes=[("stat", 2)])
                    T.op("act", lambda e: e.activation(out=stat[:, 2, 0:NT], in_=stat[:, 2, 0:NT], func=AF.Sqrt,
                                                       bias=epsr[:, 1:2]),
                         reads=[("stat", 2), "epsr"], writes=[("stat", 2)])
                    T.op("dve", lambda e: e.reciprocal(stat[:, 2, 0:NT], stat[:, 2, 0:NT]),
                         reads=[("stat", 2)], writes=[("stat", 2)])
                    for ct in range(4):
                        T.op("dve", lambda e, ct=ct: e.tensor_tensor(cv[:, ct, 0:NT], cv[:, ct, 0:NT], stat[:, 1, 0:NT], ALU.subtract),
                             reads=[("cvx", ct), ("stat", 1)], writes=[("cvx", ct), ("cv", ct // 2)])
                        T.op("dve", lambda e, ct=ct: e.tensor_tensor(cv[:, ct, 0:NT], cv[:, ct, 0:NT], stat[:, 2, 0:NT], ALU.mult),
                             reads=[("cvx", ct), ("stat", 2)], writes=[("cvx", ct), ("cv", ct // 2)])
                    for ct in range(4):
                        T.op("act", lambda e, ct=ct: e.activation(out=yb[:, ct, 0:NT], in_=cv[:, ct, 0:NT], func=AF.Silu,
                                                                  scale=cols[:, R_CLG + ct:R_CLG + ct + 1],
                                                                  bias=cols[:, R_CLB + ct:R_CLB + ct + 1]),
                             reads=[("cvx", ct), "cols"], writes=[("yb", ct)])
                    for ct in range(4):
                        T.op("act", lambda e, ct=ct: e.activation(out=sq[:, ct, 0:NT], in_=yb[:, ct, 0:NT], func=AF.Square),
                             reads=[("yb", ct)], writes=[("sq", ct)])

                def st_ones_a(t):
                    rms_cols(t, 72, sq, "sq")

                def st_out(t, part):
                    par = t["ti"] % 2
                    k0, base = (4, 64) if part == "g" else (0, 72)
                    for ci, c in enumerate(t["tl"]):
                        for hf in range(2):
                            b = bank()
                            T.mm([(banks[b][:], yb[:, kt, ci * 128:(ci + 1) * 128], w_out_sb[:, kt, hf * 512:(hf + 1) * 512])
                                  for kt in range(k0, k0 + 4)], reads=[("yb", i) for i in range(k0, k0 + 4)] + ["wout"],
                                 writes=[("ps", b)])
                            col = base + par * 4 + ci
                            T.op("dve", lambda e, b=b, c=c, hf=hf, col=col: e.scalar_tensor_tensor(
                                x1[:, c, hf * 512:(hf + 1) * 512], banks[b][:], small[:, col:col + 1],
                                x1[:, c, hf * 512:(hf + 1) * 512], ALU.mult, ALU.add),
                                reads=[("ps", b), ("x1", c), ("small", base + par * 4)], writes=[("x1", c)])

                K_ = len(TL)
                gua_b = gua[:].rearrange("p a b -> p (a b)").bitcast(BF16).rearrange("p (k n) -> p k n", k=8)
                GK = [("gua", i) for i in range(4)]

                def norm2_tile(t, which):
                    tl = t["tl"]
                    if which == "halo":
                        norm_group([x1[:, 0, :]], [("x1", 0)], 16, 24, lambda dt: h2h[:, dt, 0:2], "h2h",
                                   (junk, xns), 16, lo=126, junk_key="gl")
                    elif which == 0:
                        norm_group([x1[:, c, :] for c in tl], [("x1", c) for c in tl], 16, 24,
                                   lambda dt: hb[:, dt, :], "hb", (junk, xns), 16, junk_key="gl")
                    else:
                        norm_group([x1[:, c, :] for c in tl], [("x1", c) for c in tl], 16, 24,
                                   lambda dt: gua_b[:, dt, :], GK, (junk, xns), 16, junk_key="gl")

                if half == 1:
                    load_w_out()
                st_norm(TL[0]); st_z_gv(TL[0]); st_z_a(TL[0]); st_z_gu(TL[0])
                st_spatial(TL[0])
                for i in range(K_):
                    t = TL[i]
                    nx = TL[i + 1] if i + 1 < K_ else None
                    if half == 0 and nx is None:
                        late_blocks_start(gvn)
                    st_conv_mm(t, (0, 1))
                    if nx is not None:
                        st_norm(nx)
                    elif half == 1:
                        norm2_tile(TL[K_ - 2], 0)
                    st_conv_mm(t, (2, 3))
                    if half == 1 and i == 0:
                        fold_w_out()
                    st_conv_evac(t)
                    if nx is not None:
                        st_z_a(nx, (0, 1))
                    st_ones_c(t)
                    if half == 0 and i == 0:
                        hook_gt1()
                        fold_w_out()
                    if nx is not None:
                        st_z_a(nx, (2, 3))
                        st_z_gv(nx)
                        st_z_gu(nx)
                        if i + 1 == K_ - 1:
                            load_slice_up(0)
                    if half == 0 and nx is None:
                        def _n2():
                            norm2_tile(TL[0], "halo")
                            norm2_tile(TL[K_ - 2], 0)
                        late_blocks(after_mod=_n2)
                    if nx is None:
                        load_slice_down(0)
                    st_ones_g(t)
                    st_out(t, "g")
                    st_ones_a(t)
                    st_out(t, "a")
                    if nx is not None:
                        st_spatial(nx)
                    else:
                        norm2_tile(t, 1)
                if half == 0:
                    T.op("pool", lambda e: e.tensor_copy(a_carry[:], a_buf[:, :, ntok: ntok + 30]),
                         reads=[("a", len(tiles) - 1)], writes=["a_carry"])
                T.barrier()
            if debug is not None and half == 0:
                dbg("x1", x1[:], [128, 9, D], [("x1", c) for c in range(9)])
                dbg("modc", modc[:], [128, 48], ["modc"])
                dbg("gt1", gt1[:], [128, D], ["gt1"])
                dbg("gt2", gt2[:], [128, D], ["gt2"])

            with contextlib.ExitStack() as sB:
                junk = sb(sB, "junkB", [128, D], BF16)
                xns = [sb(sB, "xnB%d" % i, [128, D], BF16) for i in range(2)]
                Uv = [sb(sB, "Uv%d" % i, [128, 514], F32) for i in range(2)]
                Ug = [sb(sB, "Ug%d" % i, [128, 514], F32) for i in range(2)]
                av = [sb(sB, "av%d" % i, [128, 512], F32) for i in range(2)]
                ag = [sb(sB, "ag%d" % i, [128, 512], F32) for i in range(2)]
                actb = [sb(sB, "actb%d" % i, [128, 6, 512], BF16) for i in range(2)]
                ostg = [sb(sB, "ostg%d" % i, [128, D], F32) for i in range(2)]

                own = list(range(1, 9)) if half == 0 else list(range(0, 8))
                if half == 0:
                    print("phase B SBUF bytes remaining:", nc.sbuf_bytes_remaining)

                def load_slice(s_):
                    load_slice_up(s_)
                    load_slice_down(s_)
                    return slice_views(s_)

                winfo = {0: slice_views(0)}
                gua_b = gua[:].rearrange("p a b -> p (a b)").bitcast(BF16).rearrange("p (k n) -> p k n", k=8)
                h2t = [hb[:], gua_b]
                h2k = [["hb"], [("gua", i) for i in range(4)]]
                if debug is not None and half == 0:
                    dbg("hbd", hb[:], [128, 8, 512], ["hb"])
                    dbg("guad", gua_b, [128, 8, 512], [("gua", i) for i in range(4)])

                steps = [(s_, tt_) for s_ in range(4) for tt_ in range(2)]
                items = [(si, fi) for si, (s_, tt_) in enumerate(steps) for fi in range(len(SLICES[s_]))]
                DD = 2

                def head(g, si, fi):
                    s_, tt = steps[si]
                    f = SLICES[s_][fi]
                    wv, wg, wd, wkey = winfo[s_]
                    par = g % 2
                    if half == 0 and tt == 0:
                        for gi, wsrc in enumerate((wv, wg)):
                            b = bank()
                            T.mm([(banks[b][:, 0:2], wsrc[:, kt, fi * 128:(fi + 1) * 128], h2h[:, kt, 0:2]) for kt in range(8)],
                                 reads=[(wkey, "ab"[gi]), "h2h"], writes=[("ps", b)])
                            T.op("act", lambda e, b=b, f=f, gi=gi: e.activation(
                                out=carry[:, 0, f, gi, :], in_=banks[b][:, 0:2], func=AF.Identity, scale=msk[:, 0:1]),
                                reads=[("ps", b), "msk"], writes=[("carry", 0, f, gi)])
                    for gi, (wsrc, Ub, accb, nm) in enumerate(((wv, Uv, av, "v"), (wg, Ug, ag, "g"))):
                        j = f + gi * NFT
                        U = Ub[par]
                        acc = accb[par]
                        ukey = ("U" + nm, par)
                        ackey = ("acc" + nm, par)
                        b = bank()
                        T.mm([(banks[b][:], wsrc[:, kt, fi * 128:(fi + 1) * 128], h2t[tt][:, kt, :])
                              for kt in range(8)], reads=[(wkey, "ab"[gi])] + h2k[tt], writes=[("ps", b)])
                        T.op("act", lambda e, b=b, U=U: e.activation(out=U[:, 2:514], in_=banks[b][:], func=AF.Copy),
                             reads=[("ps", b)], writes=[ukey])
                        T.op("act", lambda e, b=b, f=f, gi=gi, tt=tt: e.activation(
                            out=carry[:, (tt + 1) % 2, f, gi, :], in_=banks[b][:, 510:512], func=AF.Copy),
                            reads=[("ps", b)], writes=[("carry", (tt + 1) % 2, f, gi)])
                        T.op("act", lambda e, U=U, f=f, gi=gi, tt=tt: e.activation(
                            out=U[:, 0:2], in_=carry[:, tt % 2, f, gi, :], func=AF.Copy),
                             reads=[("carry", tt % 2, f, gi)], writes=[ukey])
                        T.op("act", lambda e, b=b, acc=acc, j=j: e.activation(
                            out=acc[:], in_=banks[b][:], func=AF.Identity,
                            scale=cols[:, R_FW + 2 * 44 + j: R_FW + 2 * 44 + j + 1], bias=cols[:, R_FB + j: R_FB + j + 1]),
                            reads=[("ps", b), "cols"], writes=[ackey])
                        T.op("dve", lambda e, U=U, acc=acc, j=j: e.scalar_tensor_tensor(
                            acc[:], U[:, 1:513], cols[:, R_FW + 44 + j: R_FW + 44 + j + 1], acc[:], ALU.mult, ALU.add),
                            reads=[ukey, ackey, "cols"], writes=[ackey])
                        T.op("dve", lambda e, U=U, acc=acc, j=j: e.scalar_tensor_tensor(
                            acc[:], U[:, 0:512], cols[:, R_FW + j: R_FW + j + 1], acc[:], ALU.mult, ALU.add),
                            reads=[ukey, ackey, "cols"], writes=[ackey])

                def tail(g, si, fi):
                    par = g % 2
                    accv, accg = av[par], ag[par]
                    kv, kg = ("accv", par), ("accg", par)
                    ab = actb[si % 2]
                    T.op("act", lambda e: e.activation(out=accg[:], in_=accg[:], func=AF.Silu), reads=[kg], writes=[kg])
                    T.op("dve", lambda e: e.tensor_tensor(ab[:, fi, :], accv[:], accg[:], ALU.mult),
                         reads=[kv, kg], writes=[("actb", si % 2)])

                def down(si):
                    s_, tt = steps[si]
                    nf = len(SLICES[s_])
                    wv, wg, wd, wkey = winfo[s_]
                    ab = actb[si % 2]
                    akey = ("actb", si % 2)
                    for ci in range(4):
                        c = own[tt * 4 + ci]
                        for hf in range(2):
                            b = bank()
                            T.mm([(banks[b][:], ab[:, fi, ci * 128:(ci + 1) * 128], wd[:, fi, hf * 512:(hf + 1) * 512])
                                  for fi in range(nf)], reads=[akey, (wkey, "c")], writes=[("ps", b)])
                            T.op("dve", lambda e, b=b, c=c, hf=hf: e.tensor_tensor(
                                x1[:, c, hf * 512:(hf + 1) * 512], banks[b][:], x1[:, c, hf * 512:(hf + 1) * 512], ALU.add),
                                reads=[("ps", b), ("x1", c)], writes=[("x1", c)])
                        if s_ == 3 and half == 1 and tt == 1:
                            final_norm([c], tt * 4 + ci)
                    if s_ == 3 and not (half == 1 and tt == 1):
                        final_norm([own[tt * 4 + ci] for ci in range(4)], tt * 4)
                        if half == 0:
                            load_x(1, [tt * 4 + i for i in range(4)])

                def final_norm(cs, o0):
                    n = len(cs)
                    for ci, c in enumerate(cs):
                        T.op("act", lambda e, c=c, ci=ci: e.activation(out=junk[:], in_=x1[:, c, :], func=AF.Square,
                                                                       accum_out=small[:, 32 + ci:33 + ci]),
                             reads=[("x1", c)], writes=["junk", ("small", 32)])
                    T.op("dve", lambda e: e.tensor_scalar(small[:, 36:36 + n], small[:, 32:32 + n], 1.0 / D, RMS_EPS, ALU.mult, ALU.add),
                         reads=[("small", 32)], writes=[("small", 36)])
                    T.op("pool", lambda e: e.tensor_tensor(small[:, 40:40 + n], small[:, 36:36 + n], mhalf[:, 0:n], ALU.pow),
                         reads=[("small", 36), "mhalf"], writes=[("small", 40)])
                    for ci, c in enumerate(cs):
                        og = ostg[(o0 + ci) % 2]
                        okey = ("ostg", (o0 + ci) % 2)
                        T.op("dve", lambda e, c=c, og=og, ci=ci: e.scalar_tensor_tensor(
                            og[:], x1[:, c, :], small[:, 40 + ci:41 + ci], rows[:, O_FING:O_FING + D], ALU.mult, ALU.mult),
                            reads=[("x1", c), ("small", 40), "rows"], writes=[okey])
                        r0 = half * 1024 + (o0 + ci) * 128
                        T.dma("sp", out_d[r0:r0 + 128, :], og[:], reads=[okey])

                pending = []
                G = len(items)
                for g in range(G + 1):
                    if g < G:
                        si, fi = items[g]
                        s_, tt = steps[si]
                        if fi == 0 and tt == 1 and s_ + 1 < 4:
                            winfo[s_ + 1] = load_slice(s_ + 1)
                        if half == 0 and fi == 0 and tt == 1 and s_ == 3:
                            load_w_in()
                        head(g, si, fi)
                    if g >= 1:
                        psi, pfi = items[g - 1]
                        tail(g - 1, psi, pfi)
                        if pfi == len(SLICES[steps[psi][0]]) - 1:
                            pending.append([DD, psi])
                    for p in list(pending):
                        if p[0] <= 0 or g >= G:
                            down(p[1])
                            pending.remove(p)
                        else:
                            p[0] -= 1
                T.barrier()
        T.finish()
    return nc, dbg_out


def _pack_inputs(inputs):
    f = np.float32
    x = np.asarray(inputs["x"], f)
    c = np.asarray(inputs["c"], f)
    g = lambda k: np.asarray(inputs[k], f)[0]
    vec = np.zeros((NROWS, 128), f)
    vec[R_BADA:R_BADA + 48] = g("b_ada").reshape(48, 128)
    vec[R_G1:R_G1 + 8] = g("norm1_gain").reshape(8, 128)
    vec[R_CW:R_CW + 124] = g("conv_dw_w").reshape(31 * 4, 128)
    vec[R_CB:R_CB + 4] = g("conv_dw_b").reshape(4, 128)
    vec[R_CLG:R_CLG + 4] = g("conv_ln_g").reshape(4, 128)
    vec[R_CLB:R_CLB + 4] = g("conv_ln_b").reshape(4, 128)
    vec[R_MOG:R_MOG + 8] = g("mix_out_gain").reshape(8, 128)
    vec[R_G2:R_G2 + 8] = g("norm2_gain").reshape(8, 128)
    vec[R_FW:R_FW + 132] = g("ffn_dw_w").reshape(3 * 44, 128)
    vec[R_FB:R_FB + 44] = g("ffn_dw_b").reshape(44, 128)
    rows = np.zeros((128, NRB), f)
    rows[:, O_GMG:O_GMG + 512] = g("gm_ln_g")[None, :]
    rows[:, O_GMB:O_GMB + 512] = g("gm_ln_b")[None, :]
    rows[:, O_FING:O_FING + 1024] = np.asarray(inputs["final_gain"], f)[None, :]
    b_ada = g("b_ada")
    rows[:, O_BGT1:O_BGT1 + 1024] = b_ada[2048:3072][None, :]
    rows[:, O_BGT2:O_BGT2 + 1024] = b_ada[5120:6144][None, :]
    bs = g("gm_bs")
    rows[:, O_BS:O_BS + 512] = np.repeat(bs, 64, axis=0).reshape(4, 128, 128).transpose(1, 0, 2).reshape(128, 512)
    wsT = np.ascontiguousarray(g("gm_ws").transpose(2, 0, 1))
    shared = {
        "vecs": vec, "rows": rows, "wsT": wsT,
        "w_ada": g("w_ada"), "w_in": g("w_in"), "w_out": g("w_out"),
        "w_up": g("w_up"), "w_down": g("w_down"),
    }
    in_maps = []
    for i in range(NCORES):
        b, q = divmod(i, 4)
        t0 = q * TOK_PER_CORE
        xe = np.zeros((17 * 128, D), f)
        if q > 0:
            xe[0:128] = x[b, t0 - 128:t0]
        xe[128:] = x[b, t0:t0 + TOK_PER_CORE]
        m = dict(shared)
        m["x_ext"] = xe
        m["msk"] = np.full((128, 1), 1.0 if q > 0 else 0.0, f)
        m["c_col"] = np.ascontiguousarray(c[b].reshape(8, 128).T)
        in_maps.append(m)
    return in_maps


_NC_CACHE = {}


def kernel(**inputs):
    if "nc" not in _NC_CACHE:
        _NC_CACHE["nc"] = build_program()[0]
    nc = _NC_CACHE["nc"]
    in_maps = _pack_inputs(inputs)
    res = run_bass_kernel_spmd(nc, in_maps, core_ids=list(range(NCORES)))
    out = np.zeros((2, SEQ, D), np.float32)
    for i in range(NCORES):
        b, q = divmod(i, 4)
        out[b, q * TOK_PER_CORE:(q + 1) * TOK_PER_CORE] = res.results[i]["out"]
    return out
```

```python
import contextlib
import numpy as np
import concourse.bass as bass
import concourse.mybir as mybir
from concourse.bass_utils import run_bass_kernel_spmd

F32 = mybir.dt.float32
BF16 = mybir.dt.bfloat16
AF = mybir.ActivationFunctionType
ALU = mybir.AluOpType

D = 1024
DFF = 2816
NFT = 22
SEQ = 8192
NCORES = 8
TOK_PER_CORE = 2048
NCH = 16
RMS_EPS = 1e-6
LN_EPS = 1e-5
KW = 31

R_BADA = 0
R_G1 = 48
R_CW = 56
R_CB = 180
R_CLG = 184
R_CLB = 188
R_MOG = 192
R_G2 = 200
R_FW = 208
R_FB = 340
NROWS = 384
O_GMG = 0
O_GMB = 512
O_FING = 1024
O_BS = 2048
NRP = 2560
O_BGT1 = 2560
O_BGT2 = 3584
NRB = 4608

SLICES = [list(range(0, 6)), list(range(6, 12)), list(range(12, 17)), list(range(17, 22))]
DIAG_ENG = "dve"
SLOT_ELEMS = 18432


class Tracker:
    def __init__(self, nc, es):
        self.nc = nc
        self.eng = {"pe": nc.tensor, "act": nc.scalar, "dve": nc.vector,
                    "pool": nc.gpsimd, "sp": nc.sync}
        self.sem = {}
        for e in ("pe", "act", "dve", "pool"):
            self.sem[e] = es.enter_context(nc.semaphore("c_" + e))
        self.cnt = {e: 0 for e in self.sem}
        self.waited = {e: {} for e in self.eng}
        self.state = {}
        self.ndma = 72
        self.dsem = [es.enter_context(nc.semaphore("d%d" % i)) for i in range(self.ndma)]
        self.dval = [0] * self.ndma
        self.drr = 0
        self.semname = {}
        for e, s in self.sem.items():
            self.semname[id(s)] = e
        self.n_wait = 0
        self.n_inst = 0

    def _deps(self, eng, reads, writes):
        deps = []
        for k in reads:
            st = self.state.get(k)
            if st is not None and st[0] is not None:
                deps.append(st[0] + ("raw",))
        for k in writes:
            st = self.state.get(k)
            if st is not None:
                if st[0] is not None:
                    deps.append(st[0] + ("waw",))
                for d in st[1].values():
                    deps.append(d + ("war",))
        return deps

    def _wait(self, eng, deps):
        best = {}
        for (sem, val, prod, kind) in deps:
            if prod == eng:
                if eng == "pe" or kind != "raw":
                    continue
            key = id(sem)
            if key not in best or best[key][1] < val:
                best[key] = (sem, val)
        for key, (sem, val) in best.items():
            if self.waited[eng].get(key, 0) >= val:
                continue
            self.eng[eng].wait_ge(sem, val)
            self.waited[eng][key] = val
            self.n_wait += 1

    def _update(self, eng, dep, reads, writes):
        for k in writes:
            self.state[k] = [dep, {}]
        for k in reads:
            st = self.state.get(k)
            if st is None:
                st = [None, {}]
                self.state[k] = st
            st[1][dep[2] if dep[2] != "dma" else ("dma", id(dep[0]))] = dep

    def op(self, eng, fn, reads=(), writes=()):
        self._wait(eng, self._deps(eng, reads, writes))
        ins = fn(self.eng[eng])
        self.cnt[eng] += 1
        ins.then_inc(self.sem[eng], 1)
        self.n_inst += 1
        dep = (self.sem[eng], self.cnt[eng], eng)
        self._update(eng, dep, reads, writes)
        return dep

    def mm(self, mms, reads=(), writes=(), single=False):
        self._wait("pe", self._deps("pe", reads, writes))
        n = len(mms)
        ins = None
        for i, (out, lhsT, rhs) in enumerate(mms):
            if single:
                ins = self.nc.tensor.matmul(out, lhsT, rhs, start=True, stop=True)
            else:
                ins = self.nc.tensor.matmul(out, lhsT, rhs, start=(i == 0), stop=(i == n - 1))
        self.cnt["pe"] += 1
        ins.then_inc(self.sem["pe"], 1)
        self.n_inst += n
        dep = (self.sem["pe"], self.cnt["pe"], "pe")
        self._update("pe", dep, reads, writes)
        return dep

    def transposes(self, tps, ident, reads=(), writes=()):
        self._wait("pe", self._deps("pe", reads, writes))
        ins = None
        for (out, in_) in tps:
            ins = self.nc.tensor.transpose(out, in_, ident)
        self.cnt["pe"] += 1
        ins.then_inc(self.sem["pe"], 1)
        self.n_inst += len(tps)
        dep = (self.sem["pe"], self.cnt["pe"], "pe")
        self._update("pe", dep, reads, writes)
        return dep

    def dma(self, q, out, in_, reads=(), writes=()):
        self._wait(q, self._deps(q, reads, writes))
        i = self.drr
        self.drr = (self.drr + 1) % self.ndma
        sem = self.dsem[i]
        if self.dval[i] > 0 and self.waited[q].get(id(sem), 0) < self.dval[i]:
            self.eng[q].wait_ge(sem, self.dval[i])
            self.waited[q][id(sem)] = self.dval[i]
        self.eng[q].dma_start(out=out, in_=in_).then_inc(sem, 16)
        self.dval[i] += 16
        self.n_inst += 1
        dep = (sem, self.dval[i], "dma")
        self._update("dma", dep, reads, writes)
        return dep

    def barrier(self):
        for e in self.eng:
            for p in self.sem:
                if p == e or self.cnt[p] == 0:
                    continue
                if self.waited[e].get(id(self.sem[p]), 0) < self.cnt[p]:
                    self.eng[e].wait_ge(self.sem[p], self.cnt[p])
                    self.waited[e][id(self.sem[p])] = self.cnt[p]
            for i in range(self.ndma):
                if self.dval[i] > 0 and self.waited[e].get(id(self.dsem[i]), 0) < self.dval[i]:
                    self.eng[e].wait_ge(self.dsem[i], self.dval[i])
                    self.waited[e][id(self.dsem[i])] = self.dval[i]

    def finish(self):
        e = "sp"
        for i in range(self.ndma):
            if self.dval[i] > 0 and self.waited[e].get(id(self.dsem[i]), 0) < self.dval[i]:
                self.eng[e].wait_ge(self.dsem[i], self.dval[i])
                self.waited[e][id(self.dsem[i])] = self.dval[i]
        for p in self.sem:
            if self.cnt[p] and self.waited[e].get(id(self.sem[p]), 0) < self.cnt[p]:
                self.eng[e].wait_ge(self.sem[p], self.cnt[p])


def build_program(debug=None):
    nc = bass.Bass("TRN2", target_bir_lowering=False)
    dbg_out = {}

    def din(name, shape):
        return nc.dram_tensor(name, list(shape), F32, kind="ExternalInput").ap()

    x_ext = din("x_ext", [17 * 128, D])
    msk_d = din("msk", [128, 1])
    ccol_d = din("c_col", [128, 8])
    vecs_d = din("vecs", [NROWS, 128])
    rows_d = din("rows", [128, NRB])
    wsT_d = din("wsT", [128, 8, 128])
    w_ada = din("w_ada", [D, 6 * D])
    w_in = din("w_in", [D, 2 * D])
    w_out = din("w_out", [D, D])
    w_up = din("w_up", [D, 2 * DFF])
    w_down = din("w_down", [DFF, D])
    out_d = nc.dram_tensor("out", [TOK_PER_CORE, D], F32, kind="ExternalOutput").ap()
    diag_scr = nc.dram_tensor("diag_scr", [4, 128, KW * 128], BF16, kind="Internal").ap()

    uid = [0]

    with contextlib.ExitStack() as es:
        T = Tracker(nc, es)

        def sb(st, name, shape, dt):
            uid[0] += 1
            return st.enter_context(nc.sbuf_tensor("%s_%d" % (name, uid[0]), list(shape), dt))

        banks = [es.enter_context(nc.psum_tensor("bank%d" % i, [128, 512], F32)) for i in range(8)]
        brr = [0]

        held = set()

        def bank(hold=False):
            for _ in range(8):
                i = brr[0]
                brr[0] = (i + 1) % 8
                if i not in held:
                    if hold:
                        held.add(i)
                    return i
            raise RuntimeError("all PSUM banks held")

        def release(i):
            held.discard(i)

        cols = sb(es, "cols", [128, NROWS], F32)
        modc = sb(es, "modc", [128, 48], F32)
        ident_f = sb(es, "ident_f", [128, 128], F32)
        ident_b = sb(es, "ident_b", [128, 128], BF16)
        ones_b = sb(es, "ones_b", [128, 128], BF16)
        cmask = sb(es, "cmask", [128, 128], F32)
        iot = sb(es, "iot", [128, 128], F32)
        rows = sb(es, "rows", [128, NRP], F32)
        gt1 = sb(es, "gt1", [128, D], F32)
        gt2 = sb(es, "gt2", [128, D], F32)
        wsT = sb(es, "wsT", [128, 8, 128], BF16)
        msk = sb(es, "msk", [128, 1], F32)
        ccol = sb(es, "ccol", [128, 8], F32)
        cact = sb(es, "cact", [128, 8], BF16)
        cact_rep = sb(es, "cact_rep", [128, 8, 128], BF16)
        carry = sb(es, "carry", [128, 2, NFT, 2, 2], F32)
        cwb = sb(es, "cwb", [128, 4, KW], BF16)
        a_carry = sb(es, "a_carry", [128, 4, 30], BF16)
        small = sb(es, "small", [128, 96], F32)
        mraw = sb(es, "mraw", [128, 32], F32)
        mhalf = sb(es, "mhalf", [128, 8], F32)
        epsr = sb(es, "epsr", [128, 2], F32)
        hb = sb(es, "hb", [128, 8, 512], BF16)
        gua = sb(es, "gua", [128, 4, 512], F32)
        h2h = sb(es, "h2h", [128, 8, 2], BF16)
        x1 = sb(es, "x1", [128, 9, D], F32)
        slot = [sb(es, "slot0", [128, SLOT_ELEMS], BF16), sb(es, "slot1", [128, SLOT_ELEMS], BF16)]

        def dbg(name, ap, shape, reads):
            if debug is None or name not in debug:
                return
            t = nc.dram_tensor("dbg_" + name, list(shape), ap.dtype, kind="ExternalOutput").ap()
            dbg_out[name] = t
            T.dma("sp", t, ap, reads=reads)

        for m_, off_, keys_ in ((0, 0, ["wout"]), (1, 8192, [("diag", 0), ("diag", 1)])):
            T.dma("pool", slot[1][:, off_:off_ + 8192].rearrange("p (k n) -> p k n", k=8),
                  w_ada[:, m_ * D:(m_ + 1) * D].rearrange("(kt p) n -> p kt n", p=128), writes=keys_)
        T.dma("sp", msk[:], msk_d, writes=["msk"])
        T.dma("sp", ccol[:], ccol_d, writes=["ccol"])
        T.dma("sp", rows[:], rows_d[:, 0:NRP], writes=["rows"])
        T.op("pool", lambda e: e.iota(iot[:], [[1, 128]], base=0, channel_multiplier=-1,
                                      allow_small_or_imprecise_dtypes=True), writes=["iot"])
        T.op("pool", lambda e: e.tensor_single_scalar(ident_f[:], iot[:], 0.0, ALU.is_equal),
             reads=["iot"], writes=["ident_f"])
        T.op("pool", lambda e: e.tensor_single_scalar(cmask[:], iot[:], 0.0, ALU.is_ge),
             reads=["iot"], writes=["cmask"])
        T.op("pool", lambda e: e.tensor_copy(ident_b[:], ident_f[:]), reads=["ident_f"], writes=["ident_b"])
        T.op("pool", lambda e: e.memset(ones_b[:], 1.0), writes=["ones_b"])
        T.op("pool", lambda e: e.memset(small[:], 0.0), writes=["small"])
        T.op("pool", lambda e: e.memset(mhalf[:], -0.5), writes=["mhalf"])
        T.op("pool", lambda e: e.memset(epsr[:, 0:1], RMS_EPS), writes=["epsr"])
        T.op("pool", lambda e: e.memset(epsr[:, 1:2], LN_EPS), writes=["epsr"])

        wadaA = slot[1][:, 0:8192].rearrange("p (k n) -> p k n", k=8)
        wadaB = slot[1][:, 8192:16384].rearrange("p (k n) -> p k n", k=8)
        wadaC = x1[:, 5:9, :].rearrange("p c d -> p (c d)").bitcast(BF16).rearrange("p (k n) -> p k n", k=8)
        KA = ["wout"]
        KB = [("diag", 0), ("diag", 1)]
        KC = [("x1", c) for c in range(5, 9)]

        def ada_cols(wb, keys, bcol, cm):
            for j in range(8):
                T.mm([(banks[bcol][:, cm * 8 + j: cm * 8 + j + 1], wb[:, kt, j * 128:(j + 1) * 128],
                       cact[:, kt:kt + 1]) for kt in range(8)], reads=keys + ["cact"], writes=[("ps", bcol)])

        def ada_rows(wb, keys, dst, dkey, brow, bkeys):
            for hf in range(2):
                b = bank()
                T.mm([(banks[b][:], cact_rep[:, kt, :], wb[:, kt, hf * 512:(hf + 1) * 512])
                      for kt in range(8)], reads=keys + ["cact_rep"], writes=[("ps", b)])
                T.op("dve", lambda e, b=b, hf=hf: e.tensor_tensor(
                    dst[:, hf * 512:(hf + 1) * 512], banks[b][:], brow[:, hf * 512:(hf + 1) * 512], ALU.add),
                    reads=[("ps", b)] + bkeys, writes=[dkey])

        def ada_dma(m, wb, keys):
            T.dma("pool", wb, w_ada[:, m * D:(m + 1) * D].rearrange("(kt p) n -> p kt n", p=128), writes=keys)

        def mod_raw(bcol, cm, mod):
            T.op("dve", lambda e: e.tensor_tensor(
                mraw[:, cm * 8:(cm + 1) * 8], banks[bcol][:, cm * 8:(cm + 1) * 8],
                cols[:, R_BADA + mod * 8: R_BADA + mod * 8 + 8], ALU.add),
                reads=[("ps", bcol), "cols"], writes=["mraw"])

        def mod_finish(sh_c, sc_c, so, g):
            T.op("dve", lambda e: e.scalar_tensor_tensor(
                modc[:, so:so + 8], mraw[:, sc_c * 8:sc_c * 8 + 8], 1.0, cols[:, g:g + 8], ALU.add, ALU.mult),
                reads=["mraw", "cols"], writes=["modc"])
            T.op("dve", lambda e: e.tensor_copy(modc[:, so + 8:so + 16], mraw[:, sh_c * 8:sh_c * 8 + 8]),
                 reads=["mraw"], writes=["modc"])

        bgt = slot[0][:, 16384:18432].bitcast(F32)
        kst = [(0, "c")]

        def setup_part1(gl, sq, stat):
            vecs = gl[:, 0:384].rearrange("p (j f) -> p j f", j=3)
            wsf = sq[:].rearrange("p a b -> p (a b)").bitcast(F32).rearrange("p (h t) -> p h t", h=8)
            ksq = [("sq", i) for i in range(4)]
            T.dma("sp", vecs, vecs_d.rearrange("(j p) f -> p j f", p=128), writes=["gl"])
            T.dma("sp", wsf, wsT_d, writes=ksq)
            load_w_in(after=KB)
            ada_dma(2, wadaC, KC)
            T.dma("sp", bgt, rows_d[:, O_BGT1:O_BGT1 + D], writes=kst)
            for j in range(3):
                b = bank()
                T.transposes([(banks[b][:, 0:128], vecs[:, j, :])], ident_f[:],
                             reads=["gl", "ident_f"], writes=[("ps", b)])
                T.op("dve", lambda e, b=b, j=j: e.tensor_copy(cols[:, j * 128:(j + 1) * 128], banks[b][:, 0:128]),
                     reads=[("ps", b)], writes=["cols"])
            T.op("dve", lambda e: e.tensor_tensor(
                wsT[:], wsf, cmask[:].unsqueeze(1).to_broadcast([128, 8, 128]), ALU.mult),
                reads=ksq + ["cmask"], writes=["wsT"])
            T.op("act", lambda e: e.activation(out=cact[:], in_=ccol[:], func=AF.Silu),
                 reads=["ccol"], writes=["cact"])
            for kt in range(8):
                T.op("dve", lambda e, kt=kt: e.tensor_copy(
                    cact_rep[:, kt, :], cact[:, kt:kt + 1].to_broadcast([128, 128])),
                    reads=["cact"], writes=["cact_rep"])
            T.op("dve", lambda e: e.tensor_copy(
                cwb[:], cols[:, R_CW:R_CW + 4 * KW].rearrange("p (k c) -> p c k", c=4)),
                reads=["cols"], writes=["cwb"])
            regs = [
                (hb[:].rearrange("p a b -> p (a b)")[:, 0:KW * 128], ["hb"]),
                (gua[:].rearrange("p a b -> p (a b)").bitcast(BF16)[:, 0:KW * 128], [("gua", i) for i in range(4)]),
                (x1[:, 1:3, :].rearrange("p a b -> p (a b)").bitcast(BF16)[:, 0:KW * 128], [("x1", 1), ("x1", 2)]),
                (x1[:, 3:5, :].rearrange("p a b -> p (a b)").bitcast(BF16)[:, 0:KW * 128], [("x1", 3), ("x1", 4)]),
            ]
            for ct in range(4):
                rv, rk = regs[ct]
                T.op("dve", lambda e, ct=ct, rv=rv: e.tensor_tensor(
                    rv.rearrange("p (k n) -> p k n", k=KW), ident_b[:].unsqueeze(1).to_broadcast([128, KW, 128]),
                    cwb[:, ct, :].unsqueeze(2).to_broadcast([128, KW, 128]), ALU.mult),
                    reads=["ident_b", "cwb"], writes=rk)
                T.dma("sp", diag_scr[ct], rv, reads=rk, writes=[("dscr", ct)])
            bcol = bank()
            ada_cols(wadaA, KA, bcol, 0)
            mod_raw(bcol, 0, 0)
            T.dma("pool", w_out_sb, w_out.rearrange("(kt p) n -> p kt n", p=128), writes=["wout"])
            bcol = bank()
            ada_cols(wadaB, KB, bcol, 1)
            mod_raw(bcol, 1, 1)
            mod_finish(0, 1, 0, R_G1)

        def hook_gt1():
            ada_rows(wadaC, KC, gt1, "gt1", bgt, kst)
            T.dma("sp", bgt, rows_d[:, O_BGT2:O_BGT2 + D], writes=kst)
            load_x(0, [5, 6, 7, 8])

        late = {}

        def late_regions(gvn):
            regs = []
            for r in range(2):
                v = gua[:, 2 * r:2 * r + 2, :].rearrange("p a b -> p (a b)").bitcast(BF16).rearrange("p (k n) -> p k n", k=8)
                regs.append((v, [("gua", 2 * r), ("gua", 2 * r + 1)]))
            for r in range(2):
                v = hb[:, 4 * r:4 * r + 4, :].rearrange("p a b -> p (a b)").rearrange("p (k n) -> p k n", k=8)
                regs.append((v, ["hb"]))
            regs.append((gvn[:].rearrange("p a b -> p (a b)").rearrange("p (k n) -> p k n", k=8), [("gvn", i) for i in range(4)]))
            return regs

        RMAP = [0, 1, 4, 2, 3, 0, 1, 4, 0, 1, 4, 0]

        def late_issue(i):
            m, q = late["plan"][i]
            wb, keys = late["regs"][RMAP[i]]
            T.dma("pool", wb, w_ada[:, m * D + q * 256: m * D + (q + 1) * 256].rearrange("(kt p) n -> p kt n", p=128),
                  writes=keys)
            late["issued"][i] = (wb, keys)

        def late_blocks_start(gvn):
            late["regs"] = late_regions(gvn)
            late["plan"] = [(m, q) for m in (3, 4, 5) for q in range(4)]
            late["issued"] = {}
            seen = set()
            for i, r in enumerate(RMAP):
                if r not in seen:
                    seen.add(r)
                    late_issue(i)

        def late_blocks(after_mod=None):
            plan = late["plan"]
            for i, (m, q) in enumerate(plan):
                wb, keys = late["issued"][i]
                if m in (3, 4):
                    cm = 2 if m == 3 else 3
                    bcol = bank()
                    for j in range(2):
                        T.mm([(banks[bcol][:, j:j + 1], wb[:, kt, j * 128:(j + 1) * 128], cact[:, kt:kt + 1]) for kt in range(8)],
                             reads=keys + ["cact"], writes=[("ps", bcol)])
                    c0 = cm * 8 + q * 2
                    T.op("dve", lambda e, bcol=bcol, c0=c0, m=m, q=q: e.tensor_tensor(
                        mraw[:, c0:c0 + 2], banks[bcol][:, 0:2],
                        cols[:, R_BADA + m * 8 + q * 2: R_BADA + m * 8 + q * 2 + 2], ALU.add),
                        reads=[("ps", bcol), "cols"], writes=["mraw"])
                else:
                    b = bank()
                    T.mm([(banks[b][:, 0:256], cact_rep[:, kt, :], wb[:, kt, :]) for kt in range(8)],
                         reads=keys + ["cact_rep"], writes=[("ps", b)])
                    T.op("dve", lambda e, b=b, q=q: e.tensor_tensor(
                        gt2[:, q * 256:(q + 1) * 256], banks[b][:, 0:256], bgt[:, q * 256:(q + 1) * 256], ALU.add),
                        reads=[("ps", b)] + kst, writes=["gt2"])
                for j in range(i + 1, len(plan)):
                    if RMAP[j] == RMAP[i]:
                        if j not in late["issued"]:
                            late_issue(j)
                        break
                if (m, q) == (4, 3):
                    mod_finish(2, 3, 16, R_G2)
                    if after_mod is not None:
                        after_mod()

        def load_x(half_, tl):
            xrow0_ = 0 if half_ == 0 else 9 * 128
            c0, n = tl[0], len(tl)
            T.dma("sp", x1[:, c0:c0 + n, :],
                  x_ext[xrow0_ + c0 * 128: xrow0_ + (c0 + n) * 128, :].rearrange("(c p) d -> p c d", p=128),
                  writes=[("x1", c) for c in tl])

        S0ALL = [(0, "a"), (0, "b"), (0, "c")]

        def slice_views(s_):
            nf = len(SLICES[s_])
            buf = slot[s_ % 2]
            wv = buf[:, 0:8 * nf * 128].rearrange("p (k n) -> p k n", k=8)
            wg = buf[:, 6144:6144 + 8 * nf * 128].rearrange("p (k n) -> p k n", k=8)
            wd = buf[:, 12288:12288 + nf * 1024].rearrange("p (f n) -> p f n", f=nf)
            return wv, wg, wd, s_ % 2

        def load_w_in(after=()):
            wv_ = w_in.rearrange("(kt p) n -> p kt n", p=128)
            for pi, (c0, c1) in enumerate(((1536, 2048), (0, 1024), (1024, 1536))):
                T.dma("pool", w_in_sb[:, :, c0:c1], wv_[:, :, c0:c1], reads=list(after) if pi == 1 else [],
                      writes=(S0ALL if pi == 0 else []) + [("win", pi)])

        def load_slice_down(s_):
            wv, wg, wd, key = slice_views(s_)
            f0 = SLICES[s_][0]
            nf = len(SLICES[s_])
            T.dma("pool", wd, w_down[f0 * 128:(f0 + nf) * 128, :].rearrange("(f p) n -> p f n", p=128), writes=[(key, "c")])
            for fi in range(nf):
                T.op("pool", lambda e, fi=fi, wd=wd: e.tensor_tensor(wd[:, fi, :], wd[:, fi, :], gt2[:], ALU.mult),
                     reads=[(key, "c"), "gt2"], writes=[(key, "c")])

        def load_slice_up(s_):
            nf = len(SLICES[s_])
            f0 = SLICES[s_][0] * 128
            wv, wg, wd, key = slice_views(s_)
            T.dma("pool", wv, w_up[:, f0:f0 + nf * 128].rearrange("(kt p) n -> p kt n", p=128), writes=[(key, "a")])
            T.dma("pool", wg, w_up[:, DFF + f0:DFF + f0 + nf * 128].rearrange("(kt p) n -> p kt n", p=128), writes=[(key, "b")])


        w_in_sb = slot[0][:, 0:16384].rearrange("p (k n) -> p k n", k=8)
        w_out_sb = slot[1][:, 0:8192].rearrange("p (k n) -> p k n", k=8)
        diag = [slot[1][:, 8192 + i * 3968: 8192 + (i + 1) * 3968].rearrange("p (k n) -> p k n", k=KW)
                for i in range(2)]

        def norm_group(x_aps, xkeys, scol, shcol, dst_fn, dst_key, tmp, ssl, lo=0, junk_key="junk"):
            junk, xns = tmp
            n = len(x_aps)
            dkl = dst_key if isinstance(dst_key, list) else [dst_key]
            for ci in range(n):
                T.op("act", lambda e, ci=ci: e.activation(out=junk, in_=x_aps[ci], func=AF.Square,
                                                          accum_out=small[:, ssl + ci:ssl + ci + 1]),
                     reads=[xkeys[ci]], writes=[junk_key, ("small", ssl)])
            T.op("dve", lambda e: e.tensor_scalar(small[:, ssl + 4:ssl + 4 + n], small[:, ssl:ssl + n], 1.0 / D, RMS_EPS,
                                                   ALU.mult, ALU.add),
                 reads=[("small", ssl)], writes=[("small", ssl + 4)])
            T.op("pool", lambda e: e.tensor_tensor(small[:, ssl + 8:ssl + 8 + n], small[:, ssl + 4:ssl + 4 + n],
                                                    mhalf[:, 0:n], ALU.pow),
                 reads=[("small", ssl + 4), "mhalf"], writes=[("small", ssl + 8)])
            bks = [bank() for _ in range(4)]
            pbs = [banks[b][:].bitcast(BF16) for b in bks]
            for ci in range(n):
                xn = xns[ci % 2]
                xk = ("xn", ci % 2)
                T.op("dve", lambda e, ci=ci, xn=xn: e.tensor_scalar(xn[:], x_aps[ci], small[:, ssl + 8 + ci:ssl + 9 + ci], None, ALU.mult),
                     reads=[xkeys[ci], ("small", ssl + 8)], writes=[xk])
                T.transposes([(pbs[dt // 2][:, (dt % 2) * 512 + ci * 128:(dt % 2) * 512 + (ci + 1) * 128],
                               xn[:, dt * 128:(dt + 1) * 128]) for dt in range(8)], ident_b[:],
                             reads=[xk, "ident_b"], writes=[("ps", b) for b in bks])
            for dt in range(8):
                pb = pbs[dt // 2]
                q = dt % 2
                if dt % 2 == 0:
                    T.op("act", lambda e, dt=dt, q=q, pb=pb: e.activation(
                        out=dst_fn(dt), in_=pb[:, q * 512 + lo:q * 512 + n * 128], func=AF.Identity,
                        scale=modc[:, scol + dt:scol + dt + 1], bias=modc[:, shcol + dt:shcol + dt + 1]),
                        reads=[("ps", bks[dt // 2]), "modc"], writes=dkl)
                else:
                    T.op("dve", lambda e, dt=dt, q=q, pb=pb: e.tensor_scalar(
                        dst_fn(dt), pb[:, q * 512 + lo:q * 512 + n * 128],
                        modc[:, scol + dt:scol + dt + 1], modc[:, shcol + dt:shcol + dt + 1], ALU.mult, ALU.add),
                        reads=[("ps", bks[dt // 2]), "modc"], writes=dkl)

        for half in range(2):
            nchk = 9 if half == 0 else 8
            ntok = nchk * 128
            if half == 0:
                tiles = [[0], [1, 2, 3, 4], [5, 6, 7, 8]]
            else:
                tiles = [[0, 1, 2, 3], [4, 5, 6, 7]]
            if half == 0:
                load_x(0, tiles[0])
            with contextlib.ExitStack() as sA:
                a_buf = sb(sA, "a_buf", [128, 4, 30 + ntok], BF16)
                xns = [sb(sA, "xn%d" % i, [128, D], BF16) for i in range(2)]
                sig = sb(sA, "sig", [128, 512], F32)
                gl = sb(sA, "gl", [128, 512], F32)
                junk = gl[:].bitcast(BF16)
                gvn = sb(sA, "gvn", [128, 4, 512], BF16)
                cv = sb(sA, "cv", [128, 4, 512], F32)
                cvb = sb(sA, "cvb", [128, 4, 512], BF16)
                sq = sb(sA, "sq", [128, 4, 512], BF16)
                stat = sb(sA, "stat", [128, 3, 512], F32)
                gsq = sb(sA, "gsq", [128, 4, 512], BF16)
                yb = sb(sA, "yb", [128, 8, 512], BF16)
                bst = sb(sA, "bst", [128, 6], F32)
                if half == 0:
                    print("phase A SBUF bytes remaining:", nc.sbuf_bytes_remaining)

                if half == 0:
                    setup_part1(gl, sq, stat)
                    load_x(0, tiles[1])
                if half == 0:
                    T.op("pool", lambda e: e.memset(a_buf[:, :, 0:30], 0.0), writes=["a_pre"])
                else:
                    T.op("pool", lambda e: e.tensor_copy(a_buf[:, :, 0:30], a_carry[:]),
                         reads=["a_carry"], writes=["a_pre"])

                def load_w_out():
                    T.dma("pool", w_out_sb, w_out.rearrange("(kt p) n -> p kt n", p=128), writes=["wout"])

                def fold_w_out():
                    for ct in range(8):
                        T.op("dve", lambda e, ct=ct: e.scalar_tensor_tensor(
                            w_out_sb[:, ct, :], w_out_sb[:, ct, :], cols[:, R_MOG + ct:R_MOG + ct + 1], gt1[:], ALU.mult, ALU.mult),
                            reads=["wout", "gt1", "cols"], writes=["wout"])

                dstate = {"n": 0}
                TL = []
                for ti, tl in enumerate(tiles):
                    TL.append(dict(ti=ti, tl=tl, n=len(tl), NT=len(tl) * 128, toff=tl[0] * 128,
                                   halo=(half == 0 and ti == 0)))

                def st_norm(t):
                    tl, NT = t["tl"], t["NT"]
                    norm_group([x1[:, c, :] for c in tl], [("x1", c) for c in tl], 0, 8,
                               lambda dt: hb[:, dt, 0:NT], "hb", (junk, xns), 0, junk_key="gl")

                def st_z_a(t, cts=(0, 1, 2, 3)):
                    NT, toff, ti = t["NT"], t["toff"], t["ti"]
                    for ct in cts:
                        b1 = bank()
                        T.mm([(banks[b1][:, 0:NT], w_in_sb[:, kt, 512 + ct * 128: 512 + (ct + 1) * 128], hb[:, kt, 0:NT])
                              for kt in range(8)], reads=S0ALL + [("win", 1), "hb"], writes=[("ps", b1)])
                        T.op("act", lambda e, b1=b1: e.activation(out=sig[:, 0:NT], in_=banks[b1][:, 0:NT], func=AF.Sigmoid),
                             reads=[("ps", b1)], writes=["sig"])
                        b2 = bank()
                        T.mm([(banks[b2][:, 0:NT], w_in_sb[:, kt, ct * 128:(ct + 1) * 128], hb[:, kt, 0:NT])
                              for kt in range(8)], reads=S0ALL + [("win", 1), "hb"], writes=[("ps", b2)])
                        adst = a_buf[:, ct, 30 + toff: 30 + toff + NT]
                        if t["halo"]:
                            T.op("dve", lambda e, b2=b2, adst=adst: e.scalar_tensor_tensor(
                                adst, banks[b2][:, 0:NT], msk[:, 0:1], sig[:, 0:NT], ALU.mult, ALU.mult),
                                reads=[("ps", b2), "sig", "msk"], writes=[("a", ti)])
                        else:
                            T.op("dve", lambda e, b2=b2, adst=adst: e.tensor_tensor(
                                adst, banks[b2][:, 0:NT], sig[:, 0:NT], ALU.mult),
                                reads=[("ps", b2), "sig"], writes=[("a", ti)])

                def st_z_gu(t):
                    NT = t["NT"]
                    for ct in range(4):
                        b = bank()
                        T.mm([(banks[b][:, 0:NT], w_in_sb[:, kt, 1024 + ct * 128: 1024 + (ct + 1) * 128], hb[:, kt, 0:NT])
                              for kt in range(8)], reads=S0ALL + [("win", 2), "hb"], writes=[("ps", b)])
                        T.op("act", lambda e, b=b, ct=ct: e.activation(out=gua[:, ct, 0:NT], in_=banks[b][:, 0:NT],
                                                                       func=AF.Gelu_apprx_tanh),
                             reads=[("ps", b)], writes=[("gua", ct)])

                def st_z_gv(t):
                    for ci in range(t["n"]):
                        b = bank()
                        T.mm([(banks[b][:], hb[:, kt, ci * 128:(ci + 1) * 128], w_in_sb[:, kt, 1536:2048])
                              for kt in range(8)], reads=S0ALL + [("win", 0), "hb"], writes=[("ps", b)])
                        T.op("act", lambda e, b=b: e.activation(out=gl[:], in_=banks[b][:], func=AF.Gelu_apprx_tanh),
                             reads=[("ps", b)], writes=["gl"])
                        T.op("dve", lambda e: e.bn_stats(bst[:, 0:6], gl[:]), reads=["gl"], writes=["bst"])
                        T.op("dve", lambda e: e.bn_aggr(small[:, 8:10], bst[:, 0:6]), reads=["bst"], writes=[("small", 8)])
                        T.op("dve", lambda e: e.tensor_scalar(small[:, 11:12], small[:, 9:10], LN_EPS, None, ALU.add),
                             reads=[("small", 8)], writes=[("small", 11)])
                        T.op("pool", lambda e: e.tensor_tensor(small[:, 10:11], small[:, 11:12], mhalf[:, 0:1], ALU.pow),
                             reads=[("small", 11), "mhalf"], writes=[("small", 10)])
                        T.op("dve", lambda e: e.tensor_scalar(gl[:], gl[:], small[:, 8:9], small[:, 10:11],
                                                               ALU.subtract, ALU.mult),
                             reads=["gl", ("small", 8), ("small", 10)], writes=["gl"])
                        T.op("dve", lambda e: e.tensor_tensor(gl[:], gl[:], rows[:, O_GMG:O_GMG + 512], ALU.mult),
                             reads=["gl", "rows"], writes=["gl"])
                        T.op("dve", lambda e, ci=ci: e.tensor_tensor(gvn[:, ci, :], gl[:], rows[:, O_GMB:O_GMB + 512], ALU.add),
                             reads=["gl", "rows"], writes=[("gvn", ci)])

                def st_spatial(t):
                    n, NT = t["n"], t["NT"]
                    for ct in range(4):
                        b = bank()
                        mms = []
                        for ci in range(n):
                            for hh in range(2):
                                hd = 2 * ct + hh
                                mms.append((banks[b][hh * 64:(hh + 1) * 64, ci * 128:(ci + 1) * 128],
                                            gvn[:, ci, hd * 64:(hd + 1) * 64], wsT[:, hd, :]))
                        T.mm(mms, reads=[("gvn", ci) for ci in range(n)] + ["wsT"], writes=[("ps", b)], single=True)
                        T.op("dve", lambda e, b=b, ct=ct: e.tensor_tensor(
                            stat[:, 0, 0:NT].rearrange("p (c t) -> p c t", t=128),
                            banks[b][:, 0:NT].rearrange("p (c t) -> p c t", t=128),
                            rows[:, O_BS + ct * 128: O_BS + (ct + 1) * 128].unsqueeze(1).to_broadcast([128, n, 128]),
                            ALU.add), reads=[("ps", b), "rows"], writes=[("stat", 0)])
                        T.op("dve", lambda e, ct=ct: e.tensor_tensor(yb[:, 4 + ct, 0:NT], gua[:, ct, 0:NT], stat[:, 0, 0:NT], ALU.mult),
                             reads=[("gua", ct), ("stat", 0)], writes=[("yb", 4 + ct)])
                        T.op("act", lambda e, ct=ct: e.activation(out=gsq[:, ct, 0:NT], in_=yb[:, 4 + ct, 0:NT], func=AF.Square),
                             reads=[("yb", 4 + ct)], writes=[("gsq", ct)])

                def rms_cols(t, base, src, skey):
                    n = t["n"]
                    par = t["ti"] % 2
                    b = bank()
                    for ci in range(n):
                        T.mm([(banks[b][:, ci:ci + 1], src[:, ct, ci * 128:(ci + 1) * 128], ones_b[:, 0:1]) for ct in range(4)],
                             reads=[(skey, ct) for ct in range(4)] + ["ones_b"], writes=[("ps", b)])
                    c0 = base + par * 4
                    T.op("dve", lambda e: e.tensor_scalar(small[:, 88:88 + n], banks[b][:, 0:n], 1.0 / 512, RMS_EPS, ALU.mult, ALU.add),
                         reads=[("ps", b)], writes=[("small", 88)])
                    T.op("pool", lambda e: e.tensor_tensor(small[:, c0:c0 + n], small[:, 88:88 + n], mhalf[:, 0:n], ALU.pow),
                         reads=[("small", 88), "mhalf"], writes=[("small", c0)])

                def st_ones_g(t):
                    rms_cols(t, 64, gsq, "gsq")

                def st_yb_g(t):
                    pass

                def st_conv_mm(t, cts=(0, 1, 2, 3)):
                    NT, toff, ti = t["NT"], t["toff"], t["ti"]
                    if "cb" not in t:
                        t["cb"] = []
                    for ct in cts:
                        ds_ = dstate["n"] % 2
                        dstate["n"] += 1
                        T.dma("sp", diag[ds_].rearrange("p k n -> p (k n)"), diag_scr[ct],
                              reads=[("dscr", ct)], writes=[("diag", ds_)])
                        b = bank(hold=True)
                        rd = [("diag", ds_), ("a", ti), "a_pre"] + ([("a", ti - 1)] if ti > 0 else [])
                        T.mm([(banks[b][:, 0:NT], diag[ds_][:, k, :], a_buf[:, ct, toff + k: toff + k + NT]) for k in range(KW)],
                             reads=rd, writes=[("ps", b)])
                        t["cb"].append(b)

                def st_conv_evac(t, cts=(0, 1, 2, 3)):
                    NT = t["NT"]
                    for ct in cts:
                        b = t["cb"][ct]
                        bias = cols[:, R_CB + ct:R_CB + ct + 1]
                        T.op("act", lambda e, b=b, ct=ct, bias=bias: e.activation(
                            out=cvb[:, ct, 0:NT], in_=banks[b][:, 0:NT], func=AF.Identity, bias=bias),
                            reads=[("ps", b), "cols"], writes=[("cvb", ct)])
                        T.op("act", lambda e, b=b, ct=ct, bias=bias: e.activation(
                            out=sq[:, ct, 0:NT], in_=banks[b][:, 0:NT], func=AF.Square, bias=bias),
                            reads=[("ps", b), "cols"], writes=[("sq", ct)])
                        T.op("act", lambda e, b=b, ct=ct, bias=bias: e.activation(
                            out=cv[:, ct, 0:NT], in_=banks[b][:, 0:NT], func=AF.Identity, bias=bias),
                            reads=[("ps", b), "cols"], writes=[("cv", ct // 2), ("cvx", ct)])
                        release(b)

                def st_ones_c(t):
                    NT = t["NT"]
                    bm = bank()
                    T.mm([(banks[bm][:, 0:NT], ones_b[:], cvb[:, ct, 0:NT]) for ct in range(4)],
                         reads=[("cvb", ct) for ct in range(4)] + ["ones_b"], writes=[("ps", bm)])
                    be = bank()
                    T.mm([(banks[be][:, 0:NT], ones_b[:], sq[:, ct, 0:NT]) for ct in range(4)],
                         reads=[("sq", ct) for ct in range(4)] + ["ones_b"], writes=[("ps", be)])
                    T.op("act", lambda e: e.activation(out=stat[:, 1, 0:NT], in_=banks[bm][:, 0:NT], func=AF.Identity, scale=1.0 / 512),
                         reads=[("ps", bm)], writes=[("stat", 1)])
                    T.op("dve", lambda e: e.tensor_tensor(stat[:, 2, 0:NT], stat[:, 1, 0:NT], stat[:, 1, 0:NT], ALU.mult),
                         reads=[("stat", 1)], writes=[("stat", 2)])
                    T.op("dve", lambda e: e.scalar_tensor_tensor(stat[:, 2, 0:NT], banks[be][:, 0:NT], 1.0 / 512, stat[:, 2, 0:NT],
                                                                  ALU.mult, ALU.subtract),
                         reads=[("ps", be), ("stat", 2)], writes=[("stat", 2)])
                    T.op("act", lambda e: e.activation(out=stat[:, 2, 0:NT], in_=stat[:, 2, 0:NT], func=AF.Sqrt,
                                                       bias=epsr[:, 1:2]),
                         reads=[("stat", 2), "epsr"], writes=[("stat", 2)])
                    T.op("dve", lambda e: e.reciprocal(stat[:, 2, 0:NT], stat[:, 2, 0:NT]),
                         reads=[("stat", 2)], writes=[("stat", 2)])
                    for ct in range(4):
                        T.op("dve", lambda e, ct=ct: e.tensor_tensor(cv[:, ct, 0:NT], cv[:, ct, 0:NT], stat[:, 1, 0:NT], ALU.subtract),
                             reads=[("cvx", ct), ("stat", 1)], writes=[("cvx", ct), ("cv", ct // 2)])
                        T.op("dve", lambda e, ct=ct: e.tensor_tensor(cv[:, ct, 0:NT], cv[:, ct, 0:NT], stat[:, 2, 0:NT], ALU.mult),
                             reads=[("cvx", ct), ("stat", 2)], writes=[("cvx", ct), ("cv", ct // 2)])
                    for ct in range(4):
                        T.op("act", lambda e, ct=ct: e.activation(out=yb[:, ct, 0:NT], in_=cv[:, ct, 0:NT], func=AF.Silu,
                                                                  scale=cols[:, R_CLG + ct:R_CLG + ct + 1],
                                                                  bias=cols[:, R_CLB + ct:R_CLB + ct + 1]),
                             reads=[("cvx", ct), "cols"], writes=[("yb", ct)])
                    for ct in range(4):
                        T.op("act", lambda e, ct=ct: e.activation(out=sq[:, ct, 0:NT], in_=yb[:, ct, 0:NT], func=AF.Square),
                             reads=[("yb", ct)], writes=[("sq", ct)])

                def st_ones_a(t):
                    rms_cols(t, 72, sq, "sq")

                def st_out(t, part):
                    par = t["ti"] % 2
                    k0, base = (4, 64) if part == "g" else (0, 72)
                    for ci, c in enumerate(t["tl"]):
                        for hf in range(2):
                            b = bank()
                            T.mm([(banks[b][:], yb[:, kt, ci * 128:(ci + 1) * 128], w_out_sb[:, kt, hf * 512:(hf + 1) * 512])
                                  for kt in range(k0, k0 + 4)], reads=[("yb", i) for i in range(k0, k0 + 4)] + ["wout"],
                                 writes=[("ps", b)])
                            col = base + par * 4 + ci
                            T.op("dve", lambda e, b=b, c=c, hf=hf, col=col: e.scalar_tensor_tensor(
                                x1[:, c, hf * 512:(hf + 1) * 512], banks[b][:], small[:, col:col + 1],
                                x1[:, c, hf * 512:(hf + 1) * 512], ALU.mult, ALU.add),
                                reads=[("ps", b), ("x1", c), ("small", base + par * 4)], writes=[("x1", c)])

                K_ = len(TL)
                gua_b = gua[:].rearrange("p a b -> p (a b)").bitcast(BF16).rearrange("p (k n) -> p k n", k=8)
                GK = [("gua", i) for i in range(4)]

                def norm2_tile(t, which):
                    tl = t["tl"]
                    if which == "halo":
                        norm_group([x1[:, 0, :]], [("x1", 0)], 16, 24, lambda dt: h2h[:, dt, 0:2], "h2h",
                                   (junk, xns), 16, lo=126, junk_key="gl")
                    elif which == 0:
                        norm_group([x1[:, c, :] for c in tl], [("x1", c) for c in tl], 16, 24,
                                   lambda dt: hb[:, dt, :], "hb", (junk, xns), 16, junk_key="gl")
                    else:
                        norm_group([x1[:, c, :] for c in tl], [("x1", c) for c in tl], 16, 24,
                                   lambda dt: gua_b[:, dt, :], GK, (junk, xns), 16, junk_key="gl")

                if half == 1:
                    load_w_out()
                st_norm(TL[0]); st_z_gv(TL[0]); st_z_a(TL[0]); st_z_gu(TL[0])
                st_spatial(TL[0])
                for i in range(K_):
                    t = TL[i]
                    nx = TL[i + 1] if i + 1 < K_ else None
                    if half == 0 and nx is None:
                        late_blocks_start(gvn)
                    st_conv_mm(t, (0, 1))
                    if nx is not None:
                        st_norm(nx)
                    elif half == 1:
                        norm2_tile(TL[K_ - 2], 0)
                    st_conv_mm(t, (2, 3))
                    if half == 1 and i == 0:
                        fold_w_out()
                    st_conv_evac(t)
                    if nx is not None:
                        st_z_a(nx, (0, 1))
                    st_ones_c(t)
                    if half == 0 and i == 0:
                        hook_gt1()
                        fold_w_out()
                    if nx is not None:
                        st_z_a(nx, (2, 3))
                        st_z_gv(nx)
                        st_z_gu(nx)
                        if i + 1 == K_ - 1:
                            load_slice_up(0)
                    if half == 0 and nx is None:
                        def _n2():
                            norm2_tile(TL[0], "halo")
                            norm2_tile(TL[K_ - 2], 0)
                        late_blocks(after_mod=_n2)
                    st_ones_g(t)
                    st_out(t, "g")
                    st_ones_a(t)
                    st_out(t, "a")
                    if nx is not None:
                        st_spatial(nx)
                    else:
                        norm2_tile(t, 1)
                if half == 0:
                    T.op("pool", lambda e: e.tensor_copy(a_carry[:], a_buf[:, :, ntok: ntok + 30]),
                         reads=[("a", len(tiles) - 1)], writes=["a_carry"])
                T.barrier()
            if debug is not None and half == 0:
                dbg("x1", x1[:], [128, 9, D], [("x1", c) for c in range(9)])
                dbg("modc", modc[:], [128, 48], ["modc"])
                dbg("gt1", gt1[:], [128, D], ["gt1"])
                dbg("gt2", gt2[:], [128, D], ["gt2"])

            with contextlib.ExitStack() as sB:
                junk = sb(sB, "junkB", [128, D], BF16)
                xns = [sb(sB, "xnB%d" % i, [128, D], BF16) for i in range(2)]
                Uv = [sb(sB, "Uv%d" % i, [128, 514], F32) for i in range(2)]
                Ug = [sb(sB, "Ug%d" % i, [128, 514], F32) for i in range(2)]
                av = [sb(sB, "av%d" % i, [128, 512], F32) for i in range(2)]
                ag = [sb(sB, "ag%d" % i, [128, 512], F32) for i in range(2)]
                actb = [sb(sB, "actb%d" % i, [128, 6, 512], BF16) for i in range(2)]
                ostg = [sb(sB, "ostg%d" % i, [128, D], F32) for i in range(2)]

                own = list(range(1, 9)) if half == 0 else list(range(0, 8))
                if half == 0:
                    print("phase B SBUF bytes remaining:", nc.sbuf_bytes_remaining)

                def load_slice(s_):
                    load_slice_up(s_)
                    load_slice_down(s_)
                    return slice_views(s_)

                winfo = {0: slice_views(0)}
                gua_b = gua[:].rearrange("p a b -> p (a b)").bitcast(BF16).rearrange("p (k n) -> p k n", k=8)
                h2t = [hb[:], gua_b]
                h2k = [["hb"], [("gua", i) for i in range(4)]]
                load_slice_down(0)
                if debug is not None and half == 0:
                    dbg("hbd", hb[:], [128, 8, 512], ["hb"])
                    dbg("guad", gua_b, [128, 8, 512], [("gua", i) for i in range(4)])

                steps = [(s_, tt_) for s_ in range(4) for tt_ in range(2)]
                items = [(si, fi) for si, (s_, tt_) in enumerate(steps) for fi in range(len(SLICES[s_]))]
                DD = 2

                def head(g, si, fi):
                    s_, tt = steps[si]
                    f = SLICES[s_][fi]
                    wv, wg, wd, wkey = winfo[s_]
                    par = g % 2
                    if half == 0 and tt == 0:
                        for gi, wsrc in enumerate((wv, wg)):
                            b = bank()
                            T.mm([(banks[b][:, 0:2], wsrc[:, kt, fi * 128:(fi + 1) * 128], h2h[:, kt, 0:2]) for kt in range(8)],
                                 reads=[(wkey, "ab"[gi]), "h2h"], writes=[("ps", b)])
                            T.op("act", lambda e, b=b, f=f, gi=gi: e.activation(
                                out=carry[:, 0, f, gi, :], in_=banks[b][:, 0:2], func=AF.Identity, scale=msk[:, 0:1]),
                                reads=[("ps", b), "msk"], writes=[("carry", 0, f, gi)])
                    for gi, (wsrc, Ub, accb, nm) in enumerate(((wv, Uv, av, "v"), (wg, Ug, ag, "g"))):
                        j = f + gi * NFT
                        U = Ub[par]
                        acc = accb[par]
                        ukey = ("U" + nm, par)
                        ackey = ("acc" + nm, par)
                        b = bank()
                        T.mm([(banks[b][:], wsrc[:, kt, fi * 128:(fi + 1) * 128], h2t[tt][:, kt, :])
                              for kt in range(8)], reads=[(wkey, "ab"[gi])] + h2k[tt], writes=[("ps", b)])
                        T.op("act", lambda e, b=b, U=U: e.activation(out=U[:, 2:514], in_=banks[b][:], func=AF.Copy),
                             reads=[("ps", b)], writes=[ukey])
                        T.op("act", lambda e, b=b, f=f, gi=gi, tt=tt: e.activation(
                            out=carry[:, (tt + 1) % 2, f, gi, :], in_=banks[b][:, 510:512], func=AF.Copy),
                            reads=[("ps", b)], writes=[("carry", (tt + 1) % 2, f, gi)])
                        T.op("act", lambda e, U=U, f=f, gi=gi, tt=tt: e.activation(
                            out=U[:, 0:2], in_=carry[:, tt % 2, f, gi, :], func=AF.Copy),
                             reads=[("carry", tt % 2, f, gi)], writes=[ukey])
                        T.op("act", lambda e, b=b, acc=acc, j=j: e.activation(
                            out=acc[:], in_=banks[b][:], func=AF.Identity,
                            scale=cols[:, R_FW + 2 * 44 + j: R_FW + 2 * 44 + j + 1], bias=cols[:, R_FB + j: R_FB + j + 1]),
                            reads=[("ps", b), "cols"], writes=[ackey])
                        T.op("dve", lambda e, U=U, acc=acc, j=j: e.scalar_tensor_tensor(
                            acc[:], U[:, 1:513], cols[:, R_FW + 44 + j: R_FW + 44 + j + 1], acc[:], ALU.mult, ALU.add),
                            reads=[ukey, ackey, "cols"], writes=[ackey])
                        T.op("dve", lambda e, U=U, acc=acc, j=j: e.scalar_tensor_tensor(
                            acc[:], U[:, 0:512], cols[:, R_FW + j: R_FW + j + 1], acc[:], ALU.mult, ALU.add),
                            reads=[ukey, ackey, "cols"], writes=[ackey])

                def tail(g, si, fi):
                    par = g % 2
                    accv, accg = av[par], ag[par]
                    kv, kg = ("accv", par), ("accg", par)
                    ab = actb[si % 2]
                    T.op("act", lambda e: e.activation(out=accg[:], in_=accg[:], func=AF.Silu), reads=[kg], writes=[kg])
                    T.op("dve", lambda e: e.tensor_tensor(ab[:, fi, :], accv[:], accg[:], ALU.mult),
                         reads=[kv, kg], writes=[("actb", si % 2)])

                def down(si):
                    s_, tt = steps[si]
                    nf = len(SLICES[s_])
                    wv, wg, wd, wkey = winfo[s_]
                    ab = actb[si % 2]
                    akey = ("actb", si % 2)
                    for ci in range(4):
                        c = own[tt * 4 + ci]
                        for hf in range(2):
                            b = bank()
                            T.mm([(banks[b][:], ab[:, fi, ci * 128:(ci + 1) * 128], wd[:, fi, hf * 512:(hf + 1) * 512])
                                  for fi in range(nf)], reads=[akey, (wkey, "c")], writes=[("ps", b)])
                            T.op("dve", lambda e, b=b, c=c, hf=hf: e.tensor_tensor(
                                x1[:, c, hf * 512:(hf + 1) * 512], banks[b][:], x1[:, c, hf * 512:(hf + 1) * 512], ALU.add),
                                reads=[("ps", b), ("x1", c)], writes=[("x1", c)])
                        if s_ == 3 and half == 1 and tt == 1:
                            final_norm([c], tt * 4 + ci)
                    if s_ == 3 and not (half == 1 and tt == 1):
                        final_norm([own[tt * 4 + ci] for ci in range(4)], tt * 4)
                        if half == 0:
                            load_x(1, [tt * 4 + i for i in range(4)])

                def final_norm(cs, o0):
                    n = len(cs)
                    for ci, c in enumerate(cs):
                        T.op("act", lambda e, c=c, ci=ci: e.activation(out=junk[:], in_=x1[:, c, :], func=AF.Square,
                                                                       accum_out=small[:, 32 + ci:33 + ci]),
                             reads=[("x1", c)], writes=["junk", ("small", 32)])
                    T.op("dve", lambda e: e.tensor_scalar(small[:, 36:36 + n], small[:, 32:32 + n], 1.0 / D, RMS_EPS, ALU.mult, ALU.add),
                         reads=[("small", 32)], writes=[("small", 36)])
                    T.op("pool", lambda e: e.tensor_tensor(small[:, 40:40 + n], small[:, 36:36 + n], mhalf[:, 0:n], ALU.pow),
                         reads=[("small", 36), "mhalf"], writes=[("small", 40)])
                    for ci, c in enumerate(cs):
                        og = ostg[(o0 + ci) % 2]
                        okey = ("ostg", (o0 + ci) % 2)
                        T.op("dve", lambda e, c=c, og=og, ci=ci: e.scalar_tensor_tensor(
                            og[:], x1[:, c, :], small[:, 40 + ci:41 + ci], rows[:, O_FING:O_FING + D], ALU.mult, ALU.mult),
                            reads=[("x1", c), ("small", 40), "rows"], writes=[okey])
                        r0 = half * 1024 + (o0 + ci) * 128
                        T.dma("sp", out_d[r0:r0 + 128, :], og[:], reads=[okey])

                pending = []
                G = len(items)
                for g in range(G + 1):
                    if g < G:
                        si, fi = items[g]
                        s_, tt = steps[si]
                        if fi == 0 and tt == 1 and s_ + 1 < 4:
                            winfo[s_ + 1] = load_slice(s_ + 1)
                        if half == 0 and fi == 0 and tt == 1 and s_ == 3:
                            load_w_in()
                        head(g, si, fi)
                    if g >= 1:
                        psi, pfi = items[g - 1]
                        tail(g - 1, psi, pfi)
                        if pfi == len(SLICES[steps[psi][0]]) - 1:
                            pending.append([DD, psi])
                    for p in list(pending):
                        if p[0] <= 0 or g >= G:
                            down(p[1])
                            pending.remove(p)
                        else:
                            p[0] -= 1
                T.barrier()
        T.finish()
    return nc, dbg_out


def _pack_inputs(inputs):
    f = np.float32
    x = np.asarray(inputs["x"], f)
    c = np.asarray(inputs["c"], f)
    g = lambda k: np.asarray(inputs[k], f)[0]
    vec = np.zeros((NROWS, 128), f)
    vec[R_BADA:R_BADA + 48] = g("b_ada").reshape(48, 128)
    vec[R_G1:R_G1 + 8] = g("norm1_gain").reshape(8, 128)
    vec[R_CW:R_CW + 124] = g("conv_dw_w").reshape(31 * 4, 128)
    vec[R_CB:R_CB + 4] = g("conv_dw_b").reshape(4, 128)
    vec[R_CLG:R_CLG + 4] = g("conv_ln_g").reshape(4, 128)
    vec[R_CLB:R_CLB + 4] = g("conv_ln_b").reshape(4, 128)
    vec[R_MOG:R_MOG + 8] = g("mix_out_gain").reshape(8, 128)
    vec[R_G2:R_G2 + 8] = g("norm2_gain").reshape(8, 128)
    vec[R_FW:R_FW + 132] = g("ffn_dw_w").reshape(3 * 44, 128)
    vec[R_FB:R_FB + 44] = g("ffn_dw_b").reshape(44, 128)
    rows = np.zeros((128, NRB), f)
    rows[:, O_GMG:O_GMG + 512] = g("gm_ln_g")[None, :]
    rows[:, O_GMB:O_GMB + 512] = g("gm_ln_b")[None, :]
    rows[:, O_FING:O_FING + 1024] = np.asarray(inputs["final_gain"], f)[None, :]
    b_ada = g("b_ada")
    rows[:, O_BGT1:O_BGT1 + 1024] = b_ada[2048:3072][None, :]
    rows[:, O_BGT2:O_BGT2 + 1024] = b_ada[5120:6144][None, :]
    bs = g("gm_bs")
    rows[:, O_BS:O_BS + 512] = np.repeat(bs, 64, axis=0).reshape(4, 128, 128).transpose(1, 0, 2).reshape(128, 512)
    wsT = np.ascontiguousarray(g("gm_ws").transpose(2, 0, 1))
    shared = {
        "vecs": vec, "rows": rows, "wsT": wsT,
        "w_ada": g("w_ada"), "w_in": g("w_in"), "w_out": g("w_out"),
        "w_up": g("w_up"), "w_down": g("w_down"),
    }
    in_maps = []
    for i in range(NCORES):
        b, q = divmod(i, 4)
        t0 = q * TOK_PER_CORE
        xe = np.zeros((17 * 128, D), f)
        if q > 0:
            xe[0:128] = x[b, t0 - 128:t0]
        xe[128:] = x[b, t0:t0 + TOK_PER_CORE]
        m = dict(shared)
        m["x_ext"] = xe
        m["msk"] = np.full((128, 1), 1.0 if q > 0 else 0.0, f)
        m["c_col"] = np.ascontiguousarray(c[b].reshape(8, 128).T)
        in_maps.append(m)
    return in_maps


_NC_CACHE = {}


def kernel(**inputs):
    if "nc" not in _NC_CACHE:
        _NC_CACHE["nc"] = build_program()[0]
    nc = _NC_CACHE["nc"]
    in_maps = _pack_inputs(inputs)
    res = run_bass_kernel_spmd(nc, in_maps, core_ids=list(range(NCORES)))
    out = np.zeros((2, SEQ, D), np.float32)
    for i in range(NCORES):
        b, q = divmod(i, 4)
        out[b, q * TOK_PER_CORE:(q + 1) * TOK_PER_CORE] = res.results[i]["out"]
    return out
```

```python
import contextlib
import numpy as np
import concourse.bass as bass
import concourse.mybir as mybir
from concourse.bass_utils import run_bass_kernel_spmd

F32 = mybir.dt.float32
BF16 = mybir.dt.bfloat16
AF = mybir.ActivationFunctionType
ALU = mybir.AluOpType

D = 1024
DFF = 2816
NFT = 22
SEQ = 8192
NCORES = 8
TOK_PER_CORE = 2048
NCH = 16
RMS_EPS = 1e-6
LN_EPS = 1e-5
KW = 31

R_BADA = 0
R_G1 = 48
R_CW = 56
R_CB = 180
R_CLG = 184
R_CLB = 188
R_MOG = 192
R_G2 = 200
R_FW = 208
R_FB = 340
NROWS = 384
O_GMG = 0
O_GMB = 512
O_FING = 1024
O_BS = 2048
NRP = 2560
O_BGT1 = 2560
O_BGT2 = 3584
NRB = 4608

SLICES = [list(range(0, 6)), list(range(6, 12)), list(range(12, 17)), list(range(17, 22))]
DIAG_ENG = "dve"
SLOT_ELEMS = 18432


class Tracker:
    def __init__(self, nc, es):
        self.nc = nc
        self.eng = {"pe": nc.tensor, "act": nc.scalar, "dve": nc.vector,
                    "pool": nc.gpsimd, "sp": nc.sync}
        self.sem = {}
        for e in ("pe", "act", "dve", "pool"):
            self.sem[e] = es.enter_context(nc.semaphore("c_" + e))
        self.cnt = {e: 0 for e in self.sem}
        self.waited = {e: {} for e in self.eng}
        self.state = {}
        self.ndma = 72
        self.dsem = [es.enter_context(nc.semaphore("d%d" % i)) for i in range(self.ndma)]
        self.dval = [0] * self.ndma
        self.drr = 0
        self.semname = {}
        for e, s in self.sem.items():
            self.semname[id(s)] = e
        self.n_wait = 0
        self.n_inst = 0

    def _deps(self, eng, reads, writes):
        deps = []
        for k in reads:
            st = self.state.get(k)
            if st is not None and st[0] is not None:
                deps.append(st[0] + ("raw",))
        for k in writes:
            st = self.state.get(k)
            if st is not None:
                if st[0] is not None:
                    deps.append(st[0] + ("waw",))
                for d in st[1].values():
                    deps.append(d + ("war",))
        return deps

    def _wait(self, eng, deps):
        best = {}
        for (sem, val, prod, kind) in deps:
            if prod == eng:
                if eng == "pe" or kind != "raw":
                    continue
            key = id(sem)
            if key not in best or best[key][1] < val:
                best[key] = (sem, val)
        for key, (sem, val) in best.items():
            if self.waited[eng].get(key, 0) >= val:
                continue
            self.eng[eng].wait_ge(sem, val)
            self.waited[eng][key] = val
            self.n_wait += 1

    def _update(self, eng, dep, reads, writes):
        for k in writes:
            self.state[k] = [dep, {}]
        for k in reads:
            st = self.state.get(k)
            if st is None:
                st = [None, {}]
                self.state[k] = st
            st[1][dep[2] if dep[2] != "dma" else ("dma", id(dep[0]))] = dep

    def op(self, eng, fn, reads=(), writes=()):
        self._wait(eng, self._deps(eng, reads, writes))
        ins = fn(self.eng[eng])
        self.cnt[eng] += 1
        ins.then_inc(self.sem[eng], 1)
        self.n_inst += 1
        dep = (self.sem[eng], self.cnt[eng], eng)
        self._update(eng, dep, reads, writes)
        return dep

    def mm(self, mms, reads=(), writes=(), single=False):
        self._wait("pe", self._deps("pe", reads, writes))
        n = len(mms)
        ins = None
        for i, (out, lhsT, rhs) in enumerate(mms):
            if single:
                ins = self.nc.tensor.matmul(out, lhsT, rhs, start=True, stop=True)
            else:
                ins = self.nc.tensor.matmul(out, lhsT, rhs, start=(i == 0), stop=(i == n - 1))
        self.cnt["pe"] += 1
        ins.then_inc(self.sem["pe"], 1)
        self.n_inst += n
        dep = (self.sem["pe"], self.cnt["pe"], "pe")
        self._update("pe", dep, reads, writes)
        return dep

    def transposes(self, tps, ident, reads=(), writes=()):
        self._wait("pe", self._deps("pe", reads, writes))
        ins = None
        for (out, in_) in tps:
            ins = self.nc.tensor.transpose(out, in_, ident)
        self.cnt["pe"] += 1
        ins.then_inc(self.sem["pe"], 1)
        self.n_inst += len(tps)
        dep = (self.sem["pe"], self.cnt["pe"], "pe")
        self._update("pe", dep, reads, writes)
        return dep

    def dma(self, q, out, in_, reads=(), writes=()):
        self._wait(q, self._deps(q, reads, writes))
        i = self.drr
        self.drr = (self.drr + 1) % self.ndma
        sem = self.dsem[i]
        if self.dval[i] > 0 and self.waited[q].get(id(sem), 0) < self.dval[i]:
            self.eng[q].wait_ge(sem, self.dval[i])
            self.waited[q][id(sem)] = self.dval[i]
        self.eng[q].dma_start(out=out, in_=in_).then_inc(sem, 16)
        self.dval[i] += 16
        self.n_inst += 1
        dep = (sem, self.dval[i], "dma")
        self._update("dma", dep, reads, writes)
        return dep

    def barrier(self):
        for e in self.eng:
            for p in self.sem:
                if p == e or self.cnt[p] == 0:
                    continue
                if self.waited[e].get(id(self.sem[p]), 0) < self.cnt[p]:
                    self.eng[e].wait_ge(self.sem[p], self.cnt[p])
                    self.waited[e][id(self.sem[p])] = self.cnt[p]
            for i in range(self.ndma):
                if self.dval[i] > 0 and self.waited[e].get(id(self.dsem[i]), 0) < self.dval[i]:
                    self.eng[e].wait_ge(self.dsem[i], self.dval[i])
                    self.waited[e][id(self.dsem[i])] = self.dval[i]

    def finish(self):
        e = "sp"
        for i in range(self.ndma):
            if self.dval[i] > 0 and self.waited[e].get(id(self.dsem[i]), 0) < self.dval[i]:
                self.eng[e].wait_ge(self.dsem[i], self.dval[i])
                self.waited[e][id(self.dsem[i])] = self.dval[i]
        for p in self.sem:
            if self.cnt[p] and self.waited[e].get(id(self.sem[p]), 0) < self.cnt[p]:
                self.eng[e].wait_ge(self.sem[p], self.cnt[p])


def build_program(debug=None):
    nc = bass.Bass("TRN2", target_bir_lowering=False)
    dbg_out = {}

    def din(name, shape):
        return nc.dram_tensor(name, list(shape), F32, kind="ExternalInput").ap()

    x_ext = din("x_ext", [17 * 128, D])
    msk_d = din("msk", [128, 1])
    ccol_d = din("c_col", [128, 8])
    vecs_d = din("vecs", [NROWS, 128])
    rows_d = din("rows", [128, NRB])
    wsT_d = din("wsT", [128, 8, 128])
    w_ada = din("w_ada", [D, 6 * D])
    w_in = din("w_in", [D, 2 * D])
    w_out = din("w_out", [D, D])
    w_up = din("w_up", [D, 2 * DFF])
    w_down = din("w_down", [DFF, D])
    out_d = nc.dram_tensor("out", [TOK_PER_CORE, D], F32, kind="ExternalOutput").ap()
    diag_scr = nc.dram_tensor("diag_scr", [4, 128, KW * 128], BF16, kind="Internal").ap()

    uid = [0]

    with contextlib.ExitStack() as es:
        T = Tracker(nc, es)

        def sb(st, name, shape, dt):
            uid[0] += 1
            return st.enter_context(nc.sbuf_tensor("%s_%d" % (name, uid[0]), list(shape), dt))

        banks = [es.enter_context(nc.psum_tensor("bank%d" % i, [128, 512], F32)) for i in range(8)]
        brr = [0]

        held = set()

        def bank(hold=False):
            for _ in range(8):
                i = brr[0]
                brr[0] = (i + 1) % 8
                if i not in held:
                    if hold:
                        held.add(i)
                    return i
            raise RuntimeError("all PSUM banks held")

        def release(i):
            held.discard(i)

        cols = sb(es, "cols", [128, NROWS], F32)
        modc = sb(es, "modc", [128, 48], F32)
        ident_f = sb(es, "ident_f", [128, 128], F32)
        ident_b = sb(es, "ident_b", [128, 128], BF16)
        ones_b = sb(es, "ones_b", [128, 128], BF16)
        cmask = sb(es, "cmask", [128, 128], F32)
        iot = sb(es, "iot", [128, 128], F32)
        rows = sb(es, "rows", [128, NRP], F32)
        gt1 = sb(es, "gt1", [128, D], F32)
        gt2 = sb(es, "gt2", [128, D], F32)
        wsT = sb(es, "wsT", [128, 8, 128], BF16)
        msk = sb(es, "msk", [128, 1], F32)
        ccol = sb(es, "ccol", [128, 8], F32)
        cact = sb(es, "cact", [128, 8], BF16)
        cact_rep = sb(es, "cact_rep", [128, 8, 128], BF16)
        carry = sb(es, "carry", [128, 2, NFT, 2, 2], F32)
        cwb = sb(es, "cwb", [128, 4, KW], BF16)
        a_carry = sb(es, "a_carry", [128, 4, 30], BF16)
        small = sb(es, "small", [128, 96], F32)
        mraw = sb(es, "mraw", [128, 32], F32)
        mhalf = sb(es, "mhalf", [128, 8], F32)
        epsr = sb(es, "epsr", [128, 2], F32)
        hb = sb(es, "hb", [128, 8, 512], BF16)
        gua = sb(es, "gua", [128, 4, 512], F32)
        h2h = sb(es, "h2h", [128, 8, 2], BF16)
        x1 = sb(es, "x1", [128, 9, D], F32)
        slot = [sb(es, "slot0", [128, SLOT_ELEMS], BF16), sb(es, "slot1", [128, SLOT_ELEMS], BF16)]

        def dbg(name, ap, shape, reads):
            if debug is None or name not in debug:
                return
            t = nc.dram_tensor("dbg_" + name, list(shape), ap.dtype, kind="ExternalOutput").ap()
            dbg_out[name] = t
            T.dma("sp", t, ap, reads=reads)

        for m_, off_, keys_ in ((0, 0, ["wout"]), (1, 8192, [("diag", 0), ("diag", 1)])):
            T.dma("pool", slot[1][:, off_:off_ + 8192].rearrange("p (k n) -> p k n", k=8),
                  w_ada[:, m_ * D:(m_ + 1) * D].rearrange("(kt p) n -> p kt n", p=128), writes=keys_)
        T.dma("sp", msk[:], msk_d, writes=["msk"])
        T.dma("sp", ccol[:], ccol_d, writes=["ccol"])
        T.dma("sp", rows[:], rows_d[:, 0:NRP], writes=["rows"])
        T.op("pool", lambda e: e.iota(iot[:], [[1, 128]], base=0, channel_multiplier=-1,
                                      allow_small_or_imprecise_dtypes=True), writes=["iot"])
        T.op("pool", lambda e: e.tensor_single_scalar(ident_f[:], iot[:], 0.0, ALU.is_equal),
             reads=["iot"], writes=["ident_f"])
        T.op("pool", lambda e: e.tensor_single_scalar(cmask[:], iot[:], 0.0, ALU.is_ge),
             reads=["iot"], writes=["cmask"])
        T.op("pool", lambda e: e.tensor_copy(ident_b[:], ident_f[:]), reads=["ident_f"], writes=["ident_b"])
        T.op("pool", lambda e: e.memset(ones_b[:], 1.0), writes=["ones_b"])
        T.op("pool", lambda e: e.memset(small[:], 0.0), writes=["small"])
        T.op("pool", lambda e: e.memset(mhalf[:], -0.5), writes=["mhalf"])
        T.op("pool", lambda e: e.memset(epsr[:, 0:1], RMS_EPS), writes=["epsr"])
        T.op("pool", lambda e: e.memset(epsr[:, 1:2], LN_EPS), writes=["epsr"])

        wadaA = slot[1][:, 0:8192].rearrange("p (k n) -> p k n", k=8)
        wadaB = slot[1][:, 8192:16384].rearrange("p (k n) -> p k n", k=8)
        wadaC = x1[:, 5:9, :].rearrange("p c d -> p (c d)").bitcast(BF16).rearrange("p (k n) -> p k n", k=8)
        KA = ["wout"]
        KB = [("diag", 0), ("diag", 1)]
        KC = [("x1", c) for c in range(5, 9)]

        def ada_cols(wb, keys, bcol, cm):
            for j in range(8):
                T.mm([(banks[bcol][:, cm * 8 + j: cm * 8 + j + 1], wb[:, kt, j * 128:(j + 1) * 128],
                       cact[:, kt:kt + 1]) for kt in range(8)], reads=keys + ["cact"], writes=[("ps", bcol)])

        def ada_rows(wb, keys, dst, dkey, brow, bkeys):
            for hf in range(2):
                b = bank()
                T.mm([(banks[b][:], cact_rep[:, kt, :], wb[:, kt, hf * 512:(hf + 1) * 512])
                      for kt in range(8)], reads=keys + ["cact_rep"], writes=[("ps", b)])
                T.op("dve", lambda e, b=b, hf=hf: e.tensor_tensor(
                    dst[:, hf * 512:(hf + 1) * 512], banks[b][:], brow[:, hf * 512:(hf + 1) * 512], ALU.add),
                    reads=[("ps", b)] + bkeys, writes=[dkey])

        def ada_dma(m, wb, keys):
            T.dma("pool", wb, w_ada[:, m * D:(m + 1) * D].rearrange("(kt p) n -> p kt n", p=128), writes=keys)

        def mod_raw(bcol, cm, mod):
            T.op("dve", lambda e: e.tensor_tensor(
                mraw[:, cm * 8:(cm + 1) * 8], banks[bcol][:, cm * 8:(cm + 1) * 8],
                cols[:, R_BADA + mod * 8: R_BADA + mod * 8 + 8], ALU.add),
                reads=[("ps", bcol), "cols"], writes=["mraw"])

        def mod_finish(sh_c, sc_c, so, g):
            T.op("dve", lambda e: e.scalar_tensor_tensor(
                modc[:, so:so + 8], mraw[:, sc_c * 8:sc_c * 8 + 8], 1.0, cols[:, g:g + 8], ALU.add, ALU.mult),
                reads=["mraw", "cols"], writes=["modc"])
            T.op("dve", lambda e: e.tensor_copy(modc[:, so + 8:so + 16], mraw[:, sh_c * 8:sh_c * 8 + 8]),
                 reads=["mraw"], writes=["modc"])

        bgt = slot[0][:, 16384:18432].bitcast(F32)
        kst = [(0, "c")]

        def setup_part1(gl, sq, stat):
            vecs = gl[:, 0:384].rearrange("p (j f) -> p j f", j=3)
            wsf = sq[:].rearrange("p a b -> p (a b)").bitcast(F32).rearrange("p (h t) -> p h t", h=8)
            ksq = [("sq", i) for i in range(4)]
            T.dma("sp", vecs, vecs_d.rearrange("(j p) f -> p j f", p=128), writes=["gl"])
            T.dma("sp", wsf, wsT_d, writes=ksq)
            load_w_in(after=KB)
            ada_dma(2, wadaC, KC)
            T.dma("sp", bgt, rows_d[:, O_BGT1:O_BGT1 + D], writes=kst)
            for j in range(3):
                b = bank()
                T.transposes([(banks[b][:, 0:128], vecs[:, j, :])], ident_f[:],
                             reads=["gl", "ident_f"], writes=[("ps", b)])
                T.op("dve", lambda e, b=b, j=j: e.tensor_copy(cols[:, j * 128:(j + 1) * 128], banks[b][:, 0:128]),
                     reads=[("ps", b)], writes=["cols"])
            T.op("dve", lambda e: e.tensor_tensor(
                wsT[:], wsf, cmask[:].unsqueeze(1).to_broadcast([128, 8, 128]), ALU.mult),
                reads=ksq + ["cmask"], writes=["wsT"])
            T.op("act", lambda e: e.activation(out=cact[:], in_=ccol[:], func=AF.Silu),
                 reads=["ccol"], writes=["cact"])
            for kt in range(8):
                T.op("dve", lambda e, kt=kt: e.tensor_copy(
                    cact_rep[:, kt, :], cact[:, kt:kt + 1].to_broadcast([128, 128])),
                    reads=["cact"], writes=["cact_rep"])
            T.op("dve", lambda e: e.tensor_copy(
                cwb[:], cols[:, R_CW:R_CW + 4 * KW].rearrange("p (k c) -> p c k", c=4)),
                reads=["cols"], writes=["cwb"])
            regs = [
                (hb[:].rearrange("p a b -> p (a b)")[:, 0:KW * 128], ["hb"]),
                (gua[:].rearrange("p a b -> p (a b)").bitcast(BF16)[:, 0:KW * 128], [("gua", i) for i in range(4)]),
                (x1[:, 1:3, :].rearrange("p a b -> p (a b)").bitcast(BF16)[:, 0:KW * 128], [("x1", 1), ("x1", 2)]),
                (x1[:, 3:5, :].rearrange("p a b -> p (a b)").bitcast(BF16)[:, 0:KW * 128], [("x1", 3), ("x1", 4)]),
            ]
            for ct in range(4):
                rv, rk = regs[ct]
                T.op("dve", lambda e, ct=ct, rv=rv: e.tensor_tensor(
                    rv.rearrange("p (k n) -> p k n", k=KW), ident_b[:].unsqueeze(1).to_broadcast([128, KW, 128]),
                    cwb[:, ct, :].unsqueeze(2).to_broadcast([128, KW, 128]), ALU.mult),
                    reads=["ident_b", "cwb"], writes=rk)
                T.dma("sp", diag_scr[ct], rv, reads=rk, writes=[("dscr", ct)])
            bcol = bank()
            ada_cols(wadaA, KA, bcol, 0)
            mod_raw(bcol, 0, 0)
            T.dma("pool", w_out_sb, w_out.rearrange("(kt p) n -> p kt n", p=128), writes=["wout"])
            bcol = bank()
            ada_cols(wadaB, KB, bcol, 1)
            mod_raw(bcol, 1, 1)
            mod_finish(0, 1, 0, R_G1)

        def hook_gt1():
            ada_rows(wadaC, KC, gt1, "gt1", bgt, kst)
            T.dma("sp", bgt, rows_d[:, O_BGT2:O_BGT2 + D], writes=kst)
            load_x(0, [5, 6, 7, 8])

        late = {}

        def late_regions(gvn):
            regs = []
            for r in range(2):
                v = gua[:, 2 * r:2 * r + 2, :].rearrange("p a b -> p (a b)").bitcast(BF16).rearrange("p (k n) -> p k n", k=8)
                regs.append((v, [("gua", 2 * r), ("gua", 2 * r + 1)]))
            for r in range(2):
                v = hb[:, 4 * r:4 * r + 4, :].rearrange("p a b -> p (a b)").rearrange("p (k n) -> p k n", k=8)
                regs.append((v, ["hb"]))
            regs.append((gvn[:].rearrange("p a b -> p (a b)").rearrange("p (k n) -> p k n", k=8), [("gvn", i) for i in range(4)]))
            return regs

        RMAP = [0, 1, 4, 2, 3, 0, 1, 4, 0, 1, 4, 0]

        def late_issue(i):
            m, q = late["plan"][i]
            wb, keys = late["regs"][RMAP[i]]
            T.dma("pool", wb, w_ada[:, m * D + q * 256: m * D + (q + 1) * 256].rearrange("(kt p) n -> p kt n", p=128),
                  writes=keys)
            late["issued"][i] = (wb, keys)

        def late_blocks_start(gvn):
            late["regs"] = late_regions(gvn)
            late["plan"] = [(m, q) for m in (3, 4, 5) for q in range(4)]
            late["issued"] = {}
            seen = set()
            for i, r in enumerate(RMAP):
                if r not in seen:
                    seen.add(r)
                    late_issue(i)

        def late_blocks(after_mod=None):
            plan = late["plan"]
            for i, (m, q) in enumerate(plan):
                wb, keys = late["issued"][i]
                if m in (3, 4):
                    cm = 2 if m == 3 else 3
                    bcol = bank()
                    for j in range(2):
                        T.mm([(banks[bcol][:, j:j + 1], wb[:, kt, j * 128:(j + 1) * 128], cact[:, kt:kt + 1]) for kt in range(8)],
                             reads=keys + ["cact"], writes=[("ps", bcol)])
                    c0 = cm * 8 + q * 2
                    T.op("dve", lambda e, bcol=bcol, c0=c0, m=m, q=q: e.tensor_tensor(
                        mraw[:, c0:c0 + 2], banks[bcol][:, 0:2],
                        cols[:, R_BADA + m * 8 + q * 2: R_BADA + m * 8 + q * 2 + 2], ALU.add),
                        reads=[("ps", bcol), "cols"], writes=["mraw"])
                else:
                    b = bank()
                    T.mm([(banks[b][:, 0:256], cact_rep[:, kt, :], wb[:, kt, :]) for kt in range(8)],
                         reads=keys + ["cact_rep"], writes=[("ps", b)])
                    T.op("dve", lambda e, b=b, q=q: e.tensor_tensor(
                        gt2[:, q * 256:(q + 1) * 256], banks[b][:, 0:256], bgt[:, q * 256:(q + 1) * 256], ALU.add),
                        reads=[("ps", b)] + kst, writes=["gt2"])
                for j in range(i + 1, len(plan)):
                    if RMAP[j] == RMAP[i]:
                        if j not in late["issued"]:
                            late_issue(j)
                        break
                if (m, q) == (4, 3):
                    mod_finish(2, 3, 16, R_G2)
                    if after_mod is not None:
                        after_mod()

        def load_x(half_, tl):
            xrow0_ = 0 if half_ == 0 else 9 * 128
            c0, n = tl[0], len(tl)
            T.dma("sp", x1[:, c0:c0 + n, :],
                  x_ext[xrow0_ + c0 * 128: xrow0_ + (c0 + n) * 128, :].rearrange("(c p) d -> p c d", p=128),
                  writes=[("x1", c) for c in tl])

        S0ALL = [(0, "a"), (0, "b"), (0, "c")]

        def slice_views(s_):
            nf = len(SLICES[s_])
            buf = slot[s_ % 2]
            wv = buf[:, 0:8 * nf * 128].rearrange("p (k n) -> p k n", k=8)
            wg = buf[:, 6144:6144 + 8 * nf * 128].rearrange("p (k n) -> p k n", k=8)
            wd = buf[:, 12288:12288 + nf * 1024].rearrange("p (f n) -> p f n", f=nf)
            return wv, wg, wd, s_ % 2

        def load_w_in(after=()):
            wv_ = w_in.rearrange("(kt p) n -> p kt n", p=128)
            for pi, (c0, c1) in enumerate(((1536, 2048), (0, 1024), (1024, 1536))):
                T.dma("pool", w_in_sb[:, :, c0:c1], wv_[:, :, c0:c1], reads=list(after) if pi == 1 else [],
                      writes=(S0ALL if pi == 0 else []) + [("win", pi)])

        def load_slice_down(s_):
            wv, wg, wd, key = slice_views(s_)
            f0 = SLICES[s_][0]
            nf = len(SLICES[s_])
            T.dma("pool", wd, w_down[f0 * 128:(f0 + nf) * 128, :].rearrange("(f p) n -> p f n", p=128), writes=[(key, "c")])
            for fi in range(nf):
                T.op("pool", lambda e, fi=fi, wd=wd: e.tensor_tensor(wd[:, fi, :], wd[:, fi, :], gt2[:], ALU.mult),
                     reads=[(key, "c"), "gt2"], writes=[(key, "c")])

        def load_slice_up(s_):
            nf = len(SLICES[s_])
            f0 = SLICES[s_][0] * 128
            wv, wg, wd, key = slice_views(s_)
            T.dma("pool", wv, w_up[:, f0:f0 + nf * 128].rearrange("(kt p) n -> p kt n", p=128), writes=[(key, "a")])
            T.dma("pool", wg, w_up[:, DFF + f0:DFF + f0 + nf * 128].rearrange("(kt p) n -> p kt n", p=128), writes=[(key, "b")])


        w_in_sb = slot[0][:, 0:16384].rearrange("p (k n) -> p k n", k=8)
        w_out_sb = slot[1][:, 0:8192].rearrange("p (k n) -> p k n", k=8)
        diag = [slot[1][:, 8192 + i * 3968: 8192 + (i + 1) * 3968].rearrange("p (k n) -> p k n", k=KW)
                for i in range(2)]

        def norm_group(x_aps, xkeys, scol, shcol, dst_fn, dst_key, tmp, ssl, lo=0, junk_key="junk"):
            junk, xns = tmp
            n = len(x_aps)
            dkl = dst_key if isinstance(dst_key, list) else [dst_key]
            for ci in range(n):
                T.op("act", lambda e, ci=ci: e.activation(out=junk, in_=x_aps[ci], func=AF.Square,
                                                          accum_out=small[:, ssl + ci:ssl + ci + 1]),
                     reads=[xkeys[ci]], writes=[junk_key, ("small", ssl)])
            T.op("dve", lambda e: e.tensor_scalar(small[:, ssl + 4:ssl + 4 + n], small[:, ssl:ssl + n], 1.0 / D, RMS_EPS,
                                                   ALU.mult, ALU.add),
                 reads=[("small", ssl)], writes=[("small", ssl + 4)])
            T.op("pool", lambda e: e.tensor_tensor(small[:, ssl + 8:ssl + 8 + n], small[:, ssl + 4:ssl + 4 + n],
                                                    mhalf[:, 0:n], ALU.pow),
                 reads=[("small", ssl + 4), "mhalf"], writes=[("small", ssl + 8)])
            bks = [bank() for _ in range(4)]
            pbs = [banks[b][:].bitcast(BF16) for b in bks]
            for ci in range(n):
                xn = xns[ci % 2]
                xk = ("xn", ci % 2)
                T.op("dve", lambda e, ci=ci, xn=xn: e.tensor_scalar(xn[:], x_aps[ci], small[:, ssl + 8 + ci:ssl + 9 + ci], None, ALU.mult),
                     reads=[xkeys[ci], ("small", ssl + 8)], writes=[xk])
                T.transposes([(pbs[dt // 2][:, (dt % 2) * 512 + ci * 128:(dt % 2) * 512 + (ci + 1) * 128],
                               xn[:, dt * 128:(dt + 1) * 128]) for dt in range(8)], ident_b[:],
                             reads=[xk, "ident_b"], writes=[("ps", b) for b in bks])
            for dt in range(8):
                pb = pbs[dt // 2]
                q = dt % 2
                if dt % 2 == 0:
                    T.op("act", lambda e, dt=dt, q=q, pb=pb: e.activation(
                        out=dst_fn(dt), in_=pb[:, q * 512 + lo:q * 512 + n * 128], func=AF.Identity,
                        scale=modc[:, scol + dt:scol + dt + 1], bias=modc[:, shcol + dt:shcol + dt + 1]),
                        reads=[("ps", bks[dt // 2]), "modc"], writes=dkl)
                else:
                    T.op("dve", lambda e, dt=dt, q=q, pb=pb: e.tensor_scalar(
                        dst_fn(dt), pb[:, q * 512 + lo:q * 512 + n * 128],
                        modc[:, scol + dt:scol + dt + 1], modc[:, shcol + dt:shcol + dt + 1], ALU.mult, ALU.add),
                        reads=[("ps", bks[dt // 2]), "modc"], writes=dkl)

        for half in range(2):
            nchk = 9 if half == 0 else 8
            ntok = nchk * 128
            if half == 0:
                tiles = [[0], [1, 2, 3, 4], [5, 6, 7, 8]]
            else:
                tiles = [[0, 1, 2, 3], [4, 5, 6, 7]]
            if half == 0:
                load_x(0, tiles[0])
            with contextlib.ExitStack() as sA:
                a_buf = sb(sA, "a_buf", [128, 4, 30 + ntok], BF16)
                xns = [sb(sA, "xn%d" % i, [128, D], BF16) for i in range(2)]
                sig = sb(sA, "sig", [128, 512], F32)
                gl = sb(sA, "gl", [128, 512], F32)
                junk = gl[:].bitcast(BF16)
                gvn = sb(sA, "gvn", [128, 4, 512], BF16)
                cv = sb(sA, "cv", [128, 4, 512], F32)
                cvb = sb(sA, "cvb", [128, 4, 512], BF16)
                sq = sb(sA, "sq", [128, 4, 512], BF16)
                stat = sb(sA, "stat", [128, 3, 512], F32)
                gsq = sb(sA, "gsq", [128, 4, 512], BF16)
                yb = sb(sA, "yb", [128, 8, 512], BF16)
                bst = sb(sA, "bst", [128, 6], F32)
                if half == 0:
                    print("phase A SBUF bytes remaining:", nc.sbuf_bytes_remaining)

                if half == 0:
                    setup_part1(gl, sq, stat)
                    load_x(0, tiles[1])
                if half == 0:
                    T.op("pool", lambda e: e.memset(a_buf[:, :, 0:30], 0.0), writes=["a_pre"])
                else:
                    T.op("pool", lambda e: e.tensor_copy(a_buf[:, :, 0:30], a_carry[:]),
                         reads=["a_carry"], writes=["a_pre"])

                def load_w_out():
                    T.dma("pool", w_out_sb, w_out.rearrange("(kt p) n -> p kt n", p=128), writes=["wout"])

                def fold_w_out():
                    for ct in range(8):
                        T.op("dve", lambda e, ct=ct: e.scalar_tensor_tensor(
                            w_out_sb[:, ct, :], w_out_sb[:, ct, :], cols[:, R_MOG + ct:R_MOG + ct + 1], gt1[:], ALU.mult, ALU.mult),
                            reads=["wout", "gt1", "cols"], writes=["wout"])

                dstate = {"n": 0}
                TL = []
                for ti, tl in enumerate(tiles):
                    TL.append(dict(ti=ti, tl=tl, n=len(tl), NT=len(tl) * 128, toff=tl[0] * 128,
                                   halo=(half == 0 and ti == 0)))

                def st_norm(t):
                    tl, NT = t["tl"], t["NT"]
                    norm_group([x1[:, c, :] for c in tl], [("x1", c) for c in tl], 0, 8,
                               lambda dt: hb[:, dt, 0:NT], "hb", (junk, xns), 0, junk_key="gl")

                def st_z_a(t, cts=(0, 1, 2, 3)):
                    NT, toff, ti = t["NT"], t["toff"], t["ti"]
                    for ct in cts:
                        b1 = bank()
                        T.mm([(banks[b1][:, 0:NT], w_in_sb[:, kt, 512 + ct * 128: 512 + (ct + 1) * 128], hb[:, kt, 0:NT])
                              for kt in range(8)], reads=S0ALL + [("win", 1), "hb"], writes=[("ps", b1)])
                        T.op("act", lambda e, b1=b1: e.activation(out=sig[:, 0:NT], in_=banks[b1][:, 0:NT], func=AF.Sigmoid),
                             reads=[("ps", b1)], writes=["sig"])
                        b2 = bank()
                        T.mm([(banks[b2][:, 0:NT], w_in_sb[:, kt, ct * 128:(ct + 1) * 128], hb[:, kt, 0:NT])
                              for kt in range(8)], reads=S0ALL + [("win", 1), "hb"], writes=[("ps", b2)])
                        adst = a_buf[:, ct, 30 + toff: 30 + toff + NT]
                        if t["halo"]:
                            T.op("dve", lambda e, b2=b2, adst=adst: e.scalar_tensor_tensor(
                                adst, banks[b2][:, 0:NT], msk[:, 0:1], sig[:, 0:NT], ALU.mult, ALU.mult),
                                reads=[("ps", b2), "sig", "msk"], writes=[("a", ti)])
                        else:
                            T.op("dve", lambda e, b2=b2, adst=adst: e.tensor_tensor(
                                adst, banks[b2][:, 0:NT], sig[:, 0:NT], ALU.mult),
                                reads=[("ps", b2), "sig"], writes=[("a", ti)])

                def st_z_gu(t):
                    NT = t["NT"]
                    for ct in range(4):
                        b = bank()
                        T.mm([(banks[b][:, 0:NT], w_in_sb[:, kt, 1024 + ct * 128: 1024 + (ct + 1) * 128], hb[:, kt, 0:NT])
                              for kt in range(8)], reads=S0ALL + [("win", 2), "hb"], writes=[("ps", b)])
                        T.op("act", lambda e, b=b, ct=ct: e.activation(out=gua[:, ct, 0:NT], in_=banks[b][:, 0:NT],
                                                                       func=AF.Gelu_apprx_tanh),
                             reads=[("ps", b)], writes=[("gua", ct)])

                def st_z_gv(t):
                    for ci in range(t["n"]):
                        b = bank()
                        T.mm([(banks[b][:], hb[:, kt, ci * 128:(ci + 1) * 128], w_in_sb[:, kt, 1536:2048])
                              for kt in range(8)], reads=S0ALL + [("win", 0), "hb"], writes=[("ps", b)])
                        T.op("act", lambda e, b=b: e.activation(out=gl[:], in_=banks[b][:], func=AF.Gelu_apprx_tanh),
                             reads=[("ps", b)], writes=["gl"])
                        T.op("dve", lambda e: e.bn_stats(bst[:, 0:6], gl[:]), reads=["gl"], writes=["bst"])
                        T.op("dve", lambda e: e.bn_aggr(small[:, 8:10], bst[:, 0:6]), reads=["bst"], writes=[("small", 8)])
                        T.op("dve", lambda e: e.tensor_scalar(small[:, 11:12], small[:, 9:10], LN_EPS, None, ALU.add),
                             reads=[("small", 8)], writes=[("small", 11)])
                        T.op("pool", lambda e: e.tensor_tensor(small[:, 10:11], small[:, 11:12], mhalf[:, 0:1], ALU.pow),
                             reads=[("small", 11), "mhalf"], writes=[("small", 10)])
                        T.op("dve", lambda e: e.tensor_scalar(gl[:], gl[:], small[:, 8:9], small[:, 10:11],
                                                               ALU.subtract, ALU.mult),
                             reads=["gl", ("small", 8), ("small", 10)], writes=["gl"])
                        T.op("dve", lambda e: e.tensor_tensor(gl[:], gl[:], rows[:, O_GMG:O_GMG + 512], ALU.mult),
                             reads=["gl", "rows"], writes=["gl"])
                        T.op("dve", lambda e, ci=ci: e.tensor_tensor(gvn[:, ci, :], gl[:], rows[:, O_GMB:O_GMB + 512], ALU.add),
                             reads=["gl", "rows"], writes=[("gvn", ci)])

                def st_spatial(t):
                    n, NT = t["n"], t["NT"]
                    for ct in range(4):
                        b = bank()
                        mms = []
                        for ci in range(n):
                            for hh in range(2):
                                hd = 2 * ct + hh
                                mms.append((banks[b][hh * 64:(hh + 1) * 64, ci * 128:(ci + 1) * 128],
                                            gvn[:, ci, hd * 64:(hd + 1) * 64], wsT[:, hd, :]))
                        T.mm(mms, reads=[("gvn", ci) for ci in range(n)] + ["wsT"], writes=[("ps", b)], single=True)
                        T.op("dve", lambda e, b=b, ct=ct: e.tensor_tensor(
                            stat[:, 0, 0:NT].rearrange("p (c t) -> p c t", t=128),
                            banks[b][:, 0:NT].rearrange("p (c t) -> p c t", t=128),
                            rows[:, O_BS + ct * 128: O_BS + (ct + 1) * 128].unsqueeze(1).to_broadcast([128, n, 128]),
                            ALU.add), reads=[("ps", b), "rows"], writes=[("stat", 0)])
                        T.op("dve", lambda e, ct=ct: e.tensor_tensor(yb[:, 4 + ct, 0:NT], gua[:, ct, 0:NT], stat[:, 0, 0:NT], ALU.mult),
                             reads=[("gua", ct), ("stat", 0)], writes=[("yb", 4 + ct)])
                        T.op("act", lambda e, ct=ct: e.activation(out=gsq[:, ct, 0:NT], in_=yb[:, 4 + ct, 0:NT], func=AF.Square),
                             reads=[("yb", 4 + ct)], writes=[("gsq", ct)])

                def rms_cols(t, base, src, skey):
                    n = t["n"]
                    par = t["ti"] % 2
                    b = bank()
                    for ci in range(n):
                        T.mm([(banks[b][:, ci:ci + 1], src[:, ct, ci * 128:(ci + 1) * 128], ones_b[:, 0:1]) for ct in range(4)],
                             reads=[(skey, ct) for ct in range(4)] + ["ones_b"], writes=[("ps", b)])
                    c0 = base + par * 4
                    T.op("dve", lambda e: e.tensor_scalar(small[:, 88:88 + n], banks[b][:, 0:n], 1.0 / 512, RMS_EPS, ALU.mult, ALU.add),
                         reads=[("ps", b)], writes=[("small", 88)])
                    T.op("pool", lambda e: e.tensor_tensor(small[:, c0:c0 + n], small[:, 88:88 + n], mhalf[:, 0:n], ALU.pow),
                         reads=[("small", 88), "mhalf"], writes=[("small", c0)])

                def st_ones_g(t):
                    rms_cols(t, 64, gsq, "gsq")

                def st_yb_g(t):
                    pass

                def st_conv_mm(t, cts=(0, 1, 2, 3)):
                    NT, toff, ti = t["NT"], t["toff"], t["ti"]
                    if "cb" not in t:
                        t["cb"] = []
                    for ct in cts:
                        ds_ = dstate["n"] % 2
                        dstate["n"] += 1
                        T.dma("sp", diag[ds_].rearrange("p k n -> p (k n)"), diag_scr[ct],
                              reads=[("dscr", ct)], writes=[("diag", ds_)])
                        b = bank(hold=True)
                        rd = [("diag", ds_), ("a", ti), "a_pre"] + ([("a", ti - 1)] if ti > 0 else [])
                        T.mm([(banks[b][:, 0:NT], diag[ds_][:, k, :], a_buf[:, ct, toff + k: toff + k + NT]) for k in range(KW)],
                             reads=rd, writes=[("ps", b)])
                        t["cb"].append(b)

                def st_conv_evac(t, cts=(0, 1, 2, 3)):
                    NT = t["NT"]
                    for ct in cts:
                        b = t["cb"][ct]
                        bias = cols[:, R_CB + ct:R_CB + ct + 1]
                        T.op("act", lambda e, b=b, ct=ct, bias=bias: e.activation(
                            out=cvb[:, ct, 0:NT], in_=banks[b][:, 0:NT], func=AF.Identity, bias=bias),
                            reads=[("ps", b), "cols"], writes=[("cvb", ct)])
                        T.op("act", lambda e, b=b, ct=ct, bias=bias: e.activation(
                            out=sq[:, ct, 0:NT], in_=banks[b][:, 0:NT], func=AF.Square, bias=bias),
                            reads=[("ps", b), "cols"], writes=[("sq", ct)])
                        T.op("act", lambda e, b=b, ct=ct, bias=bias: e.activation(
                            out=cv[:, ct, 0:NT], in_=banks[b][:, 0:NT], func=AF.Identity, bias=bias),
                            reads=[("ps", b), "cols"], writes=[("cv", ct // 2), ("cvx", ct)])
                        release(b)

                def st_ones_c(t):
                    NT = t["NT"]
                    bm = bank()
                    T.mm([(banks[bm][:, 0:NT], ones_b[:], cvb[:, ct, 0:NT]) for ct in range(4)],
                         reads=[("cvb", ct) for ct in range(4)] + ["ones_b"], writes=[("ps", bm)])
                    be = bank()
                    T.mm([(banks[be][:, 0:NT], ones_b[:], sq[:, ct, 0:NT]) for ct in range(4)],
                         reads=[("sq", ct) for ct in range(4)] + ["ones_b"], writes=[("ps", be)])
                    T.op("act", lambda e: e.activation(out=stat[:, 1, 0:NT], in_=banks[bm][:, 0:NT], func=AF.Identity, scale=1.0 / 512),
                         reads=[("ps", bm)], writes=[("stat", 1)])
                    T.op("dve", lambda e: e.tensor_tensor(stat[:, 2, 0:NT], stat[:, 1, 0:NT], stat[:, 1, 0:NT], ALU.mult),
                         reads=[("stat", 1)], writes=[("stat", 2)])
                    T.op("dve", lambda e: e.scalar_tensor_tensor(stat[:, 2, 0:NT], banks[be][:, 0:NT], 1.0 / 512, stat[:, 2, 0:NT],
                                                                  ALU.mult, ALU.subtract),
                         reads=[("ps", be), ("stat", 2)], writes=[("stat", 2)])
                    T.op("act", lambda e: e.activation(out=stat[:, 2, 0:NT], in_=stat[:, 2, 0:NT], func=AF.Sqrt,
                                                       bias=epsr[:, 1:2]),
                         reads=[("stat", 2), "epsr"], writes=[("stat", 2)])
                    T.op("dve", lambda e: e.reciprocal(stat[:, 2, 0:NT], stat[:, 2, 0:NT]),
                         reads=[("stat", 2)], writes=[("stat", 2)])
                    for ct in range(4):
                        T.op("dve", lambda e, ct=ct: e.tensor_tensor(cv[:, ct, 0:NT], cv[:, ct, 0:NT], stat[:, 1, 0:NT], ALU.subtract),
                             reads=[("cvx", ct), ("stat", 1)], writes=[("cvx", ct), ("cv", ct // 2)])
                        T.op("dve", lambda e, ct=ct: e.tensor_tensor(cv[:, ct, 0:NT], cv[:, ct, 0:NT], stat[:, 2, 0:NT], ALU.mult),
                             reads=[("cvx", ct), ("stat", 2)], writes=[("cvx", ct), ("cv", ct // 2)])
                    for ct in range(4):
                        T.op("act", lambda e, ct=ct: e.activation(out=yb[:, ct, 0:NT], in_=cv[:, ct, 0:NT], func=AF.Silu,
                                                                  scale=cols[:, R_CLG + ct:R_CLG + ct + 1],
                                                                  bias=cols[:, R_CLB + ct:R_CLB + ct + 1]),
                             reads=[("cvx", ct), "cols"], writes=[("yb", ct)])
                    for ct in range(4):
                        T.op("act", lambda e, ct=ct: e.activation(out=sq[:, ct, 0:NT], in_=yb[:, ct, 0:NT], func=AF.Square),
                             reads=[("yb", ct)], writes=[("sq", ct)])

                def st_ones_a(t):
                    rms_cols(t, 72, sq, "sq")

                def st_out(t, part):
                    par = t["ti"] % 2
                    k0, base = (4, 64) if part == "g" else (0, 72)
                    for ci, c in enumerate(t["tl"]):
                        for hf in range(2):
                            b = bank()
                            T.mm([(banks[b][:], yb[:, kt, ci * 128:(ci + 1) * 128], w_out_sb[:, kt, hf * 512:(hf + 1) * 512])
                                  for kt in range(k0, k0 + 4)], reads=[("yb", i) for i in range(k0, k0 + 4)] + ["wout"],
                                 writes=[("ps", b)])
                            col = base + par * 4 + ci
                            T.op("dve", lambda e, b=b, c=c, hf=hf, col=col: e.scalar_tensor_tensor(
                                x1[:, c, hf * 512:(hf + 1) * 512], banks[b][:], small[:, col:col + 1],
                                x1[:, c, hf * 512:(hf + 1) * 512], ALU.mult, ALU.add),
                                reads=[("ps", b), ("x1", c), ("small", base + par * 4)], writes=[("x1", c)])

                K_ = len(TL)
                gua_b = gua[:].rearrange("p a b -> p (a b)").bitcast(BF16).rearrange("p (k n) -> p k n", k=8)
                GK = [("gua", i) for i in range(4)]

                def norm2_tile(t, which):
                    tl = t["tl"]
                    if which == "halo":
                        norm_group([x1[:, 0, :]], [("x1", 0)], 16, 24, lambda dt: h2h[:, dt, 0:2], "h2h",
                                   (junk, xns), 16, lo=126, junk_key="gl")
                    elif which == 0:
                        norm_group([x1[:, c, :] for c in tl], [("x1", c) for c in tl], 16, 24,
                                   lambda dt: hb[:, dt, :], "hb", (junk, xns), 16, junk_key="gl")
                    else:
                        norm_group([x1[:, c, :] for c in tl], [("x1", c) for c in tl], 16, 24,
                                   lambda dt: gua_b[:, dt, :], GK, (junk, xns), 16, junk_key="gl")

                if half == 1:
                    load_x(1, [4, 5, 6, 7])
                    load_w_out()
                st_norm(TL[0]); st_z_gv(TL[0]); st_z_a(TL[0]); st_z_gu(TL[0])
                st_spatial(TL[0])
                for i in range(K_):
                    t = TL[i]
                    nx = TL[i + 1] if i + 1 < K_ else None
                    if half == 0 and nx is None:
                        late_blocks_start(gvn)
                    st_conv_mm(t, (0, 1))
                    if nx is not None:
                        st_norm(nx)
                    elif half == 1:
                        norm2_tile(TL[K_ - 2], 0)
                    st_conv_mm(t, (2, 3))
                    if half == 1 and i == 0:
                        fold_w_out()
                    st_conv_evac(t)
                    if nx is not None:
                        st_z_a(nx, (0, 1))
                    st_ones_c(t)
                    if half == 0 and i == 0:
                        hook_gt1()
                        fold_w_out()
                    if nx is not None:
                        st_z_a(nx, (2, 3))
                        st_z_gv(nx)
                        st_z_gu(nx)
                        if i + 1 == K_ - 1:
                            load_slice_up(0)
                    if half == 0 and nx is None:
                        def _n2():
                            norm2_tile(TL[0], "halo")
                            norm2_tile(TL[K_ - 2], 0)
                        late_blocks(after_mod=_n2)
                    st_ones_g(t)
                    st_out(t, "g")
                    st_ones_a(t)
                    st_out(t, "a")
                    if nx is not None:
                        st_spatial(nx)
                    else:
                        norm2_tile(t, 1)
                if half == 0:
                    T.op("pool", lambda e: e.tensor_copy(a_carry[:], a_buf[:, :, ntok: ntok + 30]),
                         reads=[("a", len(tiles) - 1)], writes=["a_carry"])
                T.barrier()
            if debug is not None and half == 0:
                dbg("x1", x1[:], [128, 9, D], [("x1", c) for c in range(9)])
                dbg("modc", modc[:], [128, 48], ["modc"])
                dbg("gt1", gt1[:], [128, D], ["gt1"])
                dbg("gt2", gt2[:], [128, D], ["gt2"])

            with contextlib.ExitStack() as sB:
                junk = sb(sB, "junkB", [128, D], BF16)
                xns = [sb(sB, "xnB%d" % i, [128, D], BF16) for i in range(2)]
                Uv = [sb(sB, "Uv%d" % i, [128, 514], F32) for i in range(2)]
                Ug = [sb(sB, "Ug%d" % i, [128, 514], F32) for i in range(2)]
                av = [sb(sB, "av%d" % i, [128, 512], F32) for i in range(2)]
                ag = [sb(sB, "ag%d" % i, [128, 512], F32) for i in range(2)]
                actb = [sb(sB, "actb%d" % i, [128, 6, 512], BF16) for i in range(2)]
                ostg = [sb(sB, "ostg%d" % i, [128, D], F32) for i in range(2)]

                own = list(range(1, 9)) if half == 0 else list(range(0, 8))
                if half == 0:
                    print("phase B SBUF bytes remaining:", nc.sbuf_bytes_remaining)

                def load_slice(s_):
                    load_slice_up(s_)
                    load_slice_down(s_)
                    return slice_views(s_)

                winfo = {0: slice_views(0)}
                gua_b = gua[:].rearrange("p a b -> p (a b)").bitcast(BF16).rearrange("p (k n) -> p k n", k=8)
                h2t = [hb[:], gua_b]
                h2k = [["hb"], [("gua", i) for i in range(4)]]
                load_slice_down(0)
                if debug is not None and half == 0:
                    dbg("hbd", hb[:], [128, 8, 512], ["hb"])
                    dbg("guad", gua_b, [128, 8, 512], [("gua", i) for i in range(4)])

                steps = [(s_, tt_) for s_ in range(4) for tt_ in range(2)]
                items = [(si, fi) for si, (s_, tt_) in enumerate(steps) for fi in range(len(SLICES[s_]))]
                DD = 2

                def head(g, si, fi):
                    s_, tt = steps[si]
                    f = SLICES[s_][fi]
                    wv, wg, wd, wkey = winfo[s_]
                    par = g % 2
                    if half == 0 and tt == 0:
                        for gi, wsrc in enumerate((wv, wg)):
                            b = bank()
                            T.mm([(banks[b][:, 0:2], wsrc[:, kt, fi * 128:(fi + 1) * 128], h2h[:, kt, 0:2]) for kt in range(8)],
                                 reads=[(wkey, "ab"[gi]), "h2h"], writes=[("ps", b)])
                            T.op("act", lambda e, b=b, f=f, gi=gi: e.activation(
                                out=carry[:, 0, f, gi, :], in_=banks[b][:, 0:2], func=AF.Identity, scale=msk[:, 0:1]),
                                reads=[("ps", b), "msk"], writes=[("carry", 0, f, gi)])
                    for gi, (wsrc, Ub, accb, nm) in enumerate(((wv, Uv, av, "v"), (wg, Ug, ag, "g"))):
                        j = f + gi * NFT
                        U = Ub[par]
                        acc = accb[par]
                        ukey = ("U" + nm, par)
                        ackey = ("acc" + nm, par)
                        b = bank()
                        T.mm([(banks[b][:], wsrc[:, kt, fi * 128:(fi + 1) * 128], h2t[tt][:, kt, :])
                              for kt in range(8)], reads=[(wkey, "ab"[gi])] + h2k[tt], writes=[("ps", b)])
                        T.op("act", lambda e, b=b, U=U: e.activation(out=U[:, 2:514], in_=banks[b][:], func=AF.Copy),
                             reads=[("ps", b)], writes=[ukey])
                        T.op("act", lambda e, b=b, f=f, gi=gi, tt=tt: e.activation(
                            out=carry[:, (tt + 1) % 2, f, gi, :], in_=banks[b][:, 510:512], func=AF.Copy),
                            reads=[("ps", b)], writes=[("carry", (tt + 1) % 2, f, gi)])
                        T.op("act", lambda e, U=U, f=f, gi=gi, tt=tt: e.activation(
                            out=U[:, 0:2], in_=carry[:, tt % 2, f, gi, :], func=AF.Copy),
                             reads=[("carry", tt % 2, f, gi)], writes=[ukey])
                        T.op("act", lambda e, b=b, acc=acc, j=j: e.activation(
                            out=acc[:], in_=banks[b][:], func=AF.Identity,
                            scale=cols[:, R_FW + 2 * 44 + j: R_FW + 2 * 44 + j + 1], bias=cols[:, R_FB + j: R_FB + j + 1]),
                            reads=[("ps", b), "cols"], writes=[ackey])
                        T.op("dve", lambda e, U=U, acc=acc, j=j: e.scalar_tensor_tensor(
                            acc[:], U[:, 1:513], cols[:, R_FW + 44 + j: R_FW + 44 + j + 1], acc[:], ALU.mult, ALU.add),
                            reads=[ukey, ackey, "cols"], writes=[ackey])
                        T.op("dve", lambda e, U=U, acc=acc, j=j: e.scalar_tensor_tensor(
                            acc[:], U[:, 0:512], cols[:, R_FW + j: R_FW + j + 1], acc[:], ALU.mult, ALU.add),
                            reads=[ukey, ackey, "cols"], writes=[ackey])

                def tail(g, si, fi):
                    par = g % 2
                    accv, accg = av[par], ag[par]
                    kv, kg = ("accv", par), ("accg", par)
                    ab = actb[si % 2]
                    T.op("act", lambda e: e.activation(out=accg[:], in_=accg[:], func=AF.Silu), reads=[kg], writes=[kg])
                    T.op("dve", lambda e: e.tensor_tensor(ab[:, fi, :], accv[:], accg[:], ALU.mult),
                         reads=[kv, kg], writes=[("actb", si % 2)])

                def down(si):
                    s_, tt = steps[si]
                    nf = len(SLICES[s_])
                    wv, wg, wd, wkey = winfo[s_]
                    ab = actb[si % 2]
                    akey = ("actb", si % 2)
                    for ci in range(4):
                        c = own[tt * 4 + ci]
                        for hf in range(2):
                            b = bank()
                            T.mm([(banks[b][:], ab[:, fi, ci * 128:(ci + 1) * 128], wd[:, fi, hf * 512:(hf + 1) * 512])
                                  for fi in range(nf)], reads=[akey, (wkey, "c")], writes=[("ps", b)])
                            T.op("dve", lambda e, b=b, c=c, hf=hf: e.tensor_tensor(
                                x1[:, c, hf * 512:(hf + 1) * 512], banks[b][:], x1[:, c, hf * 512:(hf + 1) * 512], ALU.add),
                                reads=[("ps", b), ("x1", c)], writes=[("x1", c)])
                        if s_ == 3 and half == 1 and tt == 1:
                            final_norm([c], tt * 4 + ci)
                    if s_ == 3 and not (half == 1 and tt == 1):
                        final_norm([own[tt * 4 + ci] for ci in range(4)], tt * 4)
                        if half == 0 and tt == 0:
                            load_x(1, [0, 1, 2, 3])

                def final_norm(cs, o0):
                    n = len(cs)
                    for ci, c in enumerate(cs):
                        T.op("act", lambda e, c=c, ci=ci: e.activation(out=junk[:], in_=x1[:, c, :], func=AF.Square,
                                                                       accum_out=small[:, 32 + ci:33 + ci]),
                             reads=[("x1", c)], writes=["junk", ("small", 32)])
                    T.op("dve", lambda e: e.tensor_scalar(small[:, 36:36 + n], small[:, 32:32 + n], 1.0 / D, RMS_EPS, ALU.mult, ALU.add),
                         reads=[("small", 32)], writes=[("small", 36)])
                    T.op("pool", lambda e: e.tensor_tensor(small[:, 40:40 + n], small[:, 36:36 + n], mhalf[:, 0:n], ALU.pow),
                         reads=[("small", 36), "mhalf"], writes=[("small", 40)])
                    for ci, c in enumerate(cs):
                        og = ostg[(o0 + ci) % 2]
                        okey = ("ostg", (o0 + ci) % 2)
                        T.op("dve", lambda e, c=c, og=og, ci=ci: e.scalar_tensor_tensor(
                            og[:], x1[:, c, :], small[:, 40 + ci:41 + ci], rows[:, O_FING:O_FING + D], ALU.mult, ALU.mult),
                            reads=[("x1", c), ("small", 40), "rows"], writes=[okey])
                        r0 = half * 1024 + (o0 + ci) * 128
                        T.dma("sp", out_d[r0:r0 + 128, :], og[:], reads=[okey])

                pending = []
                G = len(items)
                for g in range(G + 1):
                    if g < G:
                        si, fi = items[g]
                        s_, tt = steps[si]
                        if fi == 0 and tt == 1 and s_ + 1 < 4:
                            winfo[s_ + 1] = load_slice(s_ + 1)
                        if half == 0 and fi == 0 and tt == 1 and s_ == 3:
                            load_w_in()
                        head(g, si, fi)
                    if g >= 1:
                        psi, pfi = items[g - 1]
                        tail(g - 1, psi, pfi)
                        if pfi == len(SLICES[steps[psi][0]]) - 1:
                            pending.append([DD, psi])
                    for p in list(pending):
                        if p[0] <= 0 or g >= G:
                            down(p[1])
                            pending.remove(p)
                        else:
                            p[0] -= 1
                T.barrier()
        T.finish()
    return nc, dbg_out


def _pack_inputs(inputs):
    f = np.float32
    x = np.asarray(inputs["x"], f)
    c = np.asarray(inputs["c"], f)
    g = lambda k: np.asarray(inputs[k], f)[0]
    vec = np.zeros((NROWS, 128), f)
    vec[R_BADA:R_BADA + 48] = g("b_ada").reshape(48, 128)
    vec[R_G1:R_G1 + 8] = g("norm1_gain").reshape(8, 128)
    vec[R_CW:R_CW + 124] = g("conv_dw_w").reshape(31 * 4, 128)
    vec[R_CB:R_CB + 4] = g("conv_dw_b").reshape(4, 128)
    vec[R_CLG:R_CLG + 4] = g("conv_ln_g").reshape(4, 128)
    vec[R_CLB:R_CLB + 4] = g("conv_ln_b").reshape(4, 128)
    vec[R_MOG:R_MOG + 8] = g("mix_out_gain").reshape(8, 128)
    vec[R_G2:R_G2 + 8] = g("norm2_gain").reshape(8, 128)
    vec[R_FW:R_FW + 132] = g("ffn_dw_w").reshape(3 * 44, 128)
    vec[R_FB:R_FB + 44] = g("ffn_dw_b").reshape(44, 128)
    rows = np.zeros((128, NRB), f)
    rows[:, O_GMG:O_GMG + 512] = g("gm_ln_g")[None, :]
    rows[:, O_GMB:O_GMB + 512] = g("gm_ln_b")[None, :]
    rows[:, O_FING:O_FING + 1024] = np.asarray(inputs["final_gain"], f)[None, :]
    b_ada = g("b_ada")
    rows[:, O_BGT1:O_BGT1 + 1024] = b_ada[2048:3072][None, :]
    rows[:, O_BGT2:O_BGT2 + 1024] = b_ada[5120:6144][None, :]
    bs = g("gm_bs")
    rows[:, O_BS:O_BS + 512] = np.repeat(bs, 64, axis=0).reshape(4, 128, 128).transpose(1, 0, 2).reshape(128, 512)
    wsT = np.ascontiguousarray(g("gm_ws").transpose(2, 0, 1))
    shared = {
        "vecs": vec, "rows": rows, "wsT": wsT,
        "w_ada": g("w_ada"), "w_in": g("w_in"), "w_out": g("w_out"),
        "w_up": g("w_up"), "w_down": g("w_down"),
    }
    in_maps = []
    for i in range(NCORES):
        b, q = divmod(i, 4)
        t0 = q * TOK_PER_CORE
        xe = np.zeros((17 * 128, D), f)
        if q > 0:
            xe[0:128] = x[b, t0 - 128:t0]
        xe[128:] = x[b, t0:t0 + TOK_PER_CORE]
        m = dict(shared)
        m["x_ext"] = xe
        m["msk"] = np.full((128, 1), 1.0 if q > 0 else 0.0, f)
        m["c_col"] = np.ascontiguousarray(c[b].reshape(8, 128).T)
        in_maps.append(m)
    return in_maps


_NC_CACHE = {}


def kernel(**inputs):
    if "nc" not in _NC_CACHE:
        _NC_CACHE["nc"] = build_program()[0]
    nc = _NC_CACHE["nc"]
    in_maps = _pack_inputs(inputs)
    res = run_bass_kernel_spmd(nc, in_maps, core_ids=list(range(NCORES)))
    out = np.zeros((2, SEQ, D), np.float32)
    for i in range(NCORES):
        b, q = divmod(i, 4)
        out[b, q * TOK_PER_CORE:(q + 1) * TOK_PER_CORE] = res.results[i]["out"]
    return out
```

```python
import contextlib
import numpy as np
import concourse.bass as bass
import concourse.mybir as mybir
from concourse.bass_utils import run_bass_kernel_spmd

F32 = mybir.dt.float32
BF16 = mybir.dt.bfloat16
AF = mybir.ActivationFunctionType
ALU = mybir.AluOpType

D = 1024
DFF = 2816
NFT = 22
SEQ = 8192
NCORES = 8
TOK_PER_CORE = 2048
NCH = 16
RMS_EPS = 1e-6
LN_EPS = 1e-5
KW = 31

R_BADA = 0
R_G1 = 48
R_CW = 56
R_CB = 180
R_CLG = 184
R_CLB = 188
R_MOG = 192
R_G2 = 200
R_FW = 208
R_FB = 340
NROWS = 384
O_GMG = 0
O_GMB = 512
O_FING = 1024
O_BS = 2048
NRP = 2560
O_BGT1 = 2560
O_BGT2 = 3584
NRB = 4608

SLICES = [list(range(0, 6)), list(range(6, 12)), list(range(12, 17)), list(range(17, 22))]
DIAG_ENG = "dve"
SLOT_ELEMS = 18432


class Tracker:
    def __init__(self, nc, es):
        self.nc = nc
        self.eng = {"pe": nc.tensor, "act": nc.scalar, "dve": nc.vector,
                    "pool": nc.gpsimd, "sp": nc.sync}
        self.sem = {}
        for e in ("pe", "act", "dve", "pool"):
            self.sem[e] = es.enter_context(nc.semaphore("c_" + e))
        self.cnt = {e: 0 for e in self.sem}
        self.waited = {e: {} for e in self.eng}
        self.state = {}
        self.ndma = 72
        self.dsem = [es.enter_context(nc.semaphore("d%d" % i)) for i in range(self.ndma)]
        self.dval = [0] * self.ndma
        self.drr = 0
        self.semname = {}
        for e, s in self.sem.items():
            self.semname[id(s)] = e
        self.n_wait = 0
        self.n_inst = 0

    def _deps(self, eng, reads, writes):
        deps = []
        for k in reads:
            st = self.state.get(k)
            if st is not None and st[0] is not None:
                deps.append(st[0] + ("raw",))
        for k in writes:
            st = self.state.get(k)
            if st is not None:
                if st[0] is not None:
                    deps.append(st[0] + ("waw",))
                for d in st[1].values():
                    deps.append(d + ("war",))
        return deps

    def _wait(self, eng, deps):
        best = {}
        for (sem, val, prod, kind) in deps:
            if prod == eng:
                if eng == "pe" or kind != "raw":
                    continue
            key = id(sem)
            if key not in best or best[key][1] < val:
                best[key] = (sem, val)
        for key, (sem, val) in best.items():
            if self.waited[eng].get(key, 0) >= val:
                continue
            self.eng[eng].wait_ge(sem, val)
            self.waited[eng][key] = val
            self.n_wait += 1

    def _update(self, eng, dep, reads, writes):
        for k in writes:
            self.state[k] = [dep, {}]
        for k in reads:
            st = self.state.get(k)
            if st is None:
                st = [None, {}]
                self.state[k] = st
            st[1][dep[2] if dep[2] != "dma" else ("dma", id(dep[0]))] = dep

    def op(self, eng, fn, reads=(), writes=()):
        self._wait(eng, self._deps(eng, reads, writes))
        ins = fn(self.eng[eng])
        self.cnt[eng] += 1
        ins.then_inc(self.sem[eng], 1)
        self.n_inst += 1
        dep = (self.sem[eng], self.cnt[eng], eng)
        self._update(eng, dep, reads, writes)
        return dep

    def mm(self, mms, reads=(), writes=(), single=False):
        self._wait("pe", self._deps("pe", reads, writes))
        n = len(mms)
        ins = None
        for i, (out, lhsT, rhs) in enumerate(mms):
            if single:
                ins = self.nc.tensor.matmul(out, lhsT, rhs, start=True, stop=True)
            else:
                ins = self.nc.tensor.matmul(out, lhsT, rhs, start=(i == 0), stop=(i == n - 1))
        self.cnt["pe"] += 1
        ins.then_inc(self.sem["pe"], 1)
        self.n_inst += n
        dep = (self.sem["pe"], self.cnt["pe"], "pe")
        self._update("pe", dep, reads, writes)
        return dep

    def transposes(self, tps, ident, reads=(), writes=()):
        self._wait("pe", self._deps("pe", reads, writes))
        ins = None
        for (out, in_) in tps:
            ins = self.nc.tensor.transpose(out, in_, ident)
        self.cnt["pe"] += 1
        ins.then_inc(self.sem["pe"], 1)
        self.n_inst += len(tps)
        dep = (self.sem["pe"], self.cnt["pe"], "pe")
        self._update("pe", dep, reads, writes)
        return dep

    def dma(self, q, out, in_, reads=(), writes=()):
        self._wait(q, self._deps(q, reads, writes))
        i = self.drr
        self.drr = (self.drr + 1) % self.ndma
        sem = self.dsem[i]
        if self.dval[i] > 0 and self.waited[q].get(id(sem), 0) < self.dval[i]:
            self.eng[q].wait_ge(sem, self.dval[i])
            self.waited[q][id(sem)] = self.dval[i]
        self.eng[q].dma_start(out=out, in_=in_).then_inc(sem, 16)
        self.dval[i] += 16
        self.n_inst += 1
        dep = (sem, self.dval[i], "dma")
        self._update("dma", dep, reads, writes)
        return dep

    def barrier(self):
        for e in self.eng:
            for p in self.sem:
                if p == e or self.cnt[p] == 0:
                    continue
                if self.waited[e].get(id(self.sem[p]), 0) < self.cnt[p]:
                    self.eng[e].wait_ge(self.sem[p], self.cnt[p])
                    self.waited[e][id(self.sem[p])] = self.cnt[p]
            for i in range(self.ndma):
                if self.dval[i] > 0 and self.waited[e].get(id(self.dsem[i]), 0) < self.dval[i]:
                    self.eng[e].wait_ge(self.dsem[i], self.dval[i])
                    self.waited[e][id(self.dsem[i])] = self.dval[i]

    def finish(self):
        e = "sp"
        for i in range(self.ndma):
            if self.dval[i] > 0 and self.waited[e].get(id(self.dsem[i]), 0) < self.dval[i]:
                self.eng[e].wait_ge(self.dsem[i], self.dval[i])
                self.waited[e][id(self.dsem[i])] = self.dval[i]
        for p in self.sem:
            if self.cnt[p] and self.waited[e].get(id(self.sem[p]), 0) < self.cnt[p]:
                self.eng[e].wait_ge(self.sem[p], self.cnt[p])


def build_program(debug=None):
    nc = bass.Bass("TRN2", target_bir_lowering=False)
    dbg_out = {}

    def din(name, shape):
        return nc.dram_tensor(name, list(shape), F32, kind="ExternalInput").ap()

    x_ext = din("x_ext", [17 * 128, D])
    msk_d = din("msk", [128, 1])
    ccol_d = din("c_col", [128, 8])
    vecs_d = din("vecs", [NROWS, 128])
    rows_d = din("rows", [128, NRB])
    wsT_d = din("wsT", [128, 8, 128])
    w_ada = din("w_ada", [D, 6 * D])
    w_in = din("w_in", [D, 2 * D])
    w_out = din("w_out", [D, D])
    w_up = din("w_up", [D, 2 * DFF])
    w_down = din("w_down", [DFF, D])
    out_d = nc.dram_tensor("out", [TOK_PER_CORE, D], F32, kind="ExternalOutput").ap()
    diag_scr = nc.dram_tensor("diag_scr", [4, 128, KW * 128], BF16, kind="Internal").ap()

    uid = [0]

    with contextlib.ExitStack() as es:
        T = Tracker(nc, es)

        def sb(st, name, shape, dt):
            uid[0] += 1
            return st.enter_context(nc.sbuf_tensor("%s_%d" % (name, uid[0]), list(shape), dt))

        banks = [es.enter_context(nc.psum_tensor("bank%d" % i, [128, 512], F32)) for i in range(8)]
        brr = [0]

        held = set()

        def bank(hold=False):
            for _ in range(8):
                i = brr[0]
                brr[0] = (i + 1) % 8
                if i not in held:
                    if hold:
                        held.add(i)
                    return i
            raise RuntimeError("all PSUM banks held")

        def release(i):
            held.discard(i)

        cols = sb(es, "cols", [128, NROWS], F32)
        modc = sb(es, "modc", [128, 48], F32)
        ident_f = sb(es, "ident_f", [128, 128], F32)
        ident_b = sb(es, "ident_b", [128, 128], BF16)
        ones_b = sb(es, "ones_b", [128, 128], BF16)
        cmask = sb(es, "cmask", [128, 128], F32)
        iot = sb(es, "iot", [128, 128], F32)
        rows = sb(es, "rows", [128, NRP], F32)
        gt1 = sb(es, "gt1", [128, D], F32)
        gt2 = sb(es, "gt2", [128, D], F32)
        wsT = sb(es, "wsT", [128, 8, 128], BF16)
        msk = sb(es, "msk", [128, 1], F32)
        ccol = sb(es, "ccol", [128, 8], F32)
        cact = sb(es, "cact", [128, 8], BF16)
        cact_rep = sb(es, "cact_rep", [128, 8, 128], BF16)
        carry = sb(es, "carry", [128, 2, NFT, 2, 2], F32)
        cwb = sb(es, "cwb", [128, 4, KW], BF16)
        a_carry = sb(es, "a_carry", [128, 4, 30], BF16)
        small = sb(es, "small", [128, 96], F32)
        mraw = sb(es, "mraw", [128, 32], F32)
        mhalf = sb(es, "mhalf", [128, 8], F32)
        epsr = sb(es, "epsr", [128, 2], F32)
        hb = sb(es, "hb", [128, 8, 512], BF16)
        gua = sb(es, "gua", [128, 4, 512], F32)
        h2h = sb(es, "h2h", [128, 8, 2], BF16)
        x1 = sb(es, "x1", [128, 9, D], F32)
        slot = [sb(es, "slot0", [128, SLOT_ELEMS], BF16), sb(es, "slot1", [128, SLOT_ELEMS], BF16)]

        def dbg(name, ap, shape, reads):
            if debug is None or name not in debug:
                return
            t = nc.dram_tensor("dbg_" + name, list(shape), ap.dtype, kind="ExternalOutput").ap()
            dbg_out[name] = t
            T.dma("sp", t, ap, reads=reads)

        for m_, off_, keys_ in ((0, 0, ["wout"]), (1, 8192, [("diag", 0), ("diag", 1)])):
            T.dma("pool", slot[1][:, off_:off_ + 8192].rearrange("p (k n) -> p k n", k=8),
                  w_ada[:, m_ * D:(m_ + 1) * D].rearrange("(kt p) n -> p kt n", p=128), writes=keys_)
        T.dma("sp", msk[:], msk_d, writes=["msk"])
        T.dma("sp", ccol[:], ccol_d, writes=["ccol"])
        T.dma("sp", rows[:], rows_d[:, 0:NRP], writes=["rows"])
        T.op("pool", lambda e: e.iota(iot[:], [[1, 128]], base=0, channel_multiplier=-1,
                                      allow_small_or_imprecise_dtypes=True), writes=["iot"])
        T.op("pool", lambda e: e.tensor_single_scalar(ident_f[:], iot[:], 0.0, ALU.is_equal),
             reads=["iot"], writes=["ident_f"])
        T.op("pool", lambda e: e.tensor_single_scalar(cmask[:], iot[:], 0.0, ALU.is_ge),
             reads=["iot"], writes=["cmask"])
        T.op("pool", lambda e: e.tensor_copy(ident_b[:], ident_f[:]), reads=["ident_f"], writes=["ident_b"])
        T.op("pool", lambda e: e.memset(ones_b[:], 1.0), writes=["ones_b"])
        T.op("pool", lambda e: e.memset(small[:], 0.0), writes=["small"])
        T.op("pool", lambda e: e.memset(mhalf[:], -0.5), writes=["mhalf"])
        T.op("pool", lambda e: e.memset(epsr[:, 0:1], RMS_EPS), writes=["epsr"])
        T.op("pool", lambda e: e.memset(epsr[:, 1:2], LN_EPS), writes=["epsr"])

        wadaA = slot[1][:, 0:8192].rearrange("p (k n) -> p k n", k=8)
        wadaB = slot[1][:, 8192:16384].rearrange("p (k n) -> p k n", k=8)
        wadaC = x1[:, 5:9, :].rearrange("p c d -> p (c d)").bitcast(BF16).rearrange("p (k n) -> p k n", k=8)
        KA = ["wout"]
        KB = [("diag", 0), ("diag", 1)]
        KC = [("x1", c) for c in range(5, 9)]

        def ada_cols(wb, keys, bcol, cm):
            for j in range(8):
                T.mm([(banks[bcol][:, cm * 8 + j: cm * 8 + j + 1], wb[:, kt, j * 128:(j + 1) * 128],
                       cact[:, kt:kt + 1]) for kt in range(8)], reads=keys + ["cact"], writes=[("ps", bcol)])

        def ada_rows(wb, keys, dst, dkey, brow, bkeys):
            for hf in range(2):
                b = bank()
                T.mm([(banks[b][:], cact_rep[:, kt, :], wb[:, kt, hf * 512:(hf + 1) * 512])
                      for kt in range(8)], reads=keys + ["cact_rep"], writes=[("ps", b)])
                T.op("dve", lambda e, b=b, hf=hf: e.tensor_tensor(
                    dst[:, hf * 512:(hf + 1) * 512], banks[b][:], brow[:, hf * 512:(hf + 1) * 512], ALU.add),
                    reads=[("ps", b)] + bkeys, writes=[dkey])

        def ada_dma(m, wb, keys):
            T.dma("pool", wb, w_ada[:, m * D:(m + 1) * D].rearrange("(kt p) n -> p kt n", p=128), writes=keys)

        def mod_raw(bcol, cm, mod):
            T.op("dve", lambda e: e.tensor_tensor(
                mraw[:, cm * 8:(cm + 1) * 8], banks[bcol][:, cm * 8:(cm + 1) * 8],
                cols[:, R_BADA + mod * 8: R_BADA + mod * 8 + 8], ALU.add),
                reads=[("ps", bcol), "cols"], writes=["mraw"])

        def mod_finish(sh_c, sc_c, so, g):
            T.op("dve", lambda e: e.scalar_tensor_tensor(
                modc[:, so:so + 8], mraw[:, sc_c * 8:sc_c * 8 + 8], 1.0, cols[:, g:g + 8], ALU.add, ALU.mult),
                reads=["mraw", "cols"], writes=["modc"])
            T.op("dve", lambda e: e.tensor_copy(modc[:, so + 8:so + 16], mraw[:, sh_c * 8:sh_c * 8 + 8]),
                 reads=["mraw"], writes=["modc"])

        bgt = slot[0][:, 16384:18432].bitcast(F32)
        kst = [(0, "c")]

        def setup_part1(gl, sq, stat):
            vecs = gl[:, 0:384].rearrange("p (j f) -> p j f", j=3)
            wsf = sq[:].rearrange("p a b -> p (a b)").bitcast(F32).rearrange("p (h t) -> p h t", h=8)
            ksq = [("sq", i) for i in range(4)]
            T.dma("sp", vecs, vecs_d.rearrange("(j p) f -> p j f", p=128), writes=["gl"])
            T.dma("sp", wsf, wsT_d, writes=ksq)
            load_w_in(after=KB)
            ada_dma(2, wadaC, KC)
            T.dma("sp", bgt, rows_d[:, O_BGT1:O_BGT1 + D], writes=kst)
            for j in range(3):
                b = bank()
                T.transposes([(banks[b][:, 0:128], vecs[:, j, :])], ident_f[:],
                             reads=["gl", "ident_f"], writes=[("ps", b)])
                T.op("dve", lambda e, b=b, j=j: e.tensor_copy(cols[:, j * 128:(j + 1) * 128], banks[b][:, 0:128]),
                     reads=[("ps", b)], writes=["cols"])
            T.op("dve", lambda e: e.tensor_tensor(
                wsT[:], wsf, cmask[:].unsqueeze(1).to_broadcast([128, 8, 128]), ALU.mult),
                reads=ksq + ["cmask"], writes=["wsT"])
            T.op("act", lambda e: e.activation(out=cact[:], in_=ccol[:], func=AF.Silu),
                 reads=["ccol"], writes=["cact"])
            for kt in range(8):
                T.op("dve", lambda e, kt=kt: e.tensor_copy(
                    cact_rep[:, kt, :], cact[:, kt:kt + 1].to_broadcast([128, 128])),
                    reads=["cact"], writes=["cact_rep"])
            T.op("dve", lambda e: e.tensor_copy(
                cwb[:], cols[:, R_CW:R_CW + 4 * KW].rearrange("p (k c) -> p c k", c=4)),
                reads=["cols"], writes=["cwb"])
            regs = [
                (hb[:].rearrange("p a b -> p (a b)")[:, 0:KW * 128], ["hb"]),
                (gua[:].rearrange("p a b -> p (a b)").bitcast(BF16)[:, 0:KW * 128], [("gua", i) for i in range(4)]),
                (x1[:, 1:3, :].rearrange("p a b -> p (a b)").bitcast(BF16)[:, 0:KW * 128], [("x1", 1), ("x1", 2)]),
                (x1[:, 3:5, :].rearrange("p a b -> p (a b)").bitcast(BF16)[:, 0:KW * 128], [("x1", 3), ("x1", 4)]),
            ]
            for ct in range(4):
                rv, rk = regs[ct]
                T.op("dve", lambda e, ct=ct, rv=rv: e.tensor_tensor(
                    rv.rearrange("p (k n) -> p k n", k=KW), ident_b[:].unsqueeze(1).to_broadcast([128, KW, 128]),
                    cwb[:, ct, :].unsqueeze(2).to_broadcast([128, KW, 128]), ALU.mult),
                    reads=["ident_b", "cwb"], writes=rk)
                T.dma("sp", diag_scr[ct], rv, reads=rk, writes=[("dscr", ct)])
            bcol = bank()
            ada_cols(wadaA, KA, bcol, 0)
            mod_raw(bcol, 0, 0)
            T.dma("pool", w_out_sb, w_out.rearrange("(kt p) n -> p kt n", p=128), writes=["wout"])
            bcol = bank()
            ada_cols(wadaB, KB, bcol, 1)
            mod_raw(bcol, 1, 1)
            mod_finish(0, 1, 0, R_G1)

        def hook_gt1():
            ada_rows(wadaC, KC, gt1, "gt1", bgt, kst)
            T.dma("sp", bgt, rows_d[:, O_BGT2:O_BGT2 + D], writes=kst)
            load_x(0, [5, 6, 7, 8])

        late = {}

        def late_regions(gvn):
            regs = []
            for r in range(2):
                v = gua[:, 2 * r:2 * r + 2, :].rearrange("p a b -> p (a b)").bitcast(BF16).rearrange("p (k n) -> p k n", k=8)
                regs.append((v, [("gua", 2 * r), ("gua", 2 * r + 1)]))
            for r in range(2):
                v = hb[:, 4 * r:4 * r + 4, :].rearrange("p a b -> p (a b)").rearrange("p (k n) -> p k n", k=8)
                regs.append((v, ["hb"]))
            regs.append((gvn[:].rearrange("p a b -> p (a b)").rearrange("p (k n) -> p k n", k=8), [("gvn", i) for i in range(4)]))
            return regs

        RMAP = [0, 1, 4, 2, 3, 0, 1, 4, 0, 1, 4, 0]

        def late_issue(i):
            m, q = late["plan"][i]
            wb, keys = late["regs"][RMAP[i]]
            T.dma("pool", wb, w_ada[:, m * D + q * 256: m * D + (q + 1) * 256].rearrange("(kt p) n -> p kt n", p=128),
                  writes=keys)
            late["issued"][i] = (wb, keys)

        def late_blocks_start(gvn):
            late["regs"] = late_regions(gvn)
            late["plan"] = [(m, q) for m in (3, 4, 5) for q in range(4)]
            late["issued"] = {}
            seen = set()
            for i, r in enumerate(RMAP):
                if r not in seen:
                    seen.add(r)
                    late_issue(i)

        def late_blocks(after_mod=None):
            plan = late["plan"]
            for i, (m, q) in enumerate(plan):
                wb, keys = late["issued"][i]
                if m in (3, 4):
                    cm = 2 if m == 3 else 3
                    bcol = bank()
                    for j in range(2):
                        T.mm([(banks[bcol][:, j:j + 1], wb[:, kt, j * 128:(j + 1) * 128], cact[:, kt:kt + 1]) for kt in range(8)],
                             reads=keys + ["cact"], writes=[("ps", bcol)])
                    c0 = cm * 8 + q * 2
                    T.op("dve", lambda e, bcol=bcol, c0=c0, m=m, q=q: e.tensor_tensor(
                        mraw[:, c0:c0 + 2], banks[bcol][:, 0:2],
                        cols[:, R_BADA + m * 8 + q * 2: R_BADA + m * 8 + q * 2 + 2], ALU.add),
                        reads=[("ps", bcol), "cols"], writes=["mraw"])
                else:
                    b = bank()
                    T.mm([(banks[b][:, 0:256], cact_rep[:, kt, :], wb[:, kt, :]) for kt in range(8)],
                         reads=keys + ["cact_rep"], writes=[("ps", b)])
                    T.op("dve", lambda e, b=b, q=q: e.tensor_tensor(
                        gt2[:, q * 256:(q + 1) * 256], banks[b][:, 0:256], bgt[:, q * 256:(q + 1) * 256], ALU.add),
                        reads=[("ps", b)] + kst, writes=["gt2"])
                for j in range(i + 1, len(plan)):
                    if RMAP[j] == RMAP[i]:
                        if j not in late["issued"]:
                            late_issue(j)
                        break
                if (m, q) == (4, 3):
                    mod_finish(2, 3, 16, R_G2)
                    if after_mod is not None:
                        after_mod()

        def load_x(half_, tl):
            xrow0_ = 0 if half_ == 0 else 9 * 128
            c0, n = tl[0], len(tl)
            T.dma("sp", x1[:, c0:c0 + n, :],
                  x_ext[xrow0_ + c0 * 128: xrow0_ + (c0 + n) * 128, :].rearrange("(c p) d -> p c d", p=128),
                  writes=[("x1", c) for c in tl])

        S0ALL = [(0, "a"), (0, "b"), (0, "c")]

        def slice_views(s_):
            nf = len(SLICES[s_])
            buf = slot[s_ % 2]
            wv = buf[:, 0:8 * nf * 128].rearrange("p (k n) -> p k n", k=8)
            wg = buf[:, 6144:6144 + 8 * nf * 128].rearrange("p (k n) -> p k n", k=8)
            wd = buf[:, 12288:12288 + nf * 1024].rearrange("p (f n) -> p f n", f=nf)
            return wv, wg, wd, s_ % 2

        def load_w_in(after=()):
            wv_ = w_in.rearrange("(kt p) n -> p kt n", p=128)
            for pi, (c0, c1) in enumerate(((1536, 2048), (0, 1024), (1024, 1536))):
                T.dma("pool", w_in_sb[:, :, c0:c1], wv_[:, :, c0:c1], reads=list(after) if pi == 1 else [],
                      writes=(S0ALL if pi == 0 else []) + [("win", pi)])

        def load_slice_down(s_):
            wv, wg, wd, key = slice_views(s_)
            f0 = SLICES[s_][0]
            nf = len(SLICES[s_])
            T.dma("pool", wd, w_down[f0 * 128:(f0 + nf) * 128, :].rearrange("(f p) n -> p f n", p=128), writes=[(key, "c")])
            for fi in range(nf):
                T.op("pool", lambda e, fi=fi, wd=wd: e.tensor_tensor(wd[:, fi, :], wd[:, fi, :], gt2[:], ALU.mult),
                     reads=[(key, "c"), "gt2"], writes=[(key, "c")])

        def load_slice_up(s_):
            nf = len(SLICES[s_])
            f0 = SLICES[s_][0] * 128
            wv, wg, wd, key = slice_views(s_)
            T.dma("pool", wv, w_up[:, f0:f0 + nf * 128].rearrange("(kt p) n -> p kt n", p=128), writes=[(key, "a")])
            T.dma("pool", wg, w_up[:, DFF + f0:DFF + f0 + nf * 128].rearrange("(kt p) n -> p kt n", p=128), writes=[(key, "b")])


        w_in_sb = slot[0][:, 0:16384].rearrange("p (k n) -> p k n", k=8)
        w_out_sb = slot[1][:, 0:8192].rearrange("p (k n) -> p k n", k=8)
        diag = [slot[1][:, 8192 + i * 3968: 8192 + (i + 1) * 3968].rearrange("p (k n) -> p k n", k=KW)
                for i in range(2)]

        def norm_group(x_aps, xkeys, scol, shcol, dst_fn, dst_key, tmp, ssl, lo=0, junk_key="junk"):
            junk, xns = tmp
            n = len(x_aps)
            dkl = dst_key if isinstance(dst_key, list) else [dst_key]
            for ci in range(n):
                T.op("act", lambda e, ci=ci: e.activation(out=junk, in_=x_aps[ci], func=AF.Square,
                                                          accum_out=small[:, ssl + ci:ssl + ci + 1]),
                     reads=[xkeys[ci]], writes=[junk_key, ("small", ssl)])
            T.op("dve", lambda e: e.tensor_scalar(small[:, ssl + 4:ssl + 4 + n], small[:, ssl:ssl + n], 1.0 / D, RMS_EPS,
                                                   ALU.mult, ALU.add),
                 reads=[("small", ssl)], writes=[("small", ssl + 4)])
            T.op("pool", lambda e: e.tensor_tensor(small[:, ssl + 8:ssl + 8 + n], small[:, ssl + 4:ssl + 4 + n],
                                                    mhalf[:, 0:n], ALU.pow),
                 reads=[("small", ssl + 4), "mhalf"], writes=[("small", ssl + 8)])
            bks = [bank() for _ in range(4)]
            pbs = [banks[b][:].bitcast(BF16) for b in bks]
            for ci in range(n):
                xn = xns[ci % 2]
                xk = ("xn", ci % 2)
                T.op("dve", lambda e, ci=ci, xn=xn: e.tensor_scalar(xn[:], x_aps[ci], small[:, ssl + 8 + ci:ssl + 9 + ci], None, ALU.mult),
                     reads=[xkeys[ci], ("small", ssl + 8)], writes=[xk])
                T.transposes([(pbs[dt // 2][:, (dt % 2) * 512 + ci * 128:(dt % 2) * 512 + (ci + 1) * 128],
                               xn[:, dt * 128:(dt + 1) * 128]) for dt in range(8)], ident_b[:],
                             reads=[xk, "ident_b"], writes=[("ps", b) for b in bks])
            for dt in range(8):
                pb = pbs[dt // 2]
                q = dt % 2
                if dt % 2 == 0:
                    T.op("act", lambda e, dt=dt, q=q, pb=pb: e.activation(
                        out=dst_fn(dt), in_=pb[:, q * 512 + lo:q * 512 + n * 128], func=AF.Identity,
                        scale=modc[:, scol + dt:scol + dt + 1], bias=modc[:, shcol + dt:shcol + dt + 1]),
                        reads=[("ps", bks[dt // 2]), "modc"], writes=dkl)
                else:
                    T.op("dve", lambda e, dt=dt, q=q, pb=pb: e.tensor_scalar(
                        dst_fn(dt), pb[:, q * 512 + lo:q * 512 + n * 128],
                        modc[:, scol + dt:scol + dt + 1], modc[:, shcol + dt:shcol + dt + 1], ALU.mult, ALU.add),
                        reads=[("ps", bks[dt // 2]), "modc"], writes=dkl)

        for half in range(2):
            nchk = 9 if half == 0 else 8
            ntok = nchk * 128
            if half == 0:
                tiles = [[0], [1, 2, 3, 4], [5, 6, 7, 8]]
            else:
                tiles = [[0, 1, 2, 3], [4, 5, 6, 7]]
            if half == 0:
                load_x(0, tiles[0])
            with contextlib.ExitStack() as sA:
                a_buf = sb(sA, "a_buf", [128, 4, 30 + ntok], BF16)
                xns = [sb(sA, "xn%d" % i, [128, D], BF16) for i in range(2)]
                sig = sb(sA, "sig", [128, 512], F32)
                gl = sb(sA, "gl", [128, 512], F32)
                junk = gl[:].bitcast(BF16)
                gvn = sb(sA, "gvn", [128, 4, 512], BF16)
                cv = sb(sA, "cv", [128, 4, 512], F32)
                cvb = sb(sA, "cvb", [128, 4, 512], BF16)
                sq = sb(sA, "sq", [128, 4, 512], BF16)
                stat = sb(sA, "stat", [128, 3, 512], F32)
                gsq = sb(sA, "gsq", [128, 4, 512], BF16)
                yb = sb(sA, "yb", [128, 8, 512], BF16)
                bst = sb(sA, "bst", [128, 6], F32)
                if half == 0:
                    print("phase A SBUF bytes remaining:", nc.sbuf_bytes_remaining)

                if half == 0:
                    setup_part1(gl, sq, stat)
                    load_x(0, tiles[1])
                if half == 0:
                    T.op("pool", lambda e: e.memset(a_buf[:, :, 0:30], 0.0), writes=["a_pre"])
                else:
                    T.op("pool", lambda e: e.tensor_copy(a_buf[:, :, 0:30], a_carry[:]),
                         reads=["a_carry"], writes=["a_pre"])

                def load_w_out():
                    T.dma("pool", w_out_sb, w_out.rearrange("(kt p) n -> p kt n", p=128), writes=["wout"])

                def fold_w_out():
                    for ct in range(8):
                        T.op("dve", lambda e, ct=ct: e.scalar_tensor_tensor(
                            w_out_sb[:, ct, :], w_out_sb[:, ct, :], cols[:, R_MOG + ct:R_MOG + ct + 1], gt1[:], ALU.mult, ALU.mult),
                            reads=["wout", "gt1", "cols"], writes=["wout"])

                dstate = {"n": 0}
                TL = []
                for ti, tl in enumerate(tiles):
                    TL.append(dict(ti=ti, tl=tl, n=len(tl), NT=len(tl) * 128, toff=tl[0] * 128,
                                   halo=(half == 0 and ti == 0)))

                def st_norm(t):
                    tl, NT = t["tl"], t["NT"]
                    norm_group([x1[:, c, :] for c in tl], [("x1", c) for c in tl], 0, 8,
                               lambda dt: hb[:, dt, 0:NT], "hb", (junk, xns), 0, junk_key="gl")

                def st_z_a(t, cts=(0, 1, 2, 3)):
                    NT, toff, ti = t["NT"], t["toff"], t["ti"]
                    for ct in cts:
                        b1 = bank()
                        T.mm([(banks[b1][:, 0:NT], w_in_sb[:, kt, 512 + ct * 128: 512 + (ct + 1) * 128], hb[:, kt, 0:NT])
                              for kt in range(8)], reads=S0ALL + [("win", 1), "hb"], writes=[("ps", b1)])
                        T.op("act", lambda e, b1=b1: e.activation(out=sig[:, 0:NT], in_=banks[b1][:, 0:NT], func=AF.Sigmoid),
                             reads=[("ps", b1)], writes=["sig"])
                        b2 = bank()
                        T.mm([(banks[b2][:, 0:NT], w_in_sb[:, kt, ct * 128:(ct + 1) * 128], hb[:, kt, 0:NT])
                              for kt in range(8)], reads=S0ALL + [("win", 1), "hb"], writes=[("ps", b2)])
                        adst = a_buf[:, ct, 30 + toff: 30 + toff + NT]
                        if t["halo"]:
                            T.op("dve", lambda e, b2=b2, adst=adst: e.scalar_tensor_tensor(
                                adst, banks[b2][:, 0:NT], msk[:, 0:1], sig[:, 0:NT], ALU.mult, ALU.mult),
                                reads=[("ps", b2), "sig", "msk"], writes=[("a", ti)])
                        else:
                            T.op("dve", lambda e, b2=b2, adst=adst: e.tensor_tensor(
                                adst, banks[b2][:, 0:NT], sig[:, 0:NT], ALU.mult),
                                reads=[("ps", b2), "sig"], writes=[("a", ti)])

                def st_z_gu(t):
                    NT = t["NT"]
                    for ct in range(4):
                        b = bank()
                        T.mm([(banks[b][:, 0:NT], w_in_sb[:, kt, 1024 + ct * 128: 1024 + (ct + 1) * 128], hb[:, kt, 0:NT])
                              for kt in range(8)], reads=S0ALL + [("win", 2), "hb"], writes=[("ps", b)])
                        T.op("act", lambda e, b=b, ct=ct: e.activation(out=gua[:, ct, 0:NT], in_=banks[b][:, 0:NT],
                                                                       func=AF.Gelu_apprx_tanh),
                             reads=[("ps", b)], writes=[("gua", ct)])

                def st_z_gv(t):
                    for ci in range(t["n"]):
                        b = bank()
                        T.mm([(banks[b][:], hb[:, kt, ci * 128:(ci + 1) * 128], w_in_sb[:, kt, 1536:2048])
                              for kt in range(8)], reads=S0ALL + [("win", 0), "hb"], writes=[("ps", b)])
                        T.op("act", lambda e, b=b: e.activation(out=gl[:], in_=banks[b][:], func=AF.Gelu_apprx_tanh),
                             reads=[("ps", b)], writes=["gl"])
                        T.op("dve", lambda e: e.bn_stats(bst[:, 0:6], gl[:]), reads=["gl"], writes=["bst"])
                        T.op("dve", lambda e: e.bn_aggr(small[:, 8:10], bst[:, 0:6]), reads=["bst"], writes=[("small", 8)])
                        T.op("dve", lambda e: e.tensor_scalar(small[:, 11:12], small[:, 9:10], LN_EPS, None, ALU.add),
                             reads=[("small", 8)], writes=[("small", 11)])
                        T.op("pool", lambda e: e.tensor_tensor(small[:, 10:11], small[:, 11:12], mhalf[:, 0:1], ALU.pow),
                             reads=[("small", 11), "mhalf"], writes=[("small", 10)])
                        T.op("dve", lambda e: e.tensor_scalar(gl[:], gl[:], small[:, 8:9], small[:, 10:11],
                                                               ALU.subtract, ALU.mult),
                             reads=["gl", ("small", 8), ("small", 10)], writes=["gl"])
                        T.op("dve", lambda e: e.tensor_tensor(gl[:], gl[:], rows[:, O_GMG:O_GMG + 512], ALU.mult),
                             reads=["gl", "rows"], writes=["gl"])
                        T.op("dve", lambda e, ci=ci: e.tensor_tensor(gvn[:, ci, :], gl[:], rows[:, O_GMB:O_GMB + 512], ALU.add),
                             reads=["gl", "rows"], writes=[("gvn", ci)])

                def st_spatial(t):
                    n, NT = t["n"], t["NT"]
                    for ct in range(4):
                        b = bank()
                        mms = []
                        for ci in range(n):
                            for hh in range(2):
                                hd = 2 * ct + hh
                                mms.append((banks[b][hh * 64:(hh + 1) * 64, ci * 128:(ci + 1) * 128],
                                            gvn[:, ci, hd * 64:(hd + 1) * 64], wsT[:, hd, :]))
                        T.mm(mms, reads=[("gvn", ci) for ci in range(n)] + ["wsT"], writes=[("ps", b)], single=True)
                        T.op("dve", lambda e, b=b, ct=ct: e.tensor_tensor(
                            stat[:, 0, 0:NT].rearrange("p (c t) -> p c t", t=128),
                            banks[b][:, 0:NT].rearrange("p (c t) -> p c t", t=128),
                            rows[:, O_BS + ct * 128: O_BS + (ct + 1) * 128].unsqueeze(1).to_broadcast([128, n, 128]),
                            ALU.add), reads=[("ps", b), "rows"], writes=[("stat", 0)])
                        T.op("dve", lambda e, ct=ct: e.tensor_tensor(yb[:, 4 + ct, 0:NT], gua[:, ct, 0:NT], stat[:, 0, 0:NT], ALU.mult),
                             reads=[("gua", ct), ("stat", 0)], writes=[("yb", 4 + ct)])
                        T.op("act", lambda e, ct=ct: e.activation(out=gsq[:, ct, 0:NT], in_=yb[:, 4 + ct, 0:NT], func=AF.Square),
                             reads=[("yb", 4 + ct)], writes=[("gsq", ct)])

                def rms_cols(t, base, src, skey):
                    n = t["n"]
                    par = t["ti"] % 2
                    b = bank()
                    for ci in range(n):
                        T.mm([(banks[b][:, ci:ci + 1], src[:, ct, ci * 128:(ci + 1) * 128], ones_b[:, 0:1]) for ct in range(4)],
                             reads=[(skey, ct) for ct in range(4)] + ["ones_b"], writes=[("ps", b)])
                    c0 = base + par * 4
                    T.op("dve", lambda e: e.tensor_scalar(small[:, 88:88 + n], banks[b][:, 0:n], 1.0 / 512, RMS_EPS, ALU.mult, ALU.add),
                         reads=[("ps", b)], writes=[("small", 88)])
                    T.op("pool", lambda e: e.tensor_tensor(small[:, c0:c0 + n], small[:, 88:88 + n], mhalf[:, 0:n], ALU.pow),
                         reads=[("small", 88), "mhalf"], writes=[("small", c0)])

                def st_ones_g(t):
                    rms_cols(t, 64, gsq, "gsq")

                def st_yb_g(t):
                    pass

                def st_conv_mm(t, cts=(0, 1, 2, 3)):
                    NT, toff, ti = t["NT"], t["toff"], t["ti"]
                    if "cb" not in t:
                        t["cb"] = []
                    for ct in cts:
                        ds_ = dstate["n"] % 2
                        dstate["n"] += 1
                        T.dma("sp", diag[ds_].rearrange("p k n -> p (k n)"), diag_scr[ct],
                              reads=[("dscr", ct)], writes=[("diag", ds_)])
                        b = bank(hold=True)
                        rd = [("diag", ds_), ("a", ti), "a_pre"] + ([("a", ti - 1)] if ti > 0 else [])
                        T.mm([(banks[b][:, 0:NT], diag[ds_][:, k, :], a_buf[:, ct, toff + k: toff + k + NT]) for k in range(KW)],
                             reads=rd, writes=[("ps", b)])
                        t["cb"].append(b)

                def st_conv_evac(t, cts=(0, 1, 2, 3)):
                    NT = t["NT"]
                    for ct in cts:
                        b = t["cb"][ct]
                        bias = cols[:, R_CB + ct:R_CB + ct + 1]
                        T.op("act", lambda e, b=b, ct=ct, bias=bias: e.activation(
                            out=cvb[:, ct, 0:NT], in_=banks[b][:, 0:NT], func=AF.Identity, bias=bias),
                            reads=[("ps", b), "cols"], writes=[("cvb", ct)])
                        T.op("act", lambda e, b=b, ct=ct, bias=bias: e.activation(
                            out=sq[:, ct, 0:NT], in_=banks[b][:, 0:NT], func=AF.Square, bias=bias),
                            reads=[("ps", b), "cols"], writes=[("sq", ct)])
                        T.op("act", lambda e, b=b, ct=ct, bias=bias: e.activation(
                            out=cv[:, ct, 0:NT], in_=banks[b][:, 0:NT], func=AF.Identity, bias=bias),
                            reads=[("ps", b), "cols"], writes=[("cv", ct // 2), ("cvx", ct)])
                        release(b)

                def st_ones_c(t):
                    NT = t["NT"]
                    bm = bank()
                    T.mm([(banks[bm][:, 0:NT], ones_b[:], cvb[:, ct, 0:NT]) for ct in range(4)],
                         reads=[("cvb", ct) for ct in range(4)] + ["ones_b"], writes=[("ps", bm)])
                    be = bank()
                    T.mm([(banks[be][:, 0:NT], ones_b[:], sq[:, ct, 0:NT]) for ct in range(4)],
                         reads=[("sq", ct) for ct in range(4)] + ["ones_b"], writes=[("ps", be)])
                    T.op("act", lambda e: e.activation(out=stat[:, 1, 0:NT], in_=banks[bm][:, 0:NT], func=AF.Identity, scale=1.0 / 512),
                         reads=[("ps", bm)], writes=[("stat", 1)])
                    T.op("dve", lambda e: e.tensor_tensor(stat[:, 2, 0:NT], stat[:, 1, 0:NT], stat[:, 1, 0:NT], ALU.mult),
                         reads=[("stat", 1)], writes=[("stat", 2)])
                    T.op("dve", lambda e: e.scalar_tensor_tensor(stat[:, 2, 0:NT], banks[be][:, 0:NT], 1.0 / 512, stat[:, 2, 0:NT],
                                                                  ALU.mult, ALU.subtract),
                         reads=[("ps", be), ("stat", 2)], writes=[("stat", 2)])
                    T.op("act", lambda e: e.activation(out=stat[:, 2, 0:NT], in_=stat[:, 2, 0:NT], func=AF.Sqrt,
                                                       bias=epsr[:, 1:2]),
                         reads=[("stat", 2), "epsr"], writes=[("stat", 2)])
                    T.op("dve", lambda e: e.reciprocal(stat[:, 2, 0:NT], stat[:, 2, 0:NT]),
                         reads=[("stat", 2)], writes=[("stat", 2)])
                    for ct in range(4):
                        T.op("dve", lambda e, ct=ct: e.tensor_tensor(cv[:, ct, 0:NT], cv[:, ct, 0:NT], stat[:, 1, 0:NT], ALU.subtract),
                             reads=[("cvx", ct), ("stat", 1)], writes=[("cvx", ct), ("cv", ct // 2)])
                        T.op("dve", lambda e, ct=ct: e.tensor_tensor(cv[:, ct, 0:NT], cv[:, ct, 0:NT], stat[:, 2, 0:NT], ALU.mult),
                             reads=[("cvx", ct), ("stat", 2)], writes=[("cvx", ct), ("cv", ct // 2)])
                    for ct in range(4):
                        T.op("act", lambda e, ct=ct: e.activation(out=yb[:, ct, 0:NT], in_=cv[:, ct, 0:NT], func=AF.Silu,
                                                                  scale=cols[:, R_CLG + ct:R_CLG + ct + 1],
                                                                  bias=cols[:, R_CLB + ct:R_CLB + ct + 1]),
                             reads=[("cvx", ct), "cols"], writes=[("yb", ct)])
                    for ct in range(4):
                        T.op("act", lambda e, ct=ct: e.activation(out=sq[:, ct, 0:NT], in_=yb[:, ct, 0:NT], func=AF.Square),
                             reads=[("yb", ct)], writes=[("sq", ct)])

                def st_ones_a(t):
                    rms_cols(t, 72, sq, "sq")

                def st_out(t, part):
                    par = t["ti"] % 2
                    k0, base = (4, 64) if part == "g" else (0, 72)
                    for ci, c in enumerate(t["tl"]):
                        for hf in range(2):
                            b = bank()
                            T.mm([(banks[b][:], yb[:, kt, ci * 128:(ci + 1) * 128], w_out_sb[:, kt, hf * 512:(hf + 1) * 512])
                                  for kt in range(k0, k0 + 4)], reads=[("yb", i) for i in range(k0, k0 + 4)] + ["wout"],
                                 writes=[("ps", b)])
                            col = base + par * 4 + ci
                            T.op("dve", lambda e, b=b, c=c, hf=hf, col=col: e.scalar_tensor_tensor(
                                x1[:, c, hf * 512:(hf + 1) * 512], banks[b][:], small[:, col:col + 1],
                                x1[:, c, hf * 512:(hf + 1) * 512], ALU.mult, ALU.add),
                                reads=[("ps", b), ("x1", c), ("small", base + par * 4)], writes=[("x1", c)])

                K_ = len(TL)
                gua_b = gua[:].rearrange("p a b -> p (a b)").bitcast(BF16).rearrange("p (k n) -> p k n", k=8)
                GK = [("gua", i) for i in range(4)]

                def norm2_tile(t, which):
                    tl = t["tl"]
                    if which == "halo":
                        norm_group([x1[:, 0, :]], [("x1", 0)], 16, 24, lambda dt: h2h[:, dt, 0:2], "h2h",
                                   (junk, xns), 16, lo=126, junk_key="gl")
                    elif which == 0:
                        norm_group([x1[:, c, :] for c in tl], [("x1", c) for c in tl], 16, 24,
                                   lambda dt: hb[:, dt, :], "hb", (junk, xns), 16, junk_key="gl")
                    else:
                        norm_group([x1[:, c, :] for c in tl], [("x1", c) for c in tl], 16, 24,
                                   lambda dt: gua_b[:, dt, :], GK, (junk, xns), 16, junk_key="gl")

                if half == 1:
                    load_x(1, [4, 5, 6, 7])
                    load_w_out()
                if half == 0:
                    st_norm(TL[0])
                st_z_gv(TL[0]); st_z_a(TL[0]); st_z_gu(TL[0])
                st_spatial(TL[0])
                for i in range(K_):
                    t = TL[i]
                    nx = TL[i + 1] if i + 1 < K_ else None
                    if half == 0 and nx is None:
                        late_blocks_start(gvn)
                    st_conv_mm(t, (0, 1))
                    if nx is not None:
                        st_norm(nx)
                    elif half == 1:
                        norm2_tile(TL[K_ - 2], 0)
                    st_conv_mm(t, (2, 3))
                    if half == 1 and i == 0:
                        fold_w_out()
                    st_conv_evac(t)
                    if nx is not None:
                        st_z_a(nx, (0, 1))
                    st_ones_c(t)
                    if half == 0 and i == 0:
                        hook_gt1()
                        fold_w_out()
                    if nx is not None:
                        st_z_a(nx, (2, 3))
                        st_z_gv(nx)
                        st_z_gu(nx)
                        if i + 1 == K_ - 1:
                            load_slice_up(0)
                    if half == 0 and nx is None:
                        def _n2():
                            norm2_tile(TL[0], "halo")
                            norm2_tile(TL[K_ - 2], 0)
                        late_blocks(after_mod=_n2)
                    st_ones_g(t)
                    st_out(t, "g")
                    st_ones_a(t)
                    st_out(t, "a")
                    if nx is not None:
                        st_spatial(nx)
                    else:
                        norm2_tile(t, 1)
                if half == 0:
                    T.op("pool", lambda e: e.tensor_copy(a_carry[:], a_buf[:, :, ntok: ntok + 30]),
                         reads=[("a", len(tiles) - 1)], writes=["a_carry"])
                T.barrier()
            if debug is not None and half == 0:
                dbg("x1", x1[:], [128, 9, D], [("x1", c) for c in range(9)])
                dbg("modc", modc[:], [128, 48], ["modc"])
                dbg("gt1", gt1[:], [128, D], ["gt1"])
                dbg("gt2", gt2[:], [128, D], ["gt2"])

            with contextlib.ExitStack() as sB:
                junk = sb(sB, "junkB", [128, D], BF16)
                xns = [sb(sB, "xnB%d" % i, [128, D], BF16) for i in range(2)]
                Uv = [sb(sB, "Uv%d" % i, [128, 514], F32) for i in range(2)]
                Ug = [sb(sB, "Ug%d" % i, [128, 514], F32) for i in range(2)]
                av = [sb(sB, "av%d" % i, [128, 512], F32) for i in range(2)]
                ag = [sb(sB, "ag%d" % i, [128, 512], F32) for i in range(2)]
                actb = [sb(sB, "actb%d" % i, [128, 6, 512], BF16) for i in range(2)]
                ostg = [sb(sB, "ostg%d" % i, [128, D], F32) for i in range(2)]

                own = list(range(1, 9)) if half == 0 else list(range(0, 8))
                if half == 0:
                    print("phase B SBUF bytes remaining:", nc.sbuf_bytes_remaining)

                def load_slice(s_):
                    load_slice_up(s_)
                    load_slice_down(s_)
                    return slice_views(s_)

                winfo = {0: slice_views(0)}
                gua_b = gua[:].rearrange("p a b -> p (a b)").bitcast(BF16).rearrange("p (k n) -> p k n", k=8)
                h2t = [hb[:], gua_b]
                h2k = [["hb"], [("gua", i) for i in range(4)]]
                load_slice_down(0)
                if debug is not None and half == 0:
                    dbg("hbd", hb[:], [128, 8, 512], ["hb"])
                    dbg("guad", gua_b, [128, 8, 512], [("gua", i) for i in range(4)])

                steps = [(s_, tt_) for s_ in range(4) for tt_ in range(2)]
                items = [(si, fi) for si, (s_, tt_) in enumerate(steps) for fi in range(len(SLICES[s_]))]
                DD = 2

                def head(g, si, fi):
                    s_, tt = steps[si]
                    f = SLICES[s_][fi]
                    wv, wg, wd, wkey = winfo[s_]
                    par = g % 2
                    if half == 0 and tt == 0:
                        for gi, wsrc in enumerate((wv, wg)):
                            b = bank()
                            T.mm([(banks[b][:, 0:2], wsrc[:, kt, fi * 128:(fi + 1) * 128], h2h[:, kt, 0:2]) for kt in range(8)],
                                 reads=[(wkey, "ab"[gi]), "h2h"], writes=[("ps", b)])
                            T.op("act", lambda e, b=b, f=f, gi=gi: e.activation(
                                out=carry[:, 0, f, gi, :], in_=banks[b][:, 0:2], func=AF.Identity, scale=msk[:, 0:1]),
                                reads=[("ps", b), "msk"], writes=[("carry", 0, f, gi)])
                    for gi, (wsrc, Ub, accb, nm) in enumerate(((wv, Uv, av, "v"), (wg, Ug, ag, "g"))):
                        j = f + gi * NFT
                        U = Ub[par]
                        acc = accb[par]
                        ukey = ("U" + nm, par)
                        ackey = ("acc" + nm, par)
                        b = bank()
                        T.mm([(banks[b][:], wsrc[:, kt, fi * 128:(fi + 1) * 128], h2t[tt][:, kt, :])
                              for kt in range(8)], reads=[(wkey, "ab"[gi])] + h2k[tt], writes=[("ps", b)])
                        T.op("act", lambda e, b=b, U=U: e.activation(out=U[:, 2:514], in_=banks[b][:], func=AF.Copy),
                             reads=[("ps", b)], writes=[ukey])
                        T.op("act", lambda e, b=b, f=f, gi=gi, tt=tt: e.activation(
                            out=carry[:, (tt + 1) % 2, f, gi, :], in_=banks[b][:, 510:512], func=AF.Copy),
                            reads=[("ps", b)], writes=[("carry", (tt + 1) % 2, f, gi)])
                        T.op("act", lambda e, U=U, f=f, gi=gi, tt=tt: e.activation(
                            out=U[:, 0:2], in_=carry[:, tt % 2, f, gi, :], func=AF.Copy),
                             reads=[("carry", tt % 2, f, gi)], writes=[ukey])
                        T.op("act", lambda e, b=b, acc=acc, j=j: e.activation(
                            out=acc[:], in_=banks[b][:], func=AF.Identity,
                            scale=cols[:, R_FW + 2 * 44 + j: R_FW + 2 * 44 + j + 1], bias=cols[:, R_FB + j: R_FB + j + 1]),
                            reads=[("ps", b), "cols"], writes=[ackey])
                        T.op("dve", lambda e, U=U, acc=acc, j=j: e.scalar_tensor_tensor(
                            acc[:], U[:, 1:513], cols[:, R_FW + 44 + j: R_FW + 44 + j + 1], acc[:], ALU.mult, ALU.add),
                            reads=[ukey, ackey, "cols"], writes=[ackey])
                        T.op("dve", lambda e, U=U, acc=acc, j=j: e.scalar_tensor_tensor(
                            acc[:], U[:, 0:512], cols[:, R_FW + j: R_FW + j + 1], acc[:], ALU.mult, ALU.add),
                            reads=[ukey, ackey, "cols"], writes=[ackey])

                def tail(g, si, fi):
                    par = g % 2
                    accv, accg = av[par], ag[par]
                    kv, kg = ("accv", par), ("accg", par)
                    ab = actb[si % 2]
                    T.op("act", lambda e: e.activation(out=accg[:], in_=accg[:], func=AF.Silu), reads=[kg], writes=[kg])
                    T.op("dve", lambda e: e.tensor_tensor(ab[:, fi, :], accv[:], accg[:], ALU.mult),
                         reads=[kv, kg], writes=[("actb", si % 2)])

                def down(si):
                    s_, tt = steps[si]
                    nf = len(SLICES[s_])
                    wv, wg, wd, wkey = winfo[s_]
                    ab = actb[si % 2]
                    akey = ("actb", si % 2)
                    for ci in range(4):
                        c = own[tt * 4 + ci]
                        for hf in range(2):
                            b = bank()
                            T.mm([(banks[b][:], ab[:, fi, ci * 128:(ci + 1) * 128], wd[:, fi, hf * 512:(hf + 1) * 512])
                                  for fi in range(nf)], reads=[akey, (wkey, "c")], writes=[("ps", b)])
                            T.op("dve", lambda e, b=b, c=c, hf=hf: e.tensor_tensor(
                                x1[:, c, hf * 512:(hf + 1) * 512], banks[b][:], x1[:, c, hf * 512:(hf + 1) * 512], ALU.add),
                                reads=[("ps", b), ("x1", c)], writes=[("x1", c)])
                        if s_ == 3 and half == 1 and tt == 1:
                            final_norm([c], tt * 4 + ci)
                    if s_ == 3 and not (half == 1 and tt == 1):
                        final_norm([own[tt * 4 + ci] for ci in range(4)], tt * 4)
                        if half == 0 and tt == 0:
                            load_x(1, [0, 1, 2, 3])

                def final_norm(cs, o0):
                    n = len(cs)
                    for ci, c in enumerate(cs):
                        T.op("act", lambda e, c=c, ci=ci: e.activation(out=junk[:], in_=x1[:, c, :], func=AF.Square,
                                                                       accum_out=small[:, 32 + ci:33 + ci]),
                             reads=[("x1", c)], writes=["junk", ("small", 32)])
                    T.op("dve", lambda e: e.tensor_scalar(small[:, 36:36 + n], small[:, 32:32 + n], 1.0 / D, RMS_EPS, ALU.mult, ALU.add),
                         reads=[("small", 32)], writes=[("small", 36)])
                    T.op("pool", lambda e: e.tensor_tensor(small[:, 40:40 + n], small[:, 36:36 + n], mhalf[:, 0:n], ALU.pow),
                         reads=[("small", 36), "mhalf"], writes=[("small", 40)])
                    for ci, c in enumerate(cs):
                        og = ostg[(o0 + ci) % 2]
                        okey = ("ostg", (o0 + ci) % 2)
                        T.op("dve", lambda e, c=c, og=og, ci=ci: e.scalar_tensor_tensor(
                            og[:], x1[:, c, :], small[:, 40 + ci:41 + ci], rows[:, O_FING:O_FING + D], ALU.mult, ALU.mult),
                            reads=[("x1", c), ("small", 40), "rows"], writes=[okey])
                        r0 = half * 1024 + (o0 + ci) * 128
                        T.dma("sp", out_d[r0:r0 + 128, :], og[:], reads=[okey])

                pending = []
                G = len(items)
                for g in range(G + 1):
                    if g < G:
                        si, fi = items[g]
                        s_, tt = steps[si]
                        if fi == 0 and tt == 1 and s_ + 1 < 4:
                            winfo[s_ + 1] = load_slice(s_ + 1)
                        if half == 0 and fi == 0 and tt == 1 and s_ == 3:
                            load_w_in()
                        head(g, si, fi)
                    if g >= 1:
                        psi, pfi = items[g - 1]
                        tail(g - 1, psi, pfi)
                        if pfi == len(SLICES[steps[psi][0]]) - 1:
                            pending.append([DD, psi])
                    for p in list(pending):
                        if p[0] <= 0 or g >= G:
                            down(p[1])
                            pending.remove(p)
                        else:
                            p[0] -= 1
                if half == 0:
                    cs1 = [0, 1, 2, 3]
                    norm_group([x1[:, c, :] for c in cs1], [("x1", c) for c in cs1], 0, 8,
                               lambda dt: hb[:, dt, 0:512], "hb", (junk[:], xns), 0)
                T.barrier()
        T.finish()
    return nc, dbg_out


def _pack_inputs(inputs):
    f = np.float32
    x = np.asarray(inputs["x"], f)
    c = np.asarray(inputs["c"], f)
    g = lambda k: np.asarray(inputs[k], f)[0]
    vec = np.zeros((NROWS, 128), f)
    vec[R_BADA:R_BADA + 48] = g("b_ada").reshape(48, 128)
    vec[R_G1:R_G1 + 8] = g("norm1_gain").reshape(8, 128)
    vec[R_CW:R_CW + 124] = g("conv_dw_w").reshape(31 * 4, 128)
    vec[R_CB:R_CB + 4] = g("conv_dw_b").reshape(4, 128)
    vec[R_CLG:R_CLG + 4] = g("conv_ln_g").reshape(4, 128)
    vec[R_CLB:R_CLB + 4] = g("conv_ln_b").reshape(4, 128)
    vec[R_MOG:R_MOG + 8] = g("mix_out_gain").reshape(8, 128)
    vec[R_G2:R_G2 + 8] = g("norm2_gain").reshape(8, 128)
    vec[R_FW:R_FW + 132] = g("ffn_dw_w").reshape(3 * 44, 128)
    vec[R_FB:R_FB + 44] = g("ffn_dw_b").reshape(44, 128)
    rows = np.zeros((128, NRB), f)
    rows[:, O_GMG:O_GMG + 512] = g("gm_ln_g")[None, :]
    rows[:, O_GMB:O_GMB + 512] = g("gm_ln_b")[None, :]
    rows[:, O_FING:O_FING + 1024] = np.asarray(inputs["final_gain"], f)[None, :]
    b_ada = g("b_ada")
    rows[:, O_BGT1:O_BGT1 + 1024] = b_ada[2048:3072][None, :]
    rows[:, O_BGT2:O_BGT2 + 1024] = b_ada[5120:6144][None, :]
    bs = g("gm_bs")
    rows[:, O_BS:O_BS + 512] = np.repeat(bs, 64, axis=0).reshape(4, 128, 128).transpose(1, 0, 2).reshape(128, 512)
    wsT = np.ascontiguousarray(g("gm_ws").transpose(2, 0, 1))
    shared = {
        "vecs": vec, "rows": rows, "wsT": wsT,
        "w_ada": g("w_ada"), "w_in": g("w_in"), "w_out": g("w_out"),
        "w_up": g("w_up"), "w_down": g("w_down"),
    }
    in_maps = []
    for i in range(NCORES):
        b, q = divmod(i, 4)
        t0 = q * TOK_PER_CORE
        xe = np.zeros((17 * 128, D), f)
        if q > 0:
            xe[0:128] = x[b, t0 - 128:t0]
        xe[128:] = x[b, t0:t0 + TOK_PER_CORE]
        m = dict(shared)
        m["x_ext"] = xe
        m["msk"] = np.full((128, 1), 1.0 if q > 0 else 0.0, f)
        m["c_col"] = np.ascontiguousarray(c[b].reshape(8, 128).T)
        in_maps.append(m)
    return in_maps


_NC_CACHE = {}


def kernel(**inputs):
    if "nc" not in _NC_CACHE:
        _NC_CACHE["nc"] = build_program()[0]
    nc = _NC_CACHE["nc"]
    in_maps = _pack_inputs(inputs)
    res = run_bass_kernel_spmd(nc, in_maps, core_ids=list(range(NCORES)))
    out = np.zeros((2, SEQ, D), np.float32)
    for i in range(NCORES):
        b, q = divmod(i, 4)
        out[b, q * TOK_PER_CORE:(q + 1) * TOK_PER_CORE] = res.results[i]["out"]
    return out
```
